# Optimizing a Trainium2 kernel written in Bass

```python
import math
import jax, jax.numpy as jnp
from jax import lax
import numpy as np

D_MODEL = 1024
BATCH = 4
SEQ = 4096
DEPTH = 1

EPS = 1e-5
SSD_HEADS = 16
SSD_HEAD_DIM = 64
SSD_INNER = SSD_HEADS * SSD_HEAD_DIM
SSD_GROUPS = 2
SSD_HEADS_PER_GROUP = SSD_HEADS // SSD_GROUPS
SSD_STATE = 128
SSD_CONV = 4
SSD_CHUNK = 128
SSD_CONV_DIM = SSD_INNER + 2 * SSD_GROUPS * SSD_STATE
SSD_PROJ = SSD_INNER + SSD_CONV_DIM + SSD_HEADS
DA_HEADS = 8
DA_HEAD_DIM = 64
DA_V_DIM = 2 * DA_HEAD_DIM
DA_WIDTH = DA_HEADS * DA_V_DIM
DA_PROJ = 3 * DA_WIDTH
Q_BLOCK = 128
MIX_WIDTH = SSD_INNER + DA_WIDTH
IN_COLS = SSD_PROJ + DA_PROJ
D_FF = -(-8 * D_MODEL // (3 * 256)) * 256

kernel_name = "hymba_ssd_diffattn_block"


def rmsnorm(x, w):
    xf = x.astype(jnp.float32)
    y = xf * lax.rsqrt(jnp.mean(xf * xf, axis=-1, keepdims=True) + EPS)
    return (y * w.astype(jnp.float32)).astype(x.dtype)


def causal_dwconv(u, w, b):
    k = w.shape[0]
    y = lax.conv_general_dilated(u, w[:, None, :].astype(u.dtype), window_strides=(1,),
                                 padding=((k - 1, 0),), dimension_numbers=("NWC", "WIO", "NWC"),
                                 feature_group_count=u.shape[-1])
    return y + b


def segsum_exp(a):
    t = a.shape[-1]
    cs = jnp.cumsum(a, axis=-1)
    diff = cs[..., :, None] - cs[..., None, :]
    mask = jnp.tril(jnp.ones((t, t), dtype=bool))
    return jnp.exp(jnp.where(mask, diff, -jnp.inf))


def ssd_mixer(zxbcdt, conv_w, conv_b, dt_bias, a_log, d_skip, norm_w):
    bsz, seqlen, _ = zxbcdt.shape
    nc = seqlen // SSD_CHUNK
    g, e, l = SSD_GROUPS, SSD_HEADS_PER_GROUP, SSD_CHUNK
    z, xbc, dt = jnp.split(zxbcdt, [SSD_INNER, SSD_INNER + SSD_CONV_DIM], axis=-1)
    xbc = jax.nn.silu(causal_dwconv(xbc, conv_w, conv_b))
    xs, bm, cm = jnp.split(xbc, [SSD_INNER, SSD_INNER + SSD_GROUPS * SSD_STATE], axis=-1)
    dt = jax.nn.softplus((dt + dt_bias).astype(jnp.float32))
    a = -jnp.exp(a_log.astype(jnp.float32))
    dt5 = dt.reshape(bsz, nc, l, g, e)
    a_dt = (dt5 * a.reshape(g, e)).transpose(0, 3, 4, 1, 2)
    x5 = xs.reshape(bsz, nc, l, g, e, SSD_HEAD_DIM)
    xdt = x5 * dt5[..., None]
    bm = bm.reshape(bsz, nc, l, g, SSD_STATE)
    cm = cm.reshape(bsz, nc, l, g, SSD_STATE)
    a_cs = jnp.cumsum(a_dt, axis=-1)
    lmat = segsum_exp(a_dt)
    cb = jnp.einsum("bclgn,bcsgn->bcgls", cm, bm)
    y_diag = jnp.einsum("bcgls,bgecls,bcsgep->bclgep", cb, lmat, xdt)
    decay_states = jnp.exp(a_cs[..., -1:] - a_cs)
    states = jnp.einsum("bclgn,bgecl,bclgep->bcgepn", bm, decay_states, xdt)
    chunk_decay = jnp.exp(a_cs[..., -1])

    def step(carry, inp):
        st, dec = inp
        new = (carry * dec[..., None, None] + st).astype(carry.dtype)
        return new, carry

    init = jnp.zeros_like(states[:, 0])
    _, prev_states = lax.scan(step, init, (jnp.moveaxis(states, 1, 0), jnp.moveaxis(chunk_decay, -1, 0)))
    prev_states = jnp.moveaxis(prev_states, 0, 1)
    y_off = jnp.einsum("bclgn,bcgepn,bgecl->bclgep", cm, prev_states, jnp.exp(a_cs))
    y = y_diag + y_off + x5 * d_skip.astype(jnp.float32).reshape(g, e)[:, :, None]
    y = y.reshape(bsz, seqlen, SSD_INNER)
    gy = (y * jax.nn.silu(z.astype(jnp.float32))).reshape(bsz, seqlen, g, SSD_INNER // g)
    gy = gy * lax.rsqrt(jnp.mean(gy * gy, axis=-1, keepdims=True) + EPS)
    return (gy.reshape(bsz, seqlen, SSD_INNER) * norm_w.astype(jnp.float32)).astype(zxbcdt.dtype)


def diff_attention(qkv, lam_q1, lam_k1, lam_q2, lam_k2, subln_w, lambda_init):
    bsz, seqlen, _ = qkv.shape
    nqb = seqlen // Q_BLOCK
    q, k, v = jnp.split(qkv, 3, axis=-1)
    q = q.reshape(bsz, seqlen, DA_HEADS, 2, DA_HEAD_DIM).transpose(3, 0, 2, 1, 4) * (DA_HEAD_DIM ** -0.5)
    k = k.reshape(bsz, seqlen, DA_HEADS, 2, DA_HEAD_DIM).transpose(3, 0, 2, 1, 4)
    v = v.reshape(bsz, seqlen, DA_HEADS, DA_V_DIM).transpose(0, 2, 1, 3)
    f32 = jnp.float32
    lam = (jnp.exp(jnp.sum(lam_q1.astype(f32) * lam_k1.astype(f32)))
           - jnp.exp(jnp.sum(lam_q2.astype(f32) * lam_k2.astype(f32))) + lambda_init)
    qb = q.reshape(2, bsz, DA_HEADS, nqb, Q_BLOCK, DA_HEAD_DIM).transpose(3, 0, 1, 2, 4, 5)
    kpos = jnp.arange(seqlen)

    def block(args):
        qi, i = args
        s = jnp.einsum("mbhqd,mbhkd->mbhqk", qi, k).astype(f32)
        qpos = i * Q_BLOCK + jnp.arange(Q_BLOCK)
        mask = kpos[None, :] <= qpos[:, None]
        p = jax.nn.softmax(jnp.where(mask, s, -jnp.inf), axis=-1)
        w = p[0] - lam * p[1]
        return jnp.einsum("bhqk,bhkv->bhqv", w.astype(v.dtype), v)

    o = lax.map(block, (qb, jnp.arange(nqb)))
    o = o.transpose(1, 0, 3, 2, 4).reshape(bsz, seqlen, DA_HEADS, DA_V_DIM)
    o = rmsnorm(o, subln_w) * (1.0 - lambda_init)
    return o.reshape(bsz, seqlen, DA_WIDTH)


def setup_inputs(seed: int = 0) -> dict:
    key = jax.random.key(seed)
    ks = jax.random.split(key, 24)
    f32 = jnp.float32
    nrm = lambda k, shape, scale: jax.random.normal(k, shape, f32) * scale
    gain = lambda k, shape: 1.0 + 0.02 * jax.random.normal(k, shape, f32)
    dt = jnp.exp(jax.random.uniform(ks[5], (DEPTH, SSD_HEADS), f32) * (math.log(0.1) - math.log(0.001)) + math.log(0.001))
    dt = jnp.maximum(dt, 1e-4)
    return {
        "x": jax.random.normal(ks[0], (BATCH, SEQ, D_MODEL), f32),
        "mix_norm_w": gain(ks[1], (DEPTH, D_MODEL)),
        "w_in": nrm(ks[2], (DEPTH, D_MODEL, IN_COLS), D_MODEL ** -0.5),
        "conv_w": nrm(ks[3], (DEPTH, SSD_CONV, SSD_CONV_DIM), SSD_CONV ** -0.5),
        "conv_b": nrm(ks[4], (DEPTH, SSD_CONV_DIM), 0.02),
        "dt_bias": dt + jnp.log(-jnp.expm1(-dt)),
        "a_log": jnp.log(jax.random.uniform(ks[6], (DEPTH, SSD_HEADS), f32, 1.0, 16.0)),
        "d_skip": 1.0 + 0.1 * jax.random.normal(ks[7], (DEPTH, SSD_HEADS), f32),
        "ssd_norm_w": gain(ks[8], (DEPTH, SSD_INNER)),
        "lam_q1": nrm(ks[9], (DEPTH, DA_HEAD_DIM), 0.1),
        "lam_k1": nrm(ks[10], (DEPTH, DA_HEAD_DIM), 0.1),
        "lam_q2": nrm(ks[11], (DEPTH, DA_HEAD_DIM), 0.1),
        "lam_k2": nrm(ks[12], (DEPTH, DA_HEAD_DIM), 0.1),
        "subln_w": gain(ks[13], (DEPTH, DA_V_DIM)),
        "w_out": nrm(ks[14], (DEPTH, MIX_WIDTH, D_MODEL), MIX_WIDTH ** -0.5),
        "ffn_norm_w": gain(ks[15], (DEPTH, D_MODEL)),
        "w_gate": nrm(ks[16], (DEPTH, D_MODEL, D_FF), D_MODEL ** -0.5),
        "w_up": nrm(ks[17], (DEPTH, D_MODEL, D_FF), D_MODEL ** -0.5),
        "w_down": nrm(ks[18], (DEPTH, D_FF, D_MODEL), D_FF ** -0.5),
        "final_norm_w": gain(ks[19], (D_MODEL,)),
    }


def reference(x, mix_norm_w, w_in, conv_w, conv_b, dt_bias, a_log, d_skip, ssd_norm_w,
              lam_q1, lam_k1, lam_q2, lam_k2, subln_w, w_out, ffn_norm_w, w_gate, w_up, w_down,
              final_norm_w):
    h = x
    for layer in range(DEPTH):
        lambda_init = 0.8 - 0.6 * math.exp(-0.3 * layer)
        n = rmsnorm(h, mix_norm_w[layer])
        proj = n @ w_in[layer]
        ssd_in, da_in = jnp.split(proj, [SSD_PROJ], axis=-1)
        y_ssd = ssd_mixer(ssd_in, conv_w[layer], conv_b[layer], dt_bias[layer], a_log[layer],
                          d_skip[layer], ssd_norm_w[layer])
        y_da = diff_attention(da_in, lam_q1[layer], lam_k1[layer], lam_q2[layer], lam_k2[layer],
                              subln_w[layer], lambda_init)
        h = h + jnp.concatenate([y_ssd, y_da], axis=-1) @ w_out[layer]
        n2 = rmsnorm(h, ffn_norm_w[layer])
        h = h + (jax.nn.silu(n2 @ w_gate[layer]) * (n2 @ w_up[layer])) @ w_down[layer]
    return rmsnorm(h, final_norm_w)
```

```python
import bisect
from contextlib import ExitStack

import numpy as np
import ml_dtypes

import concourse.bass as bass
import concourse.mybir as mybir
from concourse.bass_utils import run_bass_kernel_spmd

F32 = mybir.dt.float32
BF16 = mybir.dt.bfloat16
AF = mybir.ActivationFunctionType
ALU = mybir.AluOpType
AX = mybir.AxisListType

ENGS = ("pe", "act", "dve", "pool", "sp")
EPS = 1e-5
T = 4096
D = 1024
DFF = 2816
NCOL = 2824
LAMBDA_INIT = 0.8 - 0.6 * 1.0
DEBUG = False
RUN_A = True
RUN_P2 = True
NT_B = 8
NT_A = 8


class Prog:
    def __init__(self, nc):
        self.nc = nc
        self.ops = []

    def op(self, eng, fn, reads=(), writes=(), ps=(), dma=False, semkey=None, inc=1):
        self.ops.append(dict(eng=eng, fn=fn, reads=tuple(reads), writes=tuple(writes), ps=tuple(ps),
                             dma=dma, semkey=semkey, inc=inc, deps=set(), signal=False))
        return len(self.ops) - 1

    def dma(self, eng, out, in_, reads=(), writes=(), semkey=None):
        return self.op(eng, lambda e: e.dma_start(out=out, in_=in_), reads, writes,
                       dma=True, semkey=semkey, inc=16)

    def analyze(self):
        ops = self.ops
        last_w, readers = {}, {}
        last_ps = {}
        for i, o in enumerate(ops):
            for k in o["reads"]:
                if k in last_w:
                    o["deps"].add(last_w[k])
            for k in o["writes"]:
                if k in last_w:
                    o["deps"].add(last_w[k])
                for r in readers.get(k, ()):
                    o["deps"].add(r)
            for k in o["reads"]:
                readers.setdefault(k, []).append(i)
            for k in o["writes"]:
                last_w[k] = i
                readers[k] = []
            for b in o["ps"]:
                d = last_ps.setdefault(b, {})
                for e2, j in d.items():
                    if e2 != o["eng"]:
                        o["deps"].add(j)
                d[o["eng"]] = i
            o["deps"].discard(i)
        for i, o in enumerate(ops):
            if o["eng"] == "pe" and not o["dma"]:
                o["deps"] = {d for d in o["deps"] if not (ops[d]["eng"] == "pe" and ops[d]["semkey"] is None)}
            for d in o["deps"]:
                ops[d]["signal"] = True
        cnt = {e: 0 for e in ENGS}
        dcnt = {}
        self.own_keys = []
        self._idx = {}
        for i, o in enumerate(ops):
            if o["semkey"] is not None:
                k = o["semkey"]
                if k not in dcnt:
                    dcnt[k] = 0
                    self.own_keys.append(k)
                dcnt[k] += o["inc"]
                o["sem"] = ("own", k)
                o["val"] = dcnt[k]
                self._idx.setdefault(k, []).append((i, dcnt[k]))
            elif o["signal"]:
                cnt[o["eng"]] += 1
                o["sem"] = ("eng", o["eng"])
                o["val"] = cnt[o["eng"]]
        self.final = dict(dcnt)

    def count_before(self, key, i):
        lst = self._idx[key]
        p = bisect.bisect_left(lst, (i, -1))
        return lst[p - 1][1] if p > 0 else 0

    def emit(self):
        nc = self.nc
        self.analyze()
        ops = self.ops
        with ExitStack() as es:
            sems = {}
            for e in ENGS:
                sems[("eng", e)] = es.enter_context(nc.semaphore("s_" + e))
            for n, k in enumerate(self.own_keys):
                sems[("own", k)] = es.enter_context(nc.semaphore("o%d" % n))
            block = es.enter_context(nc.Block())

            def run_engine(ename, eng):
                waited = {}
                for i, o in enumerate(ops):
                    if o["eng"] != ename:
                        continue
                    need = {}
                    for d in o["deps"]:
                        do = ops[d]
                        s, v = do["sem"], do["val"]
                        if do["semkey"] is not None:
                            v = max(v, self.count_before(do["semkey"], i))
                        if need.get(s, 0) < v:
                            need[s] = v
                    for s, v in need.items():
                        if waited.get(s, 0) >= v:
                            continue
                        eng.wait_ge(sems[s], v)
                        waited[s] = v
                    ins = o["fn"](eng)
                    if o["semkey"] is not None:
                        ins.then_inc(sems[o["sem"]], o["inc"])
                    elif o["signal"]:
                        ins.then_inc(sems[o["sem"]], 1)
                return waited

            @block.tensor
            def _(e):
                run_engine("pe", e)

            @block.scalar
            def _(e):
                run_engine("act", e)

            @block.vector
            def _(e):
                run_engine("dve", e)

            @block.gpsimd
            def _(e):
                run_engine("pool", e)

            @block.sync
            def _(e):
                w = run_engine("sp", e)
                for k, v in self.final.items():
                    if w.get(("own", k), 0) < v:
                        e.wait_ge(sems[("own", k)], v)


def build_program():
    nc = bass.Bass("TRN2", target_bir_lowering=False)

    def din(name, shape, dt=F32):
        return nc.dram_tensor(name, list(shape), dt, kind="ExternalInput").ap()

    xT = din("xT", [D, T])
    xtok = din("xtok", [2048, D])
    w_in = din("w_in", [D, NCOL])
    mixw = din("mixw", [128, 8])
    convw = din("convw", [128, 6, 4])
    convb = din("convb", [128, 6])
    ssdp = din("ssdp", [128, 3, 8])
    lamv = din("lamv", [128, 4, 64])
    w_out = din("w_out", [2048, D])
    wosc = din("wosc", [128, 16])
    ffnw = din("ffnw", [128, 8])
    w_gate = din("w_gate", [D, DFF])
    w_up = din("w_up", [D, DFF])
    w_down = din("w_down", [DFF, D])
    finw = din("finw", [128, D])
    cU = din("cU", [128, 128])
    cG = din("cG", [128, 128])
    cI = din("cI", [128, 128])
    flags = din("flags", [128, 2])
    out = nc.dram_tensor("out", [2048, D], F32, kind="ExternalOutput").ap()
    if DEBUG:
        dbg_da = nc.dram_tensor("dbg_da", [512, T], BF16, kind="ExternalOutput").ap()
        dbg_ssd = nc.dram_tensor("dbg_ssd", [512, T], BF16, kind="ExternalOutput").ap()

    yda_in = [nc.dram_tensor("yda_in%d" % i, [512, 512], BF16) for i in range(8)]
    yda_out = [nc.dram_tensor("yda_out%d" % i, [1024, 512], BF16) for i in range(8)]
    yssd_in = [nc.dram_tensor("yssd_in%d" % i, [512, 512], BF16) for i in range(8)]
    yssd_out = [nc.dram_tensor("yssd_out%d" % i, [1024, 512], BF16) for i in range(8)]

    P = Prog(nc)
    with ExitStack() as es:
        MEMW = 53000
        mem = es.enter_context(nc.sbuf_tensor("mem", [128, MEMW], F32))

        class Alloc:
            def __init__(self, start, end):
                self.off, self.end = start, end

            def take(self, shape, dt=F32):
                n = int(np.prod(shape[1:]))
                nbytes = n * (2 if dt == BF16 else 4)
                nbytes = (nbytes + 3) // 4 * 4
                assert self.off % 4 == 0 and self.off + nbytes <= self.end, (self.off, nbytes, self.end, shape)
                a = mem[:, self.off // 4:(self.off + nbytes) // 4]
                self.off += nbytes
                if dt == BF16:
                    a = a.bitcast(BF16)[:, 0:n]
                if len(shape) == 3:
                    a = a.rearrange("p (a b) -> p a b", a=shape[1])
                elif len(shape) == 4:
                    a = a.rearrange("p (a b c) -> p a b c", a=shape[1], b=shape[2])
                return a

        psf = es.enter_context(nc.psum_tensor("psf", [128, 7, 512], F32))
        psb = es.enter_context(nc.psum_tensor("psb", [128, 1024], BF16))

        pers = Alloc(0, 1024)
        cst = Alloc(1024, 8192)
        WO_OFF, WG_OFF, STG_OFF, TAIL_OFF, MEM_END = 111440, 144208, 189264, 197456, 212000

        def sb(name, shape, dt=F32):
            return pers.take(shape, dt)

        def sc(name, shape, dt=F32):
            return cst.take(shape, dt)

        Ib = sb("Ib", [128, 128], BF16)
        wosc_s = sb("wosc_s", [128, 16]); ffnw_s = sb("ffnw_s", [128, 8]); flags_s = sb("flags_s", [128, 2])
        wo_scale = sb("wo_scale", [128, 16]); st2 = sb("st2", [128, 4]); neglam = sb("neglam", [128, 1])
        mixw_s = sb("mixw_s", [128, 8])
        Uf = sc("Uf", [128, 128]); Gf = sc("Gf", [128, 128]); If = sc("If", [128, 128])
        Ub = sc("Ub", [128, 128], BF16)
        onesb = sc("onesb", [128, 128], BF16); onesf = sc("onesf", [128, 128])
        convw_s = sc("convw_s", [128, 6, 4]); convb_s = sc("convb_s", [128, 6])
        ssdp_s = sc("ssdp_s", [128, 3, 8]); lamv_s = sc("lamv_s", [128, 4, 64])
        Abc = sc("Abc", [128, 8]); Dsk = sc("Dsk", [128, 8, 64])
        lam2 = sc("lam2", [128, 2]); lamt = sc("lamt", [128, 2, 64])

        small_loads = [(Uf, cU), (Gf, cG), (If, cI), (mixw_s, mixw), (convw_s, convw), (convb_s, convb),
                       (ssdp_s, ssdp), (lamv_s, lamv), (wosc_s, wosc), (ffnw_s, ffnw), (flags_s, flags)]
        for n, (dst, src) in enumerate(small_loads):
            P.dma("sp", dst[:], src, writes=["c%d" % n], semkey="c%d" % n)
        CK = ["c%d" % n for n in range(len(small_loads))]
        P.op("dve", lambda e: e.tensor_copy(out=Ub[:], in_=Uf[:]), CK, ["Ub"])
        P.op("dve", lambda e: e.tensor_copy(out=Ib[:], in_=If[:]), CK, ["Ib"])
        P.op("pool", lambda e: e.memset(onesb[:], 1.0), [], ["onesb"])
        P.op("pool", lambda e: e.memset(onesf[:], 1.0), [], ["onesf"])
        P.op("act", lambda e: e.activation(out=Abc[:], in_=ssdp_s[:, 1, :], func=AF.Exp), CK, ["Abc"])
        P.op("dve", lambda e: e.tensor_scalar(out=Abc[:], in0=Abc[:], scalar1=-1.0, scalar2=None, op0=ALU.mult), ["Abc"], ["Abc"])
        P.op("dve", lambda e: e.tensor_copy(out=Dsk[:], in_=ssdp_s[:, 2, :].unsqueeze(2).to_broadcast([128, 8, 64])), CK, ["Dsk"])
        P.op("dve", lambda e: e.tensor_tensor(out=lamt[:, 0, :], in0=lamv_s[:, 0, :], in1=lamv_s[:, 1, :], op=ALU.mult), CK, ["lamt"])
        P.op("dve", lambda e: e.tensor_tensor(out=lamt[:, 1, :], in0=lamv_s[:, 2, :], in1=lamv_s[:, 3, :], op=ALU.mult), ["lamt"], ["lamt"])
        P.op("dve", lambda e: e.tensor_reduce(out=lam2[:], in_=lamt[:], axis=AX.X, op=ALU.add), ["lamt"], ["lam2"])
        P.op("act", lambda e: e.activation(out=lam2[:], in_=lam2[:], func=AF.Exp), ["lam2"], ["lam2"])
        P.op("dve", lambda e: e.tensor_tensor(out=neglam[:], in0=lam2[:, 1:2], in1=lam2[:, 0:1], op=ALU.subtract), ["lam2"], ["neglam"])
        P.op("dve", lambda e: e.tensor_scalar(out=neglam[:], in0=neglam[:], scalar1=-LAMBDA_INIT, scalar2=None, op0=ALU.add), ["neglam"], ["neglam"])

        shared = Alloc(8192, 8192 + 53248)
        Win = shared.take([128, 8, 1536], BF16)
        xTt = shared.take([128, 8, 512])
        nT = shared.take([128, 8, 512], BF16)
        lnv = shared.take([128, 512]); rstd = shared.take([128, 512])
        ARENA0 = shared.off
        Wo = Alloc(WO_OFF, WG_OFF).take([128, 16, 1024], BF16)
        Wg = Alloc(WG_OFF, STG_OFF).take([128, 8, DFF], BF16)
        stg = Alloc(STG_OFF, TAIL_OFF).take([128, 2, 1024])
        tail = Alloc(TAIL_OFF, MEM_END)
        finw_s = tail.take([128, D])
        actT = tail.take([128, 22, 128], BF16)
        sg = tail.take([128, 2, 128])

        xT_v = xT.rearrange("(kt p) t -> p kt t", p=128)
        w_in_v = w_in.rearrange("(kt p) c -> p kt c", p=128)

        def load_win(c0, ncols, tag):
            for kt in range(8):
                half = kt % 2
                st = xTt[:, 4 * half:4 * half + 4, :].rearrange("p a b -> p (a b)")[:, 0:ncols]
                P.dma("sp", st, w_in_v[:, kt, c0:c0 + ncols], writes=["xTt%d" % half], semkey="xTt%d" % half)
                P.op("act", (lambda st=st, kt=kt: lambda e: e.activation(out=Win[:, kt, 0:ncols], in_=st, func=AF.Copy, scale=mixw_s[:, kt:kt + 1]))(),
                     ["xTt%d" % half] + CK, ["Win"])

        def norm_tile(i):
            t0 = i * 512
            P.dma("sp", xTt[:, 0:4, :], xT_v[:, 0:4, t0:t0 + 512], writes=["xTt0"], semkey="xTt0")
            P.dma("sp", xTt[:, 4:8, :], xT_v[:, 4:8, t0:t0 + 512], writes=["xTt1"], semkey="xTt1")
            P.op("act", lambda e: e.activation(out=nT[:], in_=xTt[:], func=AF.Square), ["xTt0", "xTt1"], ["nT"])
            for kt in range(8):
                P.op("pe", (lambda kt=kt: lambda e: e.matmul(psf[:, 6, :], lhsT=onesb[:], rhs=nT[:, kt, :], start=(kt == 0), stop=(kt == 7)))(),
                     ["nT", "onesb"], [], ps=[6])
            P.op("act", lambda e: e.activation(out=lnv[:], in_=psf[:, 6, :], func=AF.Ln, scale=1.0 / D, bias=EPS), [], ["lnv"], ps=[6])
            P.op("act", lambda e: e.activation(out=rstd[:], in_=lnv[:], func=AF.Exp, scale=-0.5), ["lnv"], ["rstd"])
            P.op("dve", lambda e: e.tensor_tensor(out=nT[:], in0=xTt[:], in1=rstd[:].unsqueeze(1).to_broadcast([128, 8, 512]), op=ALU.mult),
                 ["xTt0", "xTt1", "rstd", "nT"], ["nT"])

        evac_rr = [0]

        def evac(out_ap, in_ap, reads, writes, ps, eng=None):
            if eng is None:
                eng = ("act", "dve")[evac_rr[0] % 2]
                evac_rr[0] += 1
            if eng == "act":
                P.op("act", lambda e: e.copy(out=out_ap, in_=in_ap), reads, writes, ps=ps)
            else:
                P.op(eng, lambda e: e.tensor_copy(out=out_ap, in_=in_ap), reads, writes, ps=ps)

        def proj_fm(bank, c0, reads_extra=()):
            for kt in range(8):
                P.op("pe", (lambda kt=kt: lambda e: e.matmul(psf[:, bank, :], lhsT=Win[:, kt, c0:c0 + 128], rhs=nT[:, kt, :], start=(kt == 0), stop=(kt == 7)))(),
                     ["Win", "nT"], [], ps=[bank])

        def proj_tm(bank, s, c0, n, col0=0):
            for kt in range(8):
                P.op("pe", (lambda kt=kt: lambda e: e.matmul(psf[:, bank, col0:col0 + n], lhsT=nT[:, kt, s * 128:(s + 1) * 128], rhs=Win[:, kt, c0:c0 + n], start=(kt == 0), stop=(kt == 7)))(),
                     ["Win", "nT"], [], ps=[bank])

        stg_n = [0]

        def wload(dst_fn, src_ap, ncols, scale_ap, dst_key, eng="act"):
            sidx = stg_n[0] % 2
            stg_n[0] += 1
            st = stg[:, sidx, 0:ncols]
            P.dma("sp", st, src_ap, writes=["stg%d" % sidx], semkey="stg%d" % sidx)
            if scale_ap is not None:
                P.op("act", lambda e: e.activation(out=dst_fn, in_=st, func=AF.Copy, scale=scale_ap), ["stg%d" % sidx] + CK, [dst_key])
            else:
                P.op(eng, lambda e: e.tensor_copy(out=dst_fn, in_=st), ["stg%d" % sidx], [dst_key])

        w_out_v = w_out.rearrange("(kt p) c -> p kt c", p=128)
        w_gate_v = w_gate.rearrange("(kt p) c -> p kt c", p=128)
        w_up_v = w_up.rearrange("(kt p) c -> p kt c", p=128)
        w_down_v = w_down.rearrange("(kt p) c -> p kt c", p=128)
        P.op("dve", lambda e: e.tensor_copy(out=wo_scale[:, 0:8], in_=wosc_s[:, 0:8]), CK, ["wo_scale"])
        P.op("dve", lambda e: e.tensor_scalar(out=wo_scale[:, 8:16], in0=wosc_s[:, 8:16], scalar1=1.0 - LAMBDA_INIT, scalar2=None, op0=ALU.mult), CK + ["wo_scale"], ["wo_scale"])

        pre_jobs = []
        for kt in range(16):
            pre_jobs.append((lambda kt=kt: wload(Wo[:, kt, :], w_out_v[:, kt, :], 1024, wo_scale[:, kt:kt + 1], "Wo")))
        for kt in range(8):
            for (cc0, ncl) in ((0, 1024), (1024, 1024), (2048, 768)):
                pre_jobs.append((lambda kt=kt, cc0=cc0, ncl=ncl: wload(Wg[:, kt, cc0:cc0 + ncl], w_gate_v[:, kt, cc0:cc0 + ncl], ncl, ffnw_s[:, kt:kt + 1], "Wg")))

        def gather(kind, i):
            src = (yda_in if kind == "da" else yssd_in)[i]
            dst = (yda_out if kind == "da" else yssd_out)[i]
            P.op("pool", lambda e: e.collective_compute("AllGather", ALU.bypass, replica_groups=[[0, 1], [2, 3], [4, 5], [6, 7]],
                                                        ins=[src.ap().opt()], outs=[dst.ap().opt()]),
                 ["y%s_in%d" % (kind, i)], ["y%s_out%d" % (kind, i)], semkey="cc_%s%d" % (kind, i), inc=1)

        cv = Alloc(ARENA0, 150000)
        kT = cv.take([128, 4, T], BF16)
        Vaug = cv.take([128, 32, 4, 132], BF16)
        qT = cv.take([128, 4, 512], BF16)
        PT = [[cv.take([128, 512], BF16) for _ in range(2)] for _ in range(2)]
        O_sb = cv.take([128, 8, 130], F32)
        rcp = cv.take([128, 8], F32); rn = cv.take([128, 4], F32)
        o1 = cv.take([128, 4, 128], F32); o2 = cv.take([128, 4, 128], F32)
        ssq4 = cv.take([128, 4], F32); r4 = cv.take([128, 4], F32)
        on = cv.take([128, 4, 128], BF16)
        yTda = cv.take([128, 4, 512], BF16)

        load_win(1288, 1536, "B")
        P.op("pool", lambda e: e.memset(Vaug[:, :, :, 128:129], 1.0), [], ["Vaug"])

        for i in range(NT_B):
            t0 = i * 512
            norm_tile(i)
            for h in range(4):
                bank = h % 4
                proj_fm(bank, h * 128)
                evac(qT[:, h, :], psf[:, bank, :], [], ["qT"], [bank], eng="dve")
            for h in range(4):
                bank = h % 4
                proj_fm(bank, 512 + h * 128)
                evac(kT[:, h, t0:t0 + 512], psf[:, bank, :], [], ["kT"], [bank], eng="dve")
            for s in range(4):
                bank = s % 4
                proj_tm(bank, s, 1024, 512)
                evac(Vaug[:, 4 * i + s, :, 0:128], psf[:, bank, :].rearrange("p (h v) -> p h v", h=4), [], ["Vaug"], [bank], eng="dve")
            nkb = 4 * i + 4
            for h in range(4):
                started = set()
                steps = list(range(nkb))

                def emit_S(st, kb, h=h, i=i):
                    qlo = max(0, kb - 4 * i) * 128
                    n = 512 - qlo
                    for m in range(2):
                        bank = 2 * (st % 2) + m
                        P.op("pe", (lambda m=m, bank=bank, kb=kb, qlo=qlo, n=n: lambda e: e.matmul(
                            psf[:, bank, 0:n], lhsT=kT[64 * m:64 * m + 64, h, kb * 128:(kb + 1) * 128],
                            rhs=qT[64 * m:64 * m + 64, h, qlo:512], start=True, stop=True))(),
                            ["kT", "qT"], [], ps=[bank])

                def emit_rest(st, kb, h=h, i=i):
                    qlo = max(0, kb - 4 * i) * 128
                    n = 512 - qlo
                    b = st % 2
                    for m in range(2):
                        bank = 2 * b + m
                        P.op("act", (lambda m=m, bank=bank, n=n, b=b: lambda e: e.activation(out=PT[m][b][:, 0:n], in_=psf[:, bank, 0:n], func=AF.Exp, scale=0.125))(),
                             [], ["PT%d%d" % (m, b)], ps=[bank])
                        if kb >= 4 * i:
                            P.op("pool", (lambda m=m, b=b: lambda e: e.tensor_tensor(out=PT[m][b][:, 0:128], in0=PT[m][b][:, 0:128], in1=Ub[:], op=ALU.mult))(),
                                 ["PT%d%d" % (m, b), "Ub"], ["PT%d%d" % (m, b)])
                    for m in range(2):
                        for qs in range(max(0, kb - 4 * i), 4):
                            a = m * 4 + qs
                            bank = 4 + a // 3
                            col = (a % 3) * 130
                            first = (kb == 0) and (bank not in started)
                            if kb == 0:
                                started.add(bank)
                            last = (kb == 4 * i + qs)
                            P.op("pe", (lambda m=m, b=b, qs=qs, qlo=qlo, bank=bank, col=col, first=first, last=last, kb=kb: lambda e: e.matmul(
                                psf[:, bank, col:col + 129], lhsT=PT[m][b][:, qs * 128 - qlo:qs * 128 - qlo + 128],
                                rhs=Vaug[:, kb, h, 0:129], start=first, stop=last, skip_group_check=True))(),
                                ["PT%d%d" % (m, b), "Vaug"], [], ps=[bank])

                emit_S(0, 0)
                for st, kb in enumerate(steps):
                    if st + 1 < nkb:
                        emit_S(st + 1, steps[st + 1])
                    emit_rest(st, kb)
                for bank, na in ((4, 3), (5, 3), (6, 2)):
                    a0 = (bank - 4) * 3
                    P.op("act", (lambda bank=bank, na=na, a0=a0: lambda e: e.copy(
                        out=O_sb[:, a0:a0 + na, :], in_=psf[:, bank, 0:na * 130].rearrange("p (a c) -> p a c", a=na)))(),
                        [], ["O_sb"], ps=[bank])
                P.op("dve", lambda e: e.reciprocal(out=rcp[:], in_=O_sb[:, :, 128]), ["O_sb"], ["rcp"])
                P.op("dve", lambda e: e.tensor_scalar(out=rn[:], in0=rcp[:, 4:8], scalar1=neglam[:, 0:1], scalar2=None, op0=ALU.mult), ["rcp", "neglam"], ["rn"])
                P.op("dve", lambda e: e.tensor_tensor(out=o1[:], in0=O_sb[:, 0:4, 0:128], in1=rcp[:, 0:4].unsqueeze(2).to_broadcast([128, 4, 128]), op=ALU.mult), ["O_sb", "rcp"], ["o1"])
                P.op("pool", lambda e: e.tensor_tensor(out=o2[:], in0=O_sb[:, 4:8, 0:128], in1=rn[:].unsqueeze(2).to_broadcast([128, 4, 128]), op=ALU.mult), ["O_sb", "rn"], ["o2"])
                P.op("dve", lambda e: e.tensor_tensor(out=o1[:], in0=o1[:], in1=o2[:], op=ALU.add), ["o1", "o2"], ["o1"])
                P.op("pool", lambda e: e.tensor_tensor(out=o2[:], in0=o1[:], in1=o1[:], op=ALU.mult), ["o1", "o2"], ["o2"])
                P.op("dve", lambda e: e.tensor_reduce(out=ssq4[:], in_=o2[:], axis=AX.X, op=ALU.add), ["o2"], ["ssq4"])
                P.op("act", lambda e: e.activation(out=r4[:], in_=ssq4[:], func=AF.Ln, scale=1.0 / 128, bias=EPS), ["ssq4"], ["r4"])
                P.op("act", lambda e: e.activation(out=r4[:], in_=r4[:], func=AF.Exp, scale=-0.5), ["r4"], ["r4"])
                P.op("dve", lambda e: e.tensor_tensor(out=on[:], in0=o1[:], in1=r4[:].unsqueeze(2).to_broadcast([128, 4, 128]), op=ALU.mult), ["o1", "r4"], ["on"])
                for qs in range(4):
                    P.op("pe", (lambda qs=qs: lambda e: e.transpose(psb[:, qs * 128:(qs + 1) * 128], on[:, qs, :], Ib[:]))(), ["on", "Ib"], [], ps=[7])
                P.op("act", (lambda h=h: lambda e: e.copy(out=yTda[:, h, :], in_=psb[:, 0:512]))(), [], ["yTda"], ps=[7])
            P.dma("sp", yda_in[i].ap().rearrange("(h p) t -> p h t", p=128), yTda[:], reads=["yTda"], writes=["yda_in%d" % i], semkey="yda_st")
            if i >= 1:
                gather("da", i - 1)
        if NT_B:
            gather("da", NT_B - 1)

        cv = Alloc(ARENA0, WO_OFF)
        ub = [cv.take([128, 516], F32) for _ in range(2)]
        cacc = [cv.take([128, 512], F32) for _ in range(2)]
        carry = cv.take([128, 6, 3], F32)
        xc = cv.take([128, 6, 512], BF16)
        zs = cv.take([128, 4, 512], F32)
        dtb = cv.take([128, 32], F32); dtv = cv.take([128, 32], F32); adt = cv.take([128, 32], F32)
        acs = cv.take([128, 32], F32); eacs = cv.take([128, 32], F32); dd = cv.take([128, 32], F32)
        decst = cv.take([128, 32], F32); cdec = cv.take([128, 32], F32); dtdec = cv.take([128, 32], F32)
        xtm = cv.take([128, 640], BF16)
        rhsc = cv.take([128, 8, 128], F32)
        expD = cv.take([128, 8, 128], F32)
        CBm = cv.take([128, 128], F32)
        MT = cv.take([128, 8, 128], BF16)
        xdt = tail.take([128, 8, 64], BF16)
        xdd = tail.take([128, 8, 64], BF16)
        t1 = cv.take([128, 8, 64], F32); t2 = cv.take([128, 8, 64], F32)
        Sst = cv.take([128, 8, 64], F32); Sbf = cv.take([128, 512], BF16)
        ssq1 = cv.take([128, 2], F32)
        ygn = cv.take([128, 512], BF16)
        yTs = cv.take([128, 4, 512], BF16)
        PB_KEYS = ["kT", "Vaug", "qT", "PT00", "PT01", "PT10", "PT11", "O_sb", "rcp", "rn", "o1", "o2", "ssq4", "r4", "on", "yTda"]
        P.op("pool", lambda e: e.memset(carry[:], 0.0), PB_KEYS, PB_KEYS + ["carry", "Wo", "Wg", "stg0", "stg1"])
        P.op("pool", lambda e: e.memset(Sst[:], 0.0), ["carry"], ["Sst"])
        P.op("pool", lambda e: e.memset(Sbf[:], 0.0), ["carry"], ["Sbf"])
        FENCE = ["carry"]

        P.op("pool", lambda e: e.memset(Win[:, :, 1280:1296], 0.0), ["Win"], ["Win"])
        load_win(0, 1288, "A")
        ZC, XC, DTC = 0, 512, 1280

        for i in range(NT_A if RUN_A else 0):
            t0 = i * 512
            norm_tile(i)
            for _ in range(6):
                if pre_jobs:
                    pre_jobs.pop(0)()
            for c in range(6):
                bank = c % 2
                u = ub[c % 2]
                acc = cacc[c % 2]
                ceng = "dve" if c % 2 == 0 else "pool"
                proj_fm(bank, XC + c * 128)
                P.op("pool", (lambda u=u, c=c: lambda e: e.tensor_copy(out=u[:, 0:3], in_=carry[:, c, :]))(), ["carry"] + FENCE, ["ub%d" % (c % 2)])
                P.op("act", (lambda u=u, bank=bank: lambda e: e.copy(out=u[:, 3:515], in_=psf[:, bank, :]))(), [], ["ub%d" % (c % 2)], ps=[bank])
                P.op("pool", (lambda u=u, c=c: lambda e: e.tensor_copy(out=carry[:, c, :], in_=u[:, 512:515]))(), ["ub%d" % (c % 2)], ["carry"])
                P.op(ceng, (lambda u=u, acc=acc, c=c: lambda e: e.tensor_scalar(out=acc[:], in0=u[:, 3:515], scalar1=convw_s[:, c, 3:4], scalar2=convb_s[:, c:c + 1], op0=ALU.mult, op1=ALU.add))(),
                     ["ub%d" % (c % 2)] + CK, ["cacc%d" % (c % 2)])
                for jt in (2, 1, 0):
                    if ceng == "dve":
                        P.op(ceng, (lambda u=u, acc=acc, c=c, jt=jt: lambda e: e.scalar_tensor_tensor(out=acc[:], in0=u[:, jt:jt + 512], scalar=convw_s[:, c, jt:jt + 1], in1=acc[:], op0=ALU.mult, op1=ALU.add))(),
                             ["ub%d" % (c % 2), "cacc%d" % (c % 2)], ["cacc%d" % (c % 2)])
                    else:
                        t2f = t2[:].rearrange("p h d -> p (h d)")
                        P.op(ceng, (lambda u=u, c=c, jt=jt, t2f=t2f: lambda e: e.tensor_scalar(out=t2f, in0=u[:, jt:jt + 512], scalar1=convw_s[:, c, jt:jt + 1], scalar2=None, op0=ALU.mult))(),
                             ["ub%d" % (c % 2), "t2"], ["t2"])
                        P.op(ceng, (lambda acc=acc, t2f=t2f: lambda e: e.tensor_tensor(out=acc[:], in0=acc[:], in1=t2f, op=ALU.add))(),
                             ["t2", "cacc%d" % (c % 2)], ["cacc%d" % (c % 2)])
                P.op("act", (lambda acc=acc, c=c: lambda e: e.activation(out=xc[:, c, :], in_=acc[:], func=AF.Silu))(), ["cacc%d" % (c % 2)], ["xc"])
            for s in range(4):
                bank = s % 2
                proj_tm(bank, s, ZC, 512)
                P.op("act", (lambda s=s, bank=bank: lambda e: e.activation(out=zs[:, s, :], in_=psf[:, bank, :], func=AF.Silu))(), FENCE, ["zs"], ps=[bank])
            for s in range(4):
                proj_tm(6, s, DTC, 8, col0=s * 8)
            P.op("dve", lambda e: e.tensor_tensor(out=dtb[:].rearrange("p (c h) -> p c h", c=4), in0=psf[:, 6, 0:32].rearrange("p (c h) -> p c h", c=4),
                                                  in1=ssdp_s[:, 0, :].unsqueeze(1).to_broadcast([128, 4, 8]), op=ALU.add), CK + FENCE, ["dtb"], ps=[6])
            P.op("act", lambda e: e.activation(out=dtb[:], in_=dtb[:], func=AF.Exp), ["dtb"], ["dtb"])
            P.op("act", lambda e: e.activation(out=dtv[:], in_=dtb[:], func=AF.Ln, bias=1.0), ["dtb"], ["dtv"])
            P.op("dve", lambda e: e.tensor_tensor(out=adt[:].rearrange("p (c h) -> p c h", c=4), in0=dtv[:].rearrange("p (c h) -> p c h", c=4),
                                                  in1=Abc[:].unsqueeze(1).to_broadcast([128, 4, 8]), op=ALU.mult), ["dtv", "Abc"], ["adt"])
            P.op("pe", lambda e: e.matmul(psf[:, 6, 32:64], lhsT=Uf[:], rhs=adt[:], start=True, stop=True), ["adt"] + CK, [], ps=[6])
            P.op("pe", lambda e: e.matmul(psf[:, 6, 64:96], lhsT=onesf[:], rhs=adt[:], start=False, stop=True, skip_group_check=True), ["adt", "onesf"], [], ps=[6])
            P.op("act", lambda e: e.copy(out=acs[:], in_=psf[:, 6, 32:64]), [], ["acs"], ps=[6])
            P.op("act", lambda e: e.activation(out=eacs[:], in_=psf[:, 6, 32:64], func=AF.Exp), [], ["eacs"], ps=[6])
            P.op("act", lambda e: e.activation(out=cdec[:], in_=psf[:, 6, 64:96], func=AF.Exp), [], ["cdec"], ps=[6])
            P.op("dve", lambda e: e.tensor_tensor(out=dd[:], in0=psf[:, 6, 64:96], in1=acs[:], op=ALU.subtract), ["acs"], ["dd"], ps=[6])
            P.op("act", lambda e: e.activation(out=decst[:], in_=dd[:], func=AF.Exp), ["dd"], ["decst"])
            P.op("dve", lambda e: e.tensor_tensor(out=dtdec[:], in0=dtv[:], in1=decst[:], op=ALU.mult), ["dtv", "decst"], ["dtdec"])

            for c in range(4):
                cs = slice(c * 128, (c + 1) * 128)
                hs = slice(c * 8, (c + 1) * 8)
                for ct in range(5):
                    P.op("pe", (lambda ct=ct, cs=cs: lambda e: e.transpose(psb[:, ct * 128:(ct + 1) * 128], xc[:, ct, cs], Ib[:]))(), ["xc", "Ib"], [], ps=[7])
                P.op("act", lambda e: e.copy(out=xtm[:], in_=psb[:, 0:640]), [], ["xtm"], ps=[7])
                P.op("dve", (lambda hs=hs: lambda e: e.tensor_tensor(out=rhsc[:], in0=Uf[:].unsqueeze(1).to_broadcast([128, 8, 128]),
                                                                    in1=adt[:, hs].unsqueeze(2).to_broadcast([128, 8, 128]), op=ALU.mult))(), ["adt"] + CK, ["rhsc"])
                for hh in range(2):
                    P.op("pe", (lambda hh=hh: lambda e: e.matmul(psf[:, 2 + hh, :], lhsT=Gf[:], rhs=rhsc[:, 4 * hh:4 * hh + 4, :], start=True, stop=True))(), ["rhsc"] + CK, [], ps=[2 + hh])
                P.op("pe", (lambda cs=cs: lambda e: e.matmul(psf[:, 4, 0:128], lhsT=xc[:, 4, cs], rhs=xc[:, 5, cs], start=True, stop=True))(), ["xc"], [], ps=[4])
                P.op("dve", lambda e: e.tensor_tensor(out=CBm[:], in0=psf[:, 4, 0:128], in1=Uf[:], op=ALU.mult), CK, ["CBm"], ps=[4])
                for hh in range(2):
                    P.op("act", (lambda hh=hh: lambda e: e.activation(out=expD[:, 4 * hh:4 * hh + 4, :], in_=psf[:, 2 + hh, :].rearrange("p (h l) -> p h l", h=4), func=AF.Exp))(),
                         [], ["expD"], ps=[2 + hh])
                P.op("dve", lambda e: e.tensor_tensor(out=MT[:], in0=expD[:], in1=CBm[:].unsqueeze(1).to_broadcast([128, 8, 128]), op=ALU.mult), ["expD", "CBm"], ["MT"])
                xv = xtm[:, 0:512].rearrange("p (h d) -> p h d", h=8)
                P.op("pool", (lambda hs=hs, xv=xv: lambda e: e.tensor_tensor(out=xdt[:], in0=xv, in1=dtv[:, hs].unsqueeze(2).to_broadcast([128, 8, 64]), op=ALU.mult))(), ["xtm", "dtv"], ["xdt"])
                P.op("pool", (lambda hs=hs, xv=xv: lambda e: e.tensor_tensor(out=xdd[:], in0=xv, in1=dtdec[:, hs].unsqueeze(2).to_broadcast([128, 8, 64]), op=ALU.mult))(), ["xtm", "dtdec"], ["xdd"])
                for h in range(8):
                    P.op("pe", (lambda h=h: lambda e: e.matmul(psf[:, 5, h * 64:(h + 1) * 64], lhsT=MT[:, h, :], rhs=xdt[:, h, :], start=(h == 0), stop=True, skip_group_check=True))(),
                         ["MT", "xdt"], [], ps=[5])
                P.op("pe", (lambda cs=cs: lambda e: e.matmul(psf[:, 0, :], lhsT=xc[:, 5, cs], rhs=Sbf[:], start=True, stop=True))(), ["xc", "Sbf"], [], ps=[0])
                P.op("dve", (lambda hs=hs: lambda e: e.tensor_tensor(out=t1[:], in0=psf[:, 0, :].rearrange("p (h d) -> p h d", h=8),
                                                                    in1=eacs[:, hs].unsqueeze(2).to_broadcast([128, 8, 64]), op=ALU.mult))(), ["eacs"], ["t1"], ps=[0])
                P.op("dve", lambda e: e.tensor_tensor(out=t1[:], in0=t1[:], in1=psf[:, 5, :].rearrange("p (h d) -> p h d", h=8), op=ALU.add), ["t1"], ["t1"], ps=[5])
                P.op("pool", (lambda xv=xv: lambda e: e.tensor_tensor(out=t2[:], in0=xv, in1=Dsk[:], op=ALU.mult))(), ["xtm", "Dsk"], ["t2"])
                P.op("dve", lambda e: e.tensor_tensor(out=t1[:], in0=t1[:], in1=t2[:], op=ALU.add), ["t1", "t2"], ["t1"])
                P.op("dve", (lambda c=c: lambda e: e.tensor_tensor(out=t1[:], in0=t1[:], in1=zs[:, c, :].rearrange("p (h d) -> p h d", h=8), op=ALU.mult))(), ["t1", "zs"], ["t1"])
                P.op("act", lambda e: e.activation(out=t2[:], in_=t1[:], func=AF.Square, accum_out=ssq1[:, 0:1]), ["t1", "t2"], ["t2", "ssq1"])
                P.op("act", lambda e: e.activation(out=ssq1[:, 1:2], in_=ssq1[:, 0:1], func=AF.Ln, scale=1.0 / 512, bias=EPS), ["ssq1"], ["ssq1"])
                P.op("act", lambda e: e.activation(out=ssq1[:, 1:2], in_=ssq1[:, 1:2], func=AF.Exp, scale=-0.5), ["ssq1"], ["ssq1"])
                P.op("dve", lambda e: e.tensor_scalar(out=ygn[:], in0=t1[:].rearrange("p h d -> p (h d)"), scalar1=ssq1[:, 1:2], scalar2=None, op0=ALU.mult), ["t1", "ssq1"], ["ygn"])
                P.op("pe", lambda e: e.matmul(psf[:, 1, :], lhsT=xtm[:, 512:640], rhs=xdd[:].rearrange("p h d -> p (h d)"), start=True, stop=True), ["xtm", "xdd"], [], ps=[1])
                P.op("dve", (lambda hs=hs: lambda e: e.tensor_tensor(out=Sst[:], in0=Sst[:], in1=cdec[:, hs].unsqueeze(2).to_broadcast([128, 8, 64]), op=ALU.mult))(), ["Sst", "cdec"], ["Sst"])
                P.op("dve", lambda e: e.tensor_tensor(out=Sst[:], in0=Sst[:], in1=psf[:, 1, :].rearrange("p (h d) -> p h d", h=8), op=ALU.add), ["Sst"], ["Sst"], ps=[1])
                P.op("act", lambda e: e.copy(out=Sbf[:], in_=Sst[:].rearrange("p h d -> p (h d)")), ["Sst"], ["Sbf"])
                for ft in range(4):
                    P.op("pe", (lambda ft=ft: lambda e: e.transpose(psb[:, 512 + ft * 128:512 + (ft + 1) * 128], ygn[:, ft * 128:(ft + 1) * 128], Ib[:]))(), ["ygn", "Ib"], [], ps=[7])
                P.op("act", (lambda cs=cs: lambda e: e.copy(out=yTs[:, :, cs], in_=psb[:, 512:1024].rearrange("p (f t) -> p f t", f=4)))(), [], ["yTs"], ps=[7])
            P.dma("sp", yssd_in[i].ap().rearrange("(f p) t -> p f t", p=128), yTs[:], reads=["yTs"], writes=["yssd_in%d" % i], semkey="yssd_st")
            if i >= 1:
                gather("ssd", i - 1)
        if RUN_A and NT_A:
            gather("ssd", NT_A - 1)

        while pre_jobs and RUN_P2:
            pre_jobs.pop(0)()

        if DEBUG:
            for i in range(NT_B):
                P.dma("sp", dbg_da[:, i * 512:(i + 1) * 512], yda_in[i].ap(), reads=["yda_in%d" % i], semkey="dbg1")
            for i in range(NT_A if RUN_A else 0):
                P.dma("sp", dbg_ssd[:, i * 512:(i + 1) * 512], yssd_in[i].ap(), reads=["yssd_in%d" % i], semkey="dbg2")

        PA_KEYS = ["ub0", "ub1", "cacc0", "cacc1", "carry", "xc", "zs", "dtb", "dtv", "adt", "acs", "eacs", "dd", "decst", "cdec", "dtdec",
                   "xtm", "rhsc", "expD", "CBm", "MT", "xdt", "xdd", "t1", "t2", "Sst", "Sbf", "ssq1", "ygn", "yTs"]
        cv = Alloc(1024, WO_OFF)
        Wu = cv.take([128, 8, DFF], BF16)
        Wd = cv.take([128, 22, 1024], BF16)
        c0b = cv.take([128, 16, 128], BF16)
        c1h = cv.take([128, 8, 128], BF16)
        h1 = [cv.take([128, 1024], F32) for _ in range(2)]
        n2 = cv.take([128, 1024], BF16)
        n2T = cv.take([128, 8, 128], BF16)
        P.dma("sp", finw_s[:], finw, writes=["finw"], semkey="finw")
        P.op("pool", lambda e: e.memset(st2[:], 0.0), PA_KEYS + CK + ["Win", "nT", "xTt0", "xTt1", "lnv", "rstd", "Ub", "onesb", "onesf", "Abc", "Dsk", "lamt", "lam2"],
             PA_KEYS + ["Wd", "Wu", "p2fence", "st2"])
        F2 = ["p2fence"]
        for kt in range(8 if RUN_P2 else 0):
            for (cc0, ncl) in ((0, 1024), (1024, 1024), (2048, 768)):
                (lambda kt=kt, cc0=cc0, ncl=ncl: wload(Wu[:, kt, cc0:cc0 + ncl], w_up_v[:, kt, cc0:cc0 + ncl], ncl, ffnw_s[:, kt:kt + 1], "Wu"))()
        for kt in range(22 if RUN_P2 else 0):
            (lambda kt=kt: wload(Wd[:, kt, :], w_down_v[:, kt, :], 1024, None, "Wd", eng="pool"))()

        ys_v = [t_.ap().rearrange("(kt p) t -> p kt t", p=128) for t_ in yssd_out]
        yd_v = [t_.ap().rearrange("(kt p) t -> p kt t", p=128) for t_ in yda_out]
        for tt in range(16 if RUN_P2 else 0):
            T0 = tt * 128
            hb = h1[tt % 2]
            hk = "h1%d" % (tt % 2)
            P.dma("sp", hb[:], xtok[T0:T0 + 128, :], reads=F2, writes=[hk], semkey=hk)
            ti, tc = T0 // 512, T0 % 512
            for half, (src, srckey) in enumerate(((ys_v, "yssd_out"), (yd_v, "yda_out"))):
                ck = "c0b%d" % half
                P.dma("sp", c0b[:, 8 * half:8 * half + 8, :], src[ti][:, :, tc:tc + 128], reads=[srckey + str(ti)] + F2, writes=[ck], semkey=ck)
                P.dma("sp", c1h[:], src[4 + ti][:, :, tc:tc + 128], reads=[srckey + str(4 + ti)] + F2, writes=["c1h"], semkey="c1h")
                P.op("pool", lambda e: e.tensor_scalar(out=c1h[:], in0=c1h[:], scalar1=flags_s[:, 1:2], scalar2=None, op0=ALU.mult), ["c1h"], ["c1h"])
                P.op("dve", (lambda half=half: lambda e: e.scalar_tensor_tensor(out=c0b[:, 8 * half:8 * half + 8, :], in0=c0b[:, 8 * half:8 * half + 8, :], scalar=flags_s[:, 0:1], in1=c1h[:], op0=ALU.mult, op1=ALU.add))(),
                     [ck, "c1h"], [ck])
            for dh in range(2):
                bank = dh
                for kt in range(16):
                    P.op("pe", (lambda kt=kt, dh=dh, bank=bank: lambda e: e.matmul(psf[:, bank, :], lhsT=c0b[:, kt, :], rhs=Wo[:, kt, dh * 512:(dh + 1) * 512], start=(kt == 0), stop=(kt == 15)))(),
                         ["c0b0", "c0b1", "Wo"], [], ps=[bank])
                P.op("dve", (lambda dh=dh, bank=bank, hb=hb: lambda e: e.tensor_tensor(out=hb[:, dh * 512:(dh + 1) * 512], in0=psf[:, bank, :], in1=hb[:, dh * 512:(dh + 1) * 512], op=ALU.add))(),
                     [hk], [hk], ps=[bank])
            P.op("act", (lambda hb=hb: lambda e: e.activation(out=n2[:], in_=hb[:], func=AF.Square, accum_out=st2[:, 0:1]))(), [hk, "n2"], ["n2", "st2"])
            P.op("act", lambda e: e.activation(out=st2[:, 1:2], in_=st2[:, 0:1], func=AF.Ln, scale=1.0 / D, bias=EPS), ["st2"], ["st2"])
            P.op("act", lambda e: e.activation(out=st2[:, 1:2], in_=st2[:, 1:2], func=AF.Exp, scale=-0.5), ["st2"], ["st2"])
            P.op("dve", (lambda hb=hb: lambda e: e.tensor_scalar(out=n2[:], in0=hb[:], scalar1=st2[:, 1:2], scalar2=None, op0=ALU.mult))(), [hk, "st2", "n2"], ["n2"])
            for kt in range(8):
                P.op("pe", (lambda kt=kt: lambda e: e.transpose(psb[:, kt * 128:(kt + 1) * 128], n2[:, kt * 128:(kt + 1) * 128], Ib[:]))(), ["n2", "Ib"], [], ps=[7])
            P.op("act", lambda e: e.copy(out=n2T[:], in_=psb[:].rearrange("p (k t) -> p k t", k=8)), F2, ["n2T"], ps=[7])
            for f in range(22):
                gb = 2 + (f % 2)
                ubk = 4 + (f % 2)
                for kt in range(8):
                    P.op("pe", (lambda kt=kt, f=f, gb=gb: lambda e: e.matmul(psf[:, gb, 0:128], lhsT=Wg[:, kt, f * 128:(f + 1) * 128], rhs=n2T[:, kt, :], start=(kt == 0), stop=(kt == 7)))(),
                         ["Wg", "n2T"], [], ps=[gb])
                for kt in range(8):
                    P.op("pe", (lambda kt=kt, f=f, ubk=ubk: lambda e: e.matmul(psf[:, ubk, 0:128], lhsT=Wu[:, kt, f * 128:(f + 1) * 128], rhs=n2T[:, kt, :], start=(kt == 0), stop=(kt == 7)))(),
                         ["Wu", "n2T"], [], ps=[ubk])
                P.op("act", (lambda f=f, gb=gb: lambda e: e.activation(out=sg[:, f % 2, :], in_=psf[:, gb, 0:128], func=AF.Silu))(), [], ["sg%d" % (f % 2)], ps=[gb])
                P.op("dve", (lambda f=f, ubk=ubk: lambda e: e.tensor_tensor(out=actT[:, f, :], in0=sg[:, f % 2, :], in1=psf[:, ubk, 0:128], op=ALU.mult))(), ["sg%d" % (f % 2)], ["actT"], ps=[ubk])
            for dh in range(2):
                bank = dh
                for f in range(22):
                    P.op("pe", (lambda f=f, dh=dh, bank=bank: lambda e: e.matmul(psf[:, bank, :], lhsT=actT[:, f, :], rhs=Wd[:, f, dh * 512:(dh + 1) * 512], start=(f == 0), stop=(f == 21)))(),
                         ["actT", "Wd"], [], ps=[bank])
                P.op("dve", (lambda dh=dh, bank=bank, hb=hb: lambda e: e.tensor_tensor(out=hb[:, dh * 512:(dh + 1) * 512], in0=psf[:, bank, :], in1=hb[:, dh * 512:(dh + 1) * 512], op=ALU.add))(),
                     [hk], [hk], ps=[bank])
            P.op("act", (lambda hb=hb: lambda e: e.activation(out=n2[:], in_=hb[:], func=AF.Square, accum_out=st2[:, 2:3]))(), [hk, "n2"], ["n2", "st2"])
            P.op("act", lambda e: e.activation(out=st2[:, 3:4], in_=st2[:, 2:3], func=AF.Ln, scale=1.0 / D, bias=EPS), ["st2"], ["st2"])
            P.op("act", lambda e: e.activation(out=st2[:, 3:4], in_=st2[:, 3:4], func=AF.Exp, scale=-0.5), ["st2"], ["st2"])
            P.op("dve", (lambda hb=hb: lambda e: e.scalar_tensor_tensor(out=hb[:], in0=hb[:], scalar=st2[:, 3:4], in1=finw_s[:], op0=ALU.mult, op1=ALU.mult))(), [hk, "st2", "finw"], [hk])
            P.dma("sp", out[T0:T0 + 128, :], hb[:], reads=[hk], writes=["out"], semkey="ost%d" % (tt % 2))

        P.emit()
    return nc


_NC = None


def kernel(x, mix_norm_w, w_in, conv_w, conv_b, dt_bias, a_log, d_skip, ssd_norm_w,
           lam_q1, lam_k1, lam_q2, lam_k2, subln_w, w_out, ffn_norm_w, w_gate, w_up, w_down,
           final_norm_w):
    global _NC
    f32 = np.float32
    x = np.asarray(x, f32)
    w_in0 = np.asarray(w_in, f32)[0]
    conv_w0 = np.asarray(conv_w, f32)[0]
    conv_b0 = np.asarray(conv_b, f32)[0]

    def pk(v):
        v = np.asarray(v, f32).reshape(-1, 128)
        return np.ascontiguousarray(v.T)

    def rep(v):
        v = np.asarray(v, f32).reshape(1, -1)
        return np.ascontiguousarray(np.broadcast_to(v, (128, v.shape[1])))

    ar = np.arange(128)
    cU = (ar[:, None] <= ar[None, :]).astype(f32)
    cG = (ar[:, None] > ar[None, :]).astype(f32)
    cI = np.eye(128, dtype=f32)
    shared = dict(
        mixw=pk(np.asarray(mix_norm_w, f32)[0]),
        lamv=np.ascontiguousarray(np.stack([rep(np.asarray(v, f32)[0]) for v in (lam_q1, lam_k1, lam_q2, lam_k2)], axis=1)),
        w_out=np.ascontiguousarray(np.asarray(w_out, f32)[0]),
        wosc=pk(np.concatenate([np.asarray(ssd_norm_w, f32)[0]] + [np.asarray(subln_w, f32)[0]] * 8)),
        ffnw=pk(np.asarray(ffn_norm_w, f32)[0]),
        w_gate=np.ascontiguousarray(np.asarray(w_gate, f32)[0]),
        w_up=np.ascontiguousarray(np.asarray(w_up, f32)[0]),
        w_down=np.ascontiguousarray(np.asarray(w_down, f32)[0]),
        finw=rep(np.asarray(final_norm_w, f32)),
        cU=cU, cG=cG, cI=cI,
    )
    percore_j = []
    for j in range(2):
        cols = np.concatenate([
            np.arange(512 * j, 512 * j + 512),
            np.arange(1024 + 512 * j, 1024 + 512 * j + 512),
            np.arange(2048 + 128 * j, 2048 + 128 * j + 128),
            np.arange(2304 + 128 * j, 2304 + 128 * j + 128),
            np.arange(2560 + 8 * j, 2560 + 8 * j + 8),
            np.arange(2576 + 512 * j, 2576 + 512 * j + 512),
            np.arange(3600 + 512 * j, 3600 + 512 * j + 512),
            np.arange(4624 + 512 * j, 4624 + 512 * j + 512),
        ])
        ch = np.concatenate([np.arange(512 * j, 512 * j + 512), np.arange(1024 + 128 * j, 1024 + 128 * j + 128),
                             np.arange(1280 + 128 * j, 1280 + 128 * j + 128)])
        cw = conv_w0[:, ch]
        convw = np.ascontiguousarray(cw.reshape(4, 6, 128).transpose(2, 1, 0))
        convb = np.ascontiguousarray(conv_b0[ch].reshape(6, 128).T)
        hsl = slice(8 * j, 8 * j + 8)
        ssdp = np.ascontiguousarray(np.stack([rep(np.asarray(dt_bias, f32)[0][hsl]), rep(np.asarray(a_log, f32)[0][hsl]),
                                              rep(np.asarray(d_skip, f32)[0][hsl])], axis=1))
        fl = np.zeros((128, 2), f32)
        fl[:, j] = 1.0
        percore_j.append(dict(w_in=np.ascontiguousarray(w_in0[:, cols]), convw=convw, convb=convb, ssdp=ssdp, flags=fl))

    in_maps = []
    for c in range(8):
        b, j = c // 2, c % 2
        m = dict(shared)
        m.update(percore_j[j])
        m["xT"] = np.ascontiguousarray(x[b].T)
        m["xtok"] = np.ascontiguousarray(x[b, j * 2048:(j + 1) * 2048, :])
        in_maps.append(m)

    if _NC is None:
        _NC = build_program()
    res = run_bass_kernel_spmd(_NC, in_maps, core_ids=list(range(8)))
    outp = np.empty((4, T, D), f32)
    for c in range(8):
        b, j = c // 2, c % 2
        outp[b, j * 2048:(j + 1) * 2048, :] = res.results[c]["out"]
    if DEBUG:
        kernel.dbg = [dict(da=np.asarray(r["dbg_da"]), ssd=np.asarray(r["dbg_ssd"])) for r in res.results]
    return outp
```

```python
import bisect
from contextlib import ExitStack

import numpy as np
import ml_dtypes

import concourse.bass as bass
import concourse.mybir as mybir
from concourse.bass_utils import run_bass_kernel_spmd

F32 = mybir.dt.float32
BF16 = mybir.dt.bfloat16
AF = mybir.ActivationFunctionType
ALU = mybir.AluOpType
AX = mybir.AxisListType

ENGS = ("pe", "act", "dve", "pool", "sp")
EPS = 1e-5
T = 4096
D = 1024
DFF = 2816
NCOL = 2824
LAMBDA_INIT = 0.8 - 0.6 * 1.0
DEBUG = False
RUN_A = True
RUN_P2 = True
NT_B = 8
NT_A = 8


class Prog:
    def __init__(self, nc):
        self.nc = nc
        self.ops = []

    def op(self, eng, fn, reads=(), writes=(), ps=(), dma=False, semkey=None, inc=1):
        self.ops.append(dict(eng=eng, fn=fn, reads=tuple(reads), writes=tuple(writes), ps=tuple(ps),
                             dma=dma, semkey=semkey, inc=inc, deps=set(), signal=False))
        return len(self.ops) - 1

    def dma(self, eng, out, in_, reads=(), writes=(), semkey=None):
        return self.op(eng, lambda e: e.dma_start(out=out, in_=in_), reads, writes,
                       dma=True, semkey=semkey, inc=16)

    def analyze(self):
        ops = self.ops
        last_w, readers = {}, {}
        last_ps = {}
        for i, o in enumerate(ops):
            for k in o["reads"]:
                if k in last_w:
                    o["deps"].add(last_w[k])
            for k in o["writes"]:
                if k in last_w:
                    o["deps"].add(last_w[k])
                for r in readers.get(k, ()):
                    o["deps"].add(r)
            for k in o["reads"]:
                readers.setdefault(k, []).append(i)
            for k in o["writes"]:
                last_w[k] = i
                readers[k] = []
            for b in o["ps"]:
                d = last_ps.setdefault(b, {})
                for e2, j in d.items():
                    if e2 != o["eng"]:
                        o["deps"].add(j)
                d[o["eng"]] = i
            o["deps"].discard(i)
        for i, o in enumerate(ops):
            if o["eng"] == "pe" and not o["dma"]:
                o["deps"] = {d for d in o["deps"] if not (ops[d]["eng"] == "pe" and ops[d]["semkey"] is None)}
            best = {}
            keep = set()
            for d in o["deps"]:
                if ops[d]["semkey"] is not None:
                    keep.add(d)
                else:
                    e2 = ops[d]["eng"]
                    if best.get(e2, -1) < d:
                        best[e2] = d
            keep.update(best.values())
            o["deps"] = keep
            for d in o["deps"]:
                ops[d]["signal"] = True
        cnt = {e: 0 for e in ENGS}
        dcnt = {}
        self.own_keys = []
        self._idx = {}
        for i, o in enumerate(ops):
            if o["semkey"] is not None:
                k = o["semkey"]
                if k not in dcnt:
                    dcnt[k] = 0
                    self.own_keys.append(k)
                dcnt[k] += o["inc"]
                o["sem"] = ("own", k)
                o["val"] = dcnt[k]
                self._idx.setdefault(k, []).append((i, dcnt[k]))
            elif o["signal"]:
                cnt[o["eng"]] += 1
                o["sem"] = ("eng", o["eng"])
                o["val"] = cnt[o["eng"]]
        self.final = dict(dcnt)

    def count_before(self, key, i):
        lst = self._idx[key]
        p = bisect.bisect_left(lst, (i, -1))
        return lst[p - 1][1] if p > 0 else 0

    def emit(self):
        nc = self.nc
        self.analyze()
        ops = self.ops
        with ExitStack() as es:
            sems = {}
            for e in ENGS:
                sems[("eng", e)] = es.enter_context(nc.semaphore("s_" + e))
            for n, k in enumerate(self.own_keys):
                sems[("own", k)] = es.enter_context(nc.semaphore("o%d" % n))
            block = es.enter_context(nc.Block())

            def run_engine(ename, eng):
                waited = {}
                for i, o in enumerate(ops):
                    if o["eng"] != ename:
                        continue
                    need = {}
                    for d in o["deps"]:
                        do = ops[d]
                        s, v = do["sem"], do["val"]
                        if do["semkey"] is not None:
                            v = max(v, self.count_before(do["semkey"], i))
                        if need.get(s, 0) < v:
                            need[s] = v
                    for s, v in need.items():
                        if waited.get(s, 0) >= v:
                            continue
                        eng.wait_ge(sems[s], v)
                        waited[s] = v
                    ins = o["fn"](eng)
                    if o["semkey"] is not None:
                        ins.then_inc(sems[o["sem"]], o["inc"])
                    elif o["signal"]:
                        ins.then_inc(sems[o["sem"]], 1)
                return waited

            @block.tensor
            def _(e):
                run_engine("pe", e)

            @block.scalar
            def _(e):
                run_engine("act", e)

            @block.vector
            def _(e):
                run_engine("dve", e)

            @block.gpsimd
            def _(e):
                run_engine("pool", e)

            @block.sync
            def _(e):
                w = run_engine("sp", e)
                for k, v in self.final.items():
                    if w.get(("own", k), 0) < v:
                        e.wait_ge(sems[("own", k)], v)


def build_program():
    nc = bass.Bass("TRN2", target_bir_lowering=False)

    def din(name, shape, dt=F32):
        return nc.dram_tensor(name, list(shape), dt, kind="ExternalInput").ap()

    xT = din("xT", [D, T])
    xtok = din("xtok", [2048, D])
    w_in = din("w_in", [D, NCOL])
    mixw = din("mixw", [128, 8])
    convw = din("convw", [128, 6, 4])
    convb = din("convb", [128, 6])
    ssdp = din("ssdp", [128, 3, 8])
    lamv = din("lamv", [128, 4, 64])
    w_out = din("w_out", [2048, D])
    wosc = din("wosc", [128, 16])
    ffnw = din("ffnw", [128, 8])
    w_gate = din("w_gate", [D, DFF])
    w_up = din("w_up", [D, DFF])
    w_down = din("w_down", [DFF, D])
    finw = din("finw", [128, D])
    cU = din("cU", [128, 128])
    cG = din("cG", [128, 128])
    cI = din("cI", [128, 128])
    flags = din("flags", [128, 2])
    out = nc.dram_tensor("out", [2048, D], F32, kind="ExternalOutput").ap()
    if DEBUG:
        dbg_da = nc.dram_tensor("dbg_da", [512, T], BF16, kind="ExternalOutput").ap()
        dbg_ssd = nc.dram_tensor("dbg_ssd", [512, T], BF16, kind="ExternalOutput").ap()

    yda_in = [nc.dram_tensor("yda_in%d" % i, [512, 512], BF16) for i in range(8)]
    yda_out = [nc.dram_tensor("yda_out%d" % i, [1024, 512], BF16) for i in range(8)]
    yssd_in = [nc.dram_tensor("yssd_in%d" % i, [512, 512], BF16) for i in range(8)]
    yssd_out = [nc.dram_tensor("yssd_out%d" % i, [1024, 512], BF16) for i in range(8)]

    P = Prog(nc)
    with ExitStack() as es:
        MEMW = 53000
        mem = es.enter_context(nc.sbuf_tensor("mem", [128, MEMW], F32))

        class Alloc:
            def __init__(self, start, end):
                self.off, self.end = start, end

            def take(self, shape, dt=F32):
                n = int(np.prod(shape[1:]))
                nbytes = n * (2 if dt == BF16 else 4)
                nbytes = (nbytes + 3) // 4 * 4
                assert self.off % 4 == 0 and self.off + nbytes <= self.end, (self.off, nbytes, self.end, shape)
                a = mem[:, self.off // 4:(self.off + nbytes) // 4]
                self.off += nbytes
                if dt == BF16:
                    a = a.bitcast(BF16)[:, 0:n]
                if len(shape) == 3:
                    a = a.rearrange("p (a b) -> p a b", a=shape[1])
                elif len(shape) == 4:
                    a = a.rearrange("p (a b c) -> p a b c", a=shape[1], b=shape[2])
                return a

        psf = es.enter_context(nc.psum_tensor("psf", [128, 7, 512], F32))
        psb = es.enter_context(nc.psum_tensor("psb", [128, 1024], BF16))

        pers = Alloc(0, 1024)
        cst = Alloc(1024, 8192)
        WO_OFF, WG_OFF, STG_OFF, TAIL_OFF, MEM_END = 111440, 144208, 189264, 197456, 212000

        def sb(name, shape, dt=F32):
            return pers.take(shape, dt)

        def sc(name, shape, dt=F32):
            return cst.take(shape, dt)

        Ib = sb("Ib", [128, 128], BF16)
        wosc_s = sb("wosc_s", [128, 16]); ffnw_s = sb("ffnw_s", [128, 8]); flags_s = sb("flags_s", [128, 2])
        wo_scale = sb("wo_scale", [128, 16]); st2 = sb("st2", [128, 4]); neglam = sb("neglam", [128, 1])
        mixw_s = sb("mixw_s", [128, 8])
        Uf = sc("Uf", [128, 128]); Gf = sc("Gf", [128, 128]); If = sc("If", [128, 128])
        Ub = sc("Ub", [128, 128], BF16)
        onesb = sc("onesb", [128, 128], BF16); onesf = sc("onesf", [128, 128])
        convw_s = sc("convw_s", [128, 6, 4]); convb_s = sc("convb_s", [128, 6])
        ssdp_s = sc("ssdp_s", [128, 3, 8]); lamv_s = sc("lamv_s", [128, 4, 64])
        Abc = sc("Abc", [128, 8]); Dsk = sc("Dsk", [128, 8, 64])
        lam2 = sc("lam2", [128, 2]); lamt = sc("lamt", [128, 2, 64])

        small_loads = [(Uf, cU), (Gf, cG), (If, cI), (mixw_s, mixw), (convw_s, convw), (convb_s, convb),
                       (ssdp_s, ssdp), (lamv_s, lamv), (wosc_s, wosc), (ffnw_s, ffnw), (flags_s, flags)]
        for n, (dst, src) in enumerate(small_loads):
            P.dma("sp", dst[:], src, writes=["c%d" % n], semkey="c%d" % n)
        CK = ["c%d" % n for n in range(len(small_loads))]
        P.op("dve", lambda e: e.tensor_copy(out=Ub[:], in_=Uf[:]), CK, ["Ub"])
        P.op("dve", lambda e: e.tensor_copy(out=Ib[:], in_=If[:]), CK, ["Ib"])
        P.op("pool", lambda e: e.memset(onesb[:], 1.0), [], ["onesb"])
        P.op("pool", lambda e: e.memset(onesf[:], 1.0), [], ["onesf"])
        P.op("act", lambda e: e.activation(out=Abc[:], in_=ssdp_s[:, 1, :], func=AF.Exp), CK, ["Abc"])
        P.op("dve", lambda e: e.tensor_scalar(out=Abc[:], in0=Abc[:], scalar1=-1.0, scalar2=None, op0=ALU.mult), ["Abc"], ["Abc"])
        P.op("dve", lambda e: e.tensor_copy(out=Dsk[:], in_=ssdp_s[:, 2, :].unsqueeze(2).to_broadcast([128, 8, 64])), CK, ["Dsk"])
        P.op("dve", lambda e: e.tensor_tensor(out=lamt[:, 0, :], in0=lamv_s[:, 0, :], in1=lamv_s[:, 1, :], op=ALU.mult), CK, ["lamt"])
        P.op("dve", lambda e: e.tensor_tensor(out=lamt[:, 1, :], in0=lamv_s[:, 2, :], in1=lamv_s[:, 3, :], op=ALU.mult), ["lamt"], ["lamt"])
        P.op("dve", lambda e: e.tensor_reduce(out=lam2[:], in_=lamt[:], axis=AX.X, op=ALU.add), ["lamt"], ["lam2"])
        P.op("act", lambda e: e.activation(out=lam2[:], in_=lam2[:], func=AF.Exp), ["lam2"], ["lam2"])
        P.op("dve", lambda e: e.tensor_tensor(out=neglam[:], in0=lam2[:, 1:2], in1=lam2[:, 0:1], op=ALU.subtract), ["lam2"], ["neglam"])
        P.op("dve", lambda e: e.tensor_scalar(out=neglam[:], in0=neglam[:], scalar1=-LAMBDA_INIT, scalar2=None, op0=ALU.add), ["neglam"], ["neglam"])

        shared = Alloc(8192, 8192 + 53248)
        Win = shared.take([128, 8, 1536], BF16)
        xTt = shared.take([128, 8, 512])
        nT = shared.take([128, 8, 512], BF16)
        lnv = shared.take([128, 512]); rstd = shared.take([128, 512])
        ARENA0 = shared.off
        Wo = Alloc(WO_OFF, WG_OFF).take([128, 16, 1024], BF16)
        Wg = Alloc(WG_OFF, STG_OFF).take([128, 8, DFF], BF16)
        stg = Alloc(STG_OFF, TAIL_OFF).take([128, 2, 1024])
        tail = Alloc(TAIL_OFF, MEM_END)
        finw_s = tail.take([128, D])
        actT = tail.take([128, 22, 128], BF16)
        sg = tail.take([128, 2, 128])

        xT_v = xT.rearrange("(kt p) t -> p kt t", p=128)
        w_in_v = w_in.rearrange("(kt p) c -> p kt c", p=128)

        def load_win(c0, ncols, tag):
            for kt in range(8):
                half = kt % 2
                st = xTt[:, 4 * half:4 * half + 4, :].rearrange("p a b -> p (a b)")[:, 0:ncols]
                P.dma("sp", st, w_in_v[:, kt, c0:c0 + ncols], writes=["xTt%d" % half], semkey="xTt%d" % half)
                P.op("act", (lambda st=st, kt=kt: lambda e: e.activation(out=Win[:, kt, 0:ncols], in_=st, func=AF.Copy, scale=mixw_s[:, kt:kt + 1]))(),
                     ["xTt%d" % half] + CK, ["Win"])

        def norm_part1(i):
            t0 = i * 512
            P.dma("sp", xTt[:, 0:4, :], xT_v[:, 0:4, t0:t0 + 512], writes=["xTt0"], semkey="xTt0")
            P.dma("sp", xTt[:, 4:8, :], xT_v[:, 4:8, t0:t0 + 512], writes=["xTt1"], semkey="xTt1")
            P.op("act", lambda e: e.activation(out=nT[:], in_=xTt[:], func=AF.Square), ["xTt0", "xTt1"], ["nT"])

        def norm_part2(i):
            for kt in range(8):
                P.op("pe", (lambda kt=kt: lambda e: e.matmul(psf[:, 6, :], lhsT=onesb[:], rhs=nT[:, kt, :], start=(kt == 0), stop=(kt == 7)))(),
                     ["nT", "onesb"], [], ps=[6])
            P.op("act", lambda e: e.activation(out=lnv[:], in_=psf[:, 6, :], func=AF.Ln, scale=1.0 / D, bias=EPS), [], ["lnv"], ps=[6])
            P.op("act", lambda e: e.activation(out=rstd[:], in_=lnv[:], func=AF.Exp, scale=-0.5), ["lnv"], ["rstd"])
            P.op("dve", lambda e: e.tensor_tensor(out=nT[:], in0=xTt[:], in1=rstd[:].unsqueeze(1).to_broadcast([128, 8, 512]), op=ALU.mult),
                 ["xTt0", "xTt1", "rstd", "nT"], ["nT"])

        def norm_tile(i):
            norm_part1(i)
            norm_part2(i)

        evac_rr = [0]

        def evac(out_ap, in_ap, reads, writes, ps, eng=None):
            if eng is None:
                eng = ("act", "dve")[evac_rr[0] % 2]
                evac_rr[0] += 1
            if eng == "act":
                P.op("act", lambda e: e.copy(out=out_ap, in_=in_ap), reads, writes, ps=ps)
            else:
                P.op(eng, lambda e: e.tensor_copy(out=out_ap, in_=in_ap), reads, writes, ps=ps)

        def proj_fm(bank, c0, reads_extra=()):
            for kt in range(8):
                P.op("pe", (lambda kt=kt: lambda e: e.matmul(psf[:, bank, :], lhsT=Win[:, kt, c0:c0 + 128], rhs=nT[:, kt, :], start=(kt == 0), stop=(kt == 7)))(),
                     ["Win", "nT"], [], ps=[bank])

        def proj_tm(bank, s, c0, n, col0=0):
            for kt in range(8):
                P.op("pe", (lambda kt=kt: lambda e: e.matmul(psf[:, bank, col0:col0 + n], lhsT=nT[:, kt, s * 128:(s + 1) * 128], rhs=Win[:, kt, c0:c0 + n], start=(kt == 0), stop=(kt == 7)))(),
                     ["Win", "nT"], [], ps=[bank])

        stg_n = [0]

        def wload(dst_fn, src_ap, ncols, scale_ap, dst_key, eng="act"):
            sidx = stg_n[0] % 2
            stg_n[0] += 1
            st = stg[:, sidx, 0:ncols]
            P.dma("sp", st, src_ap, writes=["stg%d" % sidx], semkey="stg%d" % sidx)
            if scale_ap is not None:
                P.op("act", lambda e: e.activation(out=dst_fn, in_=st, func=AF.Copy, scale=scale_ap), ["stg%d" % sidx] + CK, [dst_key])
            else:
                P.op(eng, lambda e: e.tensor_copy(out=dst_fn, in_=st), ["stg%d" % sidx], [dst_key])

        w_out_v = w_out.rearrange("(kt p) c -> p kt c", p=128)
        w_gate_v = w_gate.rearrange("(kt p) c -> p kt c", p=128)
        w_up_v = w_up.rearrange("(kt p) c -> p kt c", p=128)
        w_down_v = w_down.rearrange("(kt p) c -> p kt c", p=128)
        P.op("dve", lambda e: e.tensor_copy(out=wo_scale[:, 0:8], in_=wosc_s[:, 0:8]), CK, ["wo_scale"])
        P.op("dve", lambda e: e.tensor_scalar(out=wo_scale[:, 8:16], in0=wosc_s[:, 8:16], scalar1=1.0 - LAMBDA_INIT, scalar2=None, op0=ALU.mult), CK + ["wo_scale"], ["wo_scale"])

        pre_jobs = []
        for kt in range(16):
            pre_jobs.append((lambda kt=kt: wload(Wo[:, kt, :], w_out_v[:, kt, :], 1024, wo_scale[:, kt:kt + 1], "Wo")))
        for kt in range(8):
            for (cc0, ncl) in ((0, 1024), (1024, 1024), (2048, 768)):
                pre_jobs.append((lambda kt=kt, cc0=cc0, ncl=ncl: wload(Wg[:, kt, cc0:cc0 + ncl], w_gate_v[:, kt, cc0:cc0 + ncl], ncl, ffnw_s[:, kt:kt + 1], "Wg")))

        def gather(kind, i):
            src = (yda_in if kind == "da" else yssd_in)[i]
            dst = (yda_out if kind == "da" else yssd_out)[i]
            P.op("pool", lambda e: e.collective_compute("AllGather", ALU.bypass, replica_groups=[[0, 1], [2, 3], [4, 5], [6, 7]],
                                                        ins=[src.ap().opt()], outs=[dst.ap().opt()]),
                 ["y%s_in%d" % (kind, i)], ["y%s_out%d" % (kind, i)], semkey="cc_%s%d" % (kind, i), inc=1)

        cv = Alloc(ARENA0, 150000)
        kT = cv.take([128, 4, T], BF16)
        Vaug = cv.take([128, 32, 4, 132], BF16)
        qT = cv.take([128, 4, 512], BF16)
        PT = [[cv.take([128, 512], BF16) for _ in range(2)] for _ in range(2)]
        O_sb = cv.take([128, 8, 130], F32)
        rcp = cv.take([128, 8], F32); rn = cv.take([128, 4], F32)
        o1 = cv.take([128, 4, 128], F32); o2 = cv.take([128, 4, 128], F32)
        ssq4 = cv.take([128, 4], F32); r4 = cv.take([128, 4], F32)
        on = cv.take([128, 4, 128], BF16)
        yTda = cv.take([128, 4, 512], BF16)

        load_win(1288, 1536, "B")
        P.op("pool", lambda e: e.memset(Vaug[:, :, :, 128:129], 1.0), [], ["Vaug"])

        pend = []

        def make_fin(i, h):
            def fin1():
                P.op("dve", lambda e: e.reciprocal(out=rcp[:], in_=O_sb[:, :, 128]), ["O_sb"], ["rcp"])
                P.op("dve", lambda e: e.tensor_scalar(out=rn[:], in0=rcp[:, 4:8], scalar1=neglam[:, 0:1], scalar2=None, op0=ALU.mult), ["rcp", "neglam"], ["rn"])
                P.op("dve", lambda e: e.tensor_tensor(out=o1[:], in0=O_sb[:, 0:4, 0:128], in1=rcp[:, 0:4].unsqueeze(2).to_broadcast([128, 4, 128]), op=ALU.mult), ["O_sb", "rcp"], ["o1"])
                P.op("pool", lambda e: e.tensor_tensor(out=o2[:], in0=O_sb[:, 4:8, 0:128], in1=rn[:].unsqueeze(2).to_broadcast([128, 4, 128]), op=ALU.mult), ["O_sb", "rn"], ["o2"])
                P.op("dve", lambda e: e.tensor_tensor(out=o1[:], in0=o1[:], in1=o2[:], op=ALU.add), ["o1", "o2"], ["o1"])
                P.op("pool", lambda e: e.tensor_tensor(out=o2[:], in0=o1[:], in1=o1[:], op=ALU.mult), ["o1", "o2"], ["o2"])
                P.op("dve", lambda e: e.tensor_reduce(out=ssq4[:], in_=o2[:], axis=AX.X, op=ALU.add), ["o2"], ["ssq4"])

            def fin2():
                P.op("act", lambda e: e.activation(out=r4[:], in_=ssq4[:], func=AF.Ln, scale=1.0 / 128, bias=EPS), ["ssq4"], ["r4"])
                P.op("act", lambda e: e.activation(out=r4[:], in_=r4[:], func=AF.Exp, scale=-0.5), ["r4"], ["r4"])
                P.op("dve", lambda e: e.tensor_tensor(out=on[:], in0=o1[:], in1=r4[:].unsqueeze(2).to_broadcast([128, 4, 128]), op=ALU.mult), ["o1", "r4"], ["on"])

            def fin3():
                for qs in range(4):
                    P.op("pe", (lambda qs=qs: lambda e: e.transpose(psb[:, qs * 128:(qs + 1) * 128], on[:, qs, :], Ib[:]))(), ["on", "Ib"], [], ps=[7])
                P.op("dve", lambda e: e.tensor_copy(out=yTda[:, h, :], in_=psb[:, 0:512]), [], ["yTda"], ps=[7])
                if h == 3:
                    P.dma("sp", yda_in[i].ap().rearrange("(h p) t -> p h t", p=128), yTda[:], reads=["yTda"], writes=["yda_in%d" % i], semkey="yda_st")
                    gather("da", i)
            return [fin1, fin2, fin3]

        def run_pending(stage):
            if pend and len(pend[0]) == 3 - stage:
                pend[0].pop(0)()
                if not pend[0]:
                    pend.pop(0)

        if NT_B:
            norm_tile(0)
        for i in range(NT_B):
            t0 = i * 512
            for h in range(4):
                bank = h % 4
                proj_fm(bank, h * 128)
                evac(qT[:, h, :], psf[:, bank, :], [], ["qT"], [bank], eng="dve")
            for h in range(4):
                bank = h % 4
                proj_fm(bank, 512 + h * 128)
                evac(kT[:, h, t0:t0 + 512], psf[:, bank, :], [], ["kT"], [bank], eng="dve")
            for s in range(4):
                bank = s % 4
                proj_tm(bank, s, 1024, 512)
                evac(Vaug[:, 4 * i + s, :, 0:128], psf[:, bank, :].rearrange("p (h v) -> p h v", h=4), [], ["Vaug"], [bank], eng="dve")
            nkb = 4 * i + 4
            for h in range(4):
                started = set()
                steps = list(range(nkb))

                def emit_S(st, kb, h=h, i=i):
                    qlo = max(0, kb - 4 * i) * 128
                    n = 512 - qlo
                    for m in range(2):
                        bank = 2 * (st % 2) + m
                        P.op("pe", (lambda m=m, bank=bank, kb=kb, qlo=qlo, n=n: lambda e: e.matmul(
                            psf[:, bank, 0:n], lhsT=kT[64 * m:64 * m + 64, h, kb * 128:(kb + 1) * 128],
                            rhs=qT[64 * m:64 * m + 64, h, qlo:512], start=True, stop=True))(),
                            ["kT", "qT"], [], ps=[bank])

                def emit_rest(st, kb, h=h, i=i, started=started):
                    qlo = max(0, kb - 4 * i) * 128
                    n = 512 - qlo
                    b = st % 2
                    for m in range(2):
                        bank = 2 * b + m
                        P.op("act", (lambda m=m, bank=bank, n=n, b=b: lambda e: e.activation(out=PT[m][b][:, 0:n], in_=psf[:, bank, 0:n], func=AF.Exp, scale=0.125))(),
                             [], ["PT%d%d" % (m, b)], ps=[bank])
                        if kb >= 4 * i:
                            P.op("pool", (lambda m=m, b=b: lambda e: e.tensor_tensor(out=PT[m][b][:, 0:128], in0=PT[m][b][:, 0:128], in1=Ub[:], op=ALU.mult))(),
                                 ["PT%d%d" % (m, b), "Ub"], ["PT%d%d" % (m, b)])
                    for m in range(2):
                        for qs in range(max(0, kb - 4 * i), 4):
                            a = m * 4 + qs
                            bank = 4 + a // 3
                            col = (a % 3) * 130
                            first = (kb == 0) and (bank not in started)
                            if kb == 0:
                                started.add(bank)
                            last = (kb == 4 * i + qs)
                            P.op("pe", (lambda m=m, b=b, qs=qs, qlo=qlo, bank=bank, col=col, first=first, last=last, kb=kb: lambda e: e.matmul(
                                psf[:, bank, col:col + 129], lhsT=PT[m][b][:, qs * 128 - qlo:qs * 128 - qlo + 128],
                                rhs=Vaug[:, kb, h, 0:129], start=first, stop=last, skip_group_check=True))(),
                                ["PT%d%d" % (m, b), "Vaug"], [], ps=[bank])

                emit_S(0, 0)
                for st, kb in enumerate(steps):
                    if st + 1 < nkb:
                        emit_S(st + 1, steps[st + 1])
                    emit_rest(st, kb)
                    if st == 0:
                        run_pending(0)
                    elif st == 1:
                        run_pending(1)
                    elif st == 3:
                        run_pending(2)
                    if h == 1 and st == 2 and i + 1 < NT_B:
                        norm_part1(i + 1)
                while pend:
                    run_pending(3 - len(pend[0]))
                for bank, na in ((4, 3), (5, 3), (6, 2)):
                    a0 = (bank - 4) * 3
                    P.op("act", (lambda bank=bank, na=na, a0=a0: lambda e: e.copy(
                        out=O_sb[:, a0:a0 + na, :], in_=psf[:, bank, 0:na * 130].rearrange("p (a c) -> p a c", a=na)))(),
                        [], ["O_sb"], ps=[bank])
                pend.append(make_fin(i, h))
                if h == 2 and i + 1 < NT_B:
                    norm_part2(i + 1)
        while pend:
            run_pending(3 - len(pend[0]))

        cv = Alloc(ARENA0, WO_OFF)
        ta = Alloc(TAIL_OFF, MEM_END)
        ub = [cv.take([128, 516], F32) for _ in range(2)]
        cacc = [cv.take([128, 512], F32) for _ in range(2)]
        carry = cv.take([128, 6, 3], F32)
        xcs = [cv.take([128, 6, 512], BF16), ta.take([128, 6, 512], BF16)]
        zs = cv.take([128, 4, 512], F32)

        def dtset(al):
            d = {}
            for nm in ("dtb", "dtv", "adt", "acs", "eacs", "dd", "decst", "cdec", "dtdec", "adthf"):
                d[nm] = al.take([128, 32], F32)
            d["adth"] = al.take([128, 32], BF16)
            d["adtl"] = al.take([128, 32], BF16)
            return d
        dts = [dtset(cv), dtset(ta)]
        xtm = ta.take([128, 640], BF16)
        xdt = ta.take([128, 8, 64], BF16)
        xdd = ta.take([128, 8, 64], BF16)
        rhs_hi = cv.take([128, 8, 128], BF16)
        rhs_lo = cv.take([128, 8, 128], BF16)
        expD = cv.take([128, 8, 128], F32)
        CBm = cv.take([128, 128], F32)
        MT = cv.take([128, 8, 128], BF16)
        t1s = [cv.take([128, 8, 64], F32) for _ in range(2)]
        t2 = cv.take([128, 8, 64], F32)
        Sst = cv.take([128, 8, 64], F32); Sbf = cv.take([128, 512], BF16)
        ssq1 = cv.take([128, 2], F32)
        ygn = cv.take([128, 512], BF16)
        yTs = cv.take([128, 4, 512], BF16)
        Gb = cv.take([128, 128], BF16)
        PB_KEYS = ["kT", "Vaug", "qT", "PT00", "PT01", "PT10", "PT11", "O_sb", "rcp", "rn", "o1", "o2", "ssq4", "r4", "on", "yTda"]
        P.op("pool", lambda e: e.memset(carry[:], 0.0), PB_KEYS, PB_KEYS + ["carry", "Wo", "Wg", "stg0", "stg1"])
        P.op("pool", lambda e: e.memset(Sst[:], 0.0), ["carry"], ["Sst"])
        P.op("pool", lambda e: e.memset(Sbf[:], 0.0), ["carry"], ["Sbf"])
        P.op("dve", lambda e: e.tensor_copy(out=Gb[:], in_=Gf[:]), ["carry"] + CK, ["Gb"])
        FENCE = ["carry"]

        P.op("pool", lambda e: e.memset(Win[:, :, 1280:1296], 0.0), ["Win"], ["Win"])
        load_win(0, 1288, "A")
        ZC, XC, DTC = 0, 512, 1280
        NTA = NT_A if RUN_A else 0

        def prologue_main(i):
            par = i % 2
            xc = xcs[par]
            d = dts[par]
            sx = "%d" % par
            for _ in range(6):
                if pre_jobs:
                    pre_jobs.pop(0)()
            for c in range(6):
                bank = c % 2
                u = ub[c % 2]
                acc = cacc[c % 2]
                ceng = "dve" if c % 2 == 0 else "pool"
                proj_fm(bank, XC + c * 128)
                P.op("pool", (lambda u=u, c=c: lambda e: e.tensor_copy(out=u[:, 0:3], in_=carry[:, c, :]))(), ["carry"] + FENCE, ["ub%d" % (c % 2)])
                P.op("act", (lambda u=u, bank=bank: lambda e: e.copy(out=u[:, 3:515], in_=psf[:, bank, :]))(), [], ["ub%d" % (c % 2)], ps=[bank])
                P.op("pool", (lambda u=u, c=c: lambda e: e.tensor_copy(out=carry[:, c, :], in_=u[:, 512:515]))(), ["ub%d" % (c % 2)], ["carry"])
                if ceng == "dve":
                    P.op(ceng, (lambda u=u, acc=acc, c=c: lambda e: e.tensor_scalar(out=acc[:], in0=u[:, 3:515], scalar1=convw_s[:, c, 3:4], scalar2=convb_s[:, c:c + 1], op0=ALU.mult, op1=ALU.add))(),
                         ["ub%d" % (c % 2)] + CK, ["cacc%d" % (c % 2)])
                else:
                    P.op(ceng, (lambda u=u, acc=acc, c=c: lambda e: e.tensor_tensor(out=acc[:], in0=u[:, 3:515], in1=convw_s[:, c, 3:4].to_broadcast([128, 512]), op=ALU.mult))(),
                         ["ub%d" % (c % 2)] + CK, ["cacc%d" % (c % 2)])
                    P.op(ceng, (lambda acc=acc, c=c: lambda e: e.tensor_tensor(out=acc[:], in0=acc[:], in1=convb_s[:, c:c + 1].to_broadcast([128, 512]), op=ALU.add))(),
                         ["cacc%d" % (c % 2)] + CK, ["cacc%d" % (c % 2)])
                for jt in (2, 1, 0):
                    if ceng == "dve":
                        P.op(ceng, (lambda u=u, acc=acc, c=c, jt=jt: lambda e: e.scalar_tensor_tensor(out=acc[:], in0=u[:, jt:jt + 512], scalar=convw_s[:, c, jt:jt + 1], in1=acc[:], op0=ALU.mult, op1=ALU.add))(),
                             ["ub%d" % (c % 2), "cacc%d" % (c % 2)], ["cacc%d" % (c % 2)])
                    else:
                        t2f = t2[:].rearrange("p h d -> p (h d)")
                        P.op(ceng, (lambda u=u, c=c, jt=jt, t2f=t2f: lambda e: e.tensor_tensor(out=t2f, in0=u[:, jt:jt + 512], in1=convw_s[:, c, jt:jt + 1].to_broadcast([128, 512]), op=ALU.mult))(),
                             ["ub%d" % (c % 2), "t2"], ["t2"])
                        P.op(ceng, (lambda acc=acc, t2f=t2f: lambda e: e.tensor_tensor(out=acc[:], in0=acc[:], in1=t2f, op=ALU.add))(),
                             ["t2", "cacc%d" % (c % 2)], ["cacc%d" % (c % 2)])
                P.op("act", (lambda acc=acc, c=c, xc=xc: lambda e: e.activation(out=xc[:, c, :], in_=acc[:], func=AF.Silu))(), ["cacc%d" % (c % 2)] + FENCE, ["xc" + sx])
            for s_ in range(4):
                proj_tm(6, s_, DTC, 8, col0=s_ * 8)
            v4 = lambda t_: t_[:].rearrange("p (c h) -> p c h", c=4)
            P.op("dve", lambda e: e.tensor_tensor(out=v4(d["dtb"]), in0=psf[:, 6, 0:32].rearrange("p (c h) -> p c h", c=4),
                                                  in1=ssdp_s[:, 0, :].unsqueeze(1).to_broadcast([128, 4, 8]), op=ALU.add), CK + FENCE, ["dtb" + sx], ps=[6])
            P.op("act", lambda e: e.activation(out=d["dtb"][:], in_=d["dtb"][:], func=AF.Exp), ["dtb" + sx], ["dtb" + sx])
            P.op("act", lambda e: e.activation(out=d["dtv"][:], in_=d["dtb"][:], func=AF.Ln, bias=1.0), ["dtb" + sx], ["dtv" + sx])
            P.op("dve", lambda e: e.tensor_tensor(out=v4(d["adt"]), in0=v4(d["dtv"]), in1=Abc[:].unsqueeze(1).to_broadcast([128, 4, 8]), op=ALU.mult), ["dtv" + sx, "Abc"], ["adt" + sx])
            P.op("pe", lambda e: e.matmul(psf[:, 6, 32:64], lhsT=Uf[:], rhs=d["adt"][:], start=True, stop=True), ["adt" + sx] + CK, [], ps=[6])
            P.op("pe", lambda e: e.matmul(psf[:, 6, 64:96], lhsT=onesf[:], rhs=d["adt"][:], start=False, stop=True, skip_group_check=True), ["adt" + sx, "onesf"], [], ps=[6])
            P.op("act", lambda e: e.copy(out=d["acs"][:], in_=psf[:, 6, 32:64]), [], ["acs" + sx], ps=[6])
            P.op("act", lambda e: e.activation(out=d["eacs"][:], in_=psf[:, 6, 32:64], func=AF.Exp), [], ["eacs" + sx], ps=[6])
            P.op("act", lambda e: e.activation(out=d["cdec"][:], in_=psf[:, 6, 64:96], func=AF.Exp), [], ["cdec" + sx], ps=[6])
            P.op("dve", lambda e: e.tensor_tensor(out=d["dd"][:], in0=psf[:, 6, 64:96], in1=d["acs"][:], op=ALU.subtract), ["acs" + sx], ["dd" + sx], ps=[6])
            P.op("act", lambda e: e.activation(out=d["decst"][:], in_=d["dd"][:], func=AF.Exp), ["dd" + sx], ["decst" + sx])
            P.op("dve", lambda e: e.tensor_tensor(out=d["dtdec"][:], in0=d["dtv"][:], in1=d["decst"][:], op=ALU.mult), ["dtv" + sx, "decst" + sx], ["dtdec" + sx])
            P.op("dve", lambda e: e.tensor_copy(out=d["adth"][:], in_=d["adt"][:]), ["adt" + sx], ["adth" + sx])
            P.op("dve", lambda e: e.tensor_copy(out=d["adthf"][:], in_=d["adth"][:]), ["adth" + sx], ["adthf" + sx])
            P.op("dve", lambda e: e.tensor_tensor(out=d["adtl"][:], in0=d["adt"][:], in1=d["adthf"][:], op=ALU.subtract), ["adt" + sx, "adthf" + sx], ["adtl" + sx])

        def zproj(i):
            for s_ in range(4):
                bank = s_ % 2
                proj_tm(bank, s_, ZC, 512)
                P.op("act", (lambda s_=s_, bank=bank: lambda e: e.activation(out=zs[:, s_, :], in_=psf[:, bank, :], func=AF.Silu))(), FENCE, ["zs%d" % s_], ps=[bank])

        def stageF(i, c):
            par = i % 2
            xc = xcs[par]
            d = dts[par]
            sx = "%d" % par
            t1 = t1s[c % 2]
            t1k = "t1%d" % (c % 2)
            cs = slice(c * 128, (c + 1) * 128)
            hs = slice(c * 8, (c + 1) * 8)
            for ct in range(5):
                P.op("pe", (lambda ct=ct: lambda e: e.transpose(psb[:, ct * 128:(ct + 1) * 128], xc[:, ct, cs], Ib[:]))(), ["xc" + sx, "Ib"], [], ps=[7])
            P.op("act", lambda e: e.copy(out=xtm[:], in_=psb[:, 0:640]), [], ["xtm"], ps=[7])
            P.op("dve", lambda e: e.tensor_tensor(out=rhs_hi[:], in0=Ub[:].unsqueeze(1).to_broadcast([128, 8, 128]),
                                                  in1=d["adth"][:, hs].unsqueeze(2).to_broadcast([128, 8, 128]), op=ALU.mult), ["adth" + sx, "Ub"], ["rhs_hi"])
            P.op("pool", lambda e: e.tensor_tensor(out=rhs_lo[:], in0=Ub[:].unsqueeze(1).to_broadcast([128, 8, 128]),
                                                   in1=d["adtl"][:, hs].unsqueeze(2).to_broadcast([128, 8, 128]), op=ALU.mult), ["adtl" + sx, "Ub"], ["rhs_lo"])
            for hh in range(2):
                P.op("pe", (lambda hh=hh: lambda e: e.matmul(psf[:, 2 + hh, :], lhsT=Gb[:], rhs=rhs_hi[:, 4 * hh:4 * hh + 4, :], start=True, stop=False))(), ["rhs_hi", "Gb"], [], ps=[2 + hh])
                P.op("pe", (lambda hh=hh: lambda e: e.matmul(psf[:, 2 + hh, :], lhsT=Gb[:], rhs=rhs_lo[:, 4 * hh:4 * hh + 4, :], start=False, stop=True))(), ["rhs_lo", "Gb"], [], ps=[2 + hh])
            P.op("pe", lambda e: e.matmul(psf[:, 4, 0:128], lhsT=xc[:, 4, cs], rhs=xc[:, 5, cs], start=True, stop=True), ["xc" + sx], [], ps=[4])
            P.op("dve", lambda e: e.tensor_tensor(out=CBm[:], in0=psf[:, 4, 0:128], in1=Uf[:], op=ALU.mult), CK, ["CBm"], ps=[4])
            for hh in range(2):
                P.op("act", (lambda hh=hh: lambda e: e.activation(out=expD[:, 4 * hh:4 * hh + 4, :], in_=psf[:, 2 + hh, :].rearrange("p (h l) -> p h l", h=4), func=AF.Exp))(),
                     [], ["expD"], ps=[2 + hh])
            P.op("dve", lambda e: e.tensor_tensor(out=MT[:], in0=expD[:], in1=CBm[:].unsqueeze(1).to_broadcast([128, 8, 128]), op=ALU.mult), ["expD", "CBm"], ["MT"])
            xv = xtm[:, 0:512].rearrange("p (h d) -> p h d", h=8)
            P.op("pool", lambda e: e.tensor_tensor(out=xdt[:], in0=xv, in1=d["dtv"][:, hs].unsqueeze(2).to_broadcast([128, 8, 64]), op=ALU.mult), ["xtm", "dtv" + sx], ["xdt"])
            P.op("pool", lambda e: e.tensor_tensor(out=xdd[:], in0=xv, in1=d["dtdec"][:, hs].unsqueeze(2).to_broadcast([128, 8, 64]), op=ALU.mult), ["xtm", "dtdec" + sx], ["xdd"])
            P.op("pool", lambda e: e.tensor_tensor(out=t2[:], in0=xv, in1=Dsk[:], op=ALU.mult), ["xtm", "Dsk"], ["t2"])
            for h in range(8):
                P.op("pe", (lambda h=h: lambda e: e.matmul(psf[:, 5, h * 64:(h + 1) * 64], lhsT=MT[:, h, :], rhs=xdt[:, h, :], start=(h == 0), stop=True, skip_group_check=True))(),
                     ["MT", "xdt"], [], ps=[5])
            P.op("pe", lambda e: e.matmul(psf[:, 0, :], lhsT=xc[:, 5, cs], rhs=Sbf[:], start=True, stop=True), ["xc" + sx, "Sbf"], [], ps=[0])
            P.op("pe", lambda e: e.matmul(psf[:, 1, :], lhsT=xtm[:, 512:640], rhs=xdd[:].rearrange("p h d -> p (h d)"), start=True, stop=True), ["xtm", "xdd"], [], ps=[1])
            P.op("dve", lambda e: e.tensor_tensor(out=Sst[:], in0=Sst[:], in1=d["cdec"][:, hs].unsqueeze(2).to_broadcast([128, 8, 64]), op=ALU.mult), ["Sst", "cdec" + sx], ["Sst"])
            P.op("dve", lambda e: e.tensor_tensor(out=Sst[:], in0=Sst[:], in1=psf[:, 1, :].rearrange("p (h d) -> p h d", h=8), op=ALU.add), ["Sst"], ["Sst"], ps=[1])
            P.op("act", lambda e: e.copy(out=Sbf[:], in_=Sst[:].rearrange("p h d -> p (h d)")), ["Sst"], ["Sbf"])
            P.op("dve", lambda e: e.tensor_tensor(out=t1[:], in0=psf[:, 0, :].rearrange("p (h d) -> p h d", h=8),
                                                  in1=d["eacs"][:, hs].unsqueeze(2).to_broadcast([128, 8, 64]), op=ALU.mult), ["eacs" + sx], [t1k], ps=[0])
            P.op("dve", lambda e: e.tensor_tensor(out=t1[:], in0=t1[:], in1=psf[:, 5, :].rearrange("p (h d) -> p h d", h=8), op=ALU.add), [t1k], [t1k], ps=[5])
            P.op("dve", lambda e: e.tensor_tensor(out=t1[:], in0=t1[:], in1=t2[:], op=ALU.add), [t1k, "t2"], [t1k])

        def stageK(i, c):
            t1 = t1s[c % 2]
            t1k = "t1%d" % (c % 2)
            cs = slice(c * 128, (c + 1) * 128)
            P.op("dve", lambda e: e.tensor_tensor(out=t1[:], in0=t1[:], in1=zs[:, c, :].rearrange("p (h d) -> p h d", h=8), op=ALU.mult), [t1k, "zs%d" % c], [t1k])
            P.op("act", lambda e: e.activation(out=ygn[:], in_=t1[:].rearrange("p h d -> p (h d)"), func=AF.Square, accum_out=ssq1[:, 0:1]), [t1k, "ygn"], ["ygn", "ssq1"])
            P.op("act", lambda e: e.activation(out=ssq1[:, 1:2], in_=ssq1[:, 0:1], func=AF.Ln, scale=1.0 / 512, bias=EPS), ["ssq1"], ["ssq1"])
            P.op("act", lambda e: e.activation(out=ssq1[:, 1:2], in_=ssq1[:, 1:2], func=AF.Exp, scale=-0.5), ["ssq1"], ["ssq1"])
            P.op("dve", lambda e: e.tensor_scalar(out=ygn[:], in0=t1[:].rearrange("p h d -> p (h d)"), scalar1=ssq1[:, 1:2], scalar2=None, op0=ALU.mult), [t1k, "ssq1", "ygn"], ["ygn"])
            for ft in range(4):
                P.op("pe", (lambda ft=ft: lambda e: e.transpose(psb[:, 512 + ft * 128:512 + (ft + 1) * 128], ygn[:, ft * 128:(ft + 1) * 128], Ib[:]))(), ["ygn", "Ib"], [], ps=[7])
            P.op("act", lambda e: e.copy(out=yTs[:, :, cs], in_=psb[:, 512:1024].rearrange("p (f t) -> p f t", f=4)), [], ["yTs"], ps=[7])
            if c == 3:
                P.dma("sp", yssd_in[i].ap().rearrange("(f p) t -> p f t", p=128), yTs[:], reads=["yTs"], writes=["yssd_in%d" % i], semkey="yssd_st")
                gather("ssd", i)

        if NTA:
            norm_tile(0)
            prologue_main(0)
            zproj(0)
        for i in range(NTA):
            nxt = i + 1 < NTA
            stageF(i, 0)
            stageF(i, 1)
            stageK(i, 0)
            if nxt:
                norm_tile(i + 1)
            stageF(i, 2)
            stageK(i, 1)
            stageF(i, 3)
            stageK(i, 2)
            if nxt:
                prologue_main(i + 1)
            stageK(i, 3)
            if nxt:
                zproj(i + 1)

        while pre_jobs and RUN_P2:
            pre_jobs.pop(0)()

        if DEBUG:
            for i in range(NT_B):
                P.dma("sp", dbg_da[:, i * 512:(i + 1) * 512], yda_in[i].ap(), reads=["yda_in%d" % i], semkey="dbg1")
            for i in range(NT_A if RUN_A else 0):
                P.dma("sp", dbg_ssd[:, i * 512:(i + 1) * 512], yssd_in[i].ap(), reads=["yssd_in%d" % i], semkey="dbg2")

        PA_KEYS = (["ub0", "ub1", "cacc0", "cacc1", "carry", "xc0", "xc1", "zs0", "zs1", "zs2", "zs3", "xtm", "rhs_hi", "rhs_lo", "expD", "CBm", "MT",
                    "xdt", "xdd", "t10", "t11", "t2", "Sst", "Sbf", "ssq1", "ygn", "yTs", "Gb"]
                   + [nm + sx for nm in ("dtb", "dtv", "adt", "acs", "eacs", "dd", "decst", "cdec", "dtdec", "adthf", "adth", "adtl") for sx in ("0", "1")])
        cv = Alloc(1024, WO_OFF)
        Wu = cv.take([128, 8, DFF], BF16)
        Wd = cv.take([128, 22, 1024], BF16)
        c0b = cv.take([128, 16, 128], BF16)
        c1h = cv.take([128, 8, 128], BF16)
        h1 = [cv.take([128, 1024], F32) for _ in range(2)]
        n2 = cv.take([128, 1024], BF16)
        n2T = cv.take([128, 8, 128], BF16)
        P.op("pool", lambda e: e.memset(st2[:], 0.0), PA_KEYS + CK + ["Win", "nT", "xTt0", "xTt1", "lnv", "rstd", "Ub", "onesb", "onesf", "Abc", "Dsk", "lamt", "lam2"],
             PA_KEYS + ["Wd", "Wu", "p2fence", "st2", "finw", "actT", "sg0", "sg1"])
        F2 = ["p2fence"]
        P.dma("sp", finw_s[:], finw, reads=F2, writes=["finw"], semkey="finw")
        for kt in range(8 if RUN_P2 else 0):
            for (cc0, ncl) in ((0, 1024), (1024, 1024), (2048, 768)):
                (lambda kt=kt, cc0=cc0, ncl=ncl: wload(Wu[:, kt, cc0:cc0 + ncl], w_up_v[:, kt, cc0:cc0 + ncl], ncl, ffnw_s[:, kt:kt + 1], "Wu"))()
        for kt in range(22 if RUN_P2 else 0):
            (lambda kt=kt: wload(Wd[:, kt, :], w_down_v[:, kt, :], 1024, None, "Wd", eng="pool"))()

        ys_v = [t_.ap().rearrange("(kt p) t -> p kt t", p=128) for t_ in yssd_out]
        yd_v = [t_.ap().rearrange("(kt p) t -> p kt t", p=128) for t_ in yda_out]
        for tt in range(16 if RUN_P2 else 0):
            T0 = tt * 128
            hb = h1[tt % 2]
            hk = "h1%d" % (tt % 2)
            P.dma("sp", hb[:], xtok[T0:T0 + 128, :], reads=F2, writes=[hk], semkey=hk)
            ti, tc = T0 // 512, T0 % 512
            for half, (src, srckey) in enumerate(((ys_v, "yssd_out"), (yd_v, "yda_out"))):
                ck = "c0b%d" % half
                P.dma("sp", c0b[:, 8 * half:8 * half + 8, :], src[ti][:, :, tc:tc + 128], reads=[srckey + str(ti)] + F2, writes=[ck], semkey=ck)
                P.dma("sp", c1h[:], src[4 + ti][:, :, tc:tc + 128], reads=[srckey + str(4 + ti)] + F2, writes=["c1h"], semkey="c1h")
                P.op("dve", lambda e: e.tensor_scalar(out=c1h[:], in0=c1h[:], scalar1=flags_s[:, 1:2], scalar2=None, op0=ALU.mult), ["c1h"], ["c1h"])
                P.op("dve", (lambda half=half: lambda e: e.scalar_tensor_tensor(out=c0b[:, 8 * half:8 * half + 8, :], in0=c0b[:, 8 * half:8 * half + 8, :], scalar=flags_s[:, 0:1], in1=c1h[:], op0=ALU.mult, op1=ALU.add))(),
                     [ck, "c1h"], [ck])
            for dh in range(2):
                bank = dh
                for kt in range(16):
                    P.op("pe", (lambda kt=kt, dh=dh, bank=bank: lambda e: e.matmul(psf[:, bank, :], lhsT=c0b[:, kt, :], rhs=Wo[:, kt, dh * 512:(dh + 1) * 512], start=(kt == 0), stop=(kt == 15)))(),
                         ["c0b0", "c0b1", "Wo"], [], ps=[bank])
                P.op("dve", (lambda dh=dh, bank=bank, hb=hb: lambda e: e.tensor_tensor(out=hb[:, dh * 512:(dh + 1) * 512], in0=psf[:, bank, :], in1=hb[:, dh * 512:(dh + 1) * 512], op=ALU.add))(),
                     [hk], [hk], ps=[bank])
            P.op("act", (lambda hb=hb: lambda e: e.activation(out=n2[:], in_=hb[:], func=AF.Square, accum_out=st2[:, 0:1]))(), [hk, "n2"], ["n2", "st2"])
            P.op("act", lambda e: e.activation(out=st2[:, 1:2], in_=st2[:, 0:1], func=AF.Ln, scale=1.0 / D, bias=EPS), ["st2"], ["st2"])
            P.op("act", lambda e: e.activation(out=st2[:, 1:2], in_=st2[:, 1:2], func=AF.Exp, scale=-0.5), ["st2"], ["st2"])
            P.op("dve", (lambda hb=hb: lambda e: e.tensor_scalar(out=n2[:], in0=hb[:], scalar1=st2[:, 1:2], scalar2=None, op0=ALU.mult))(), [hk, "st2", "n2"], ["n2"])
            for kt in range(8):
                P.op("pe", (lambda kt=kt: lambda e: e.transpose(psb[:, kt * 128:(kt + 1) * 128], n2[:, kt * 128:(kt + 1) * 128], Ib[:]))(), ["n2", "Ib"], [], ps=[7])
            P.op("act", lambda e: e.copy(out=n2T[:], in_=psb[:].rearrange("p (k t) -> p k t", k=8)), F2, ["n2T"], ps=[7])
            for f in range(22):
                gb = 2 + (f % 2)
                ubk = 4 + (f % 2)
                for kt in range(8):
                    P.op("pe", (lambda kt=kt, f=f, gb=gb: lambda e: e.matmul(psf[:, gb, 0:128], lhsT=Wg[:, kt, f * 128:(f + 1) * 128], rhs=n2T[:, kt, :], start=(kt == 0), stop=(kt == 7)))(),
                         ["Wg", "n2T"], [], ps=[gb])
                for kt in range(8):
                    P.op("pe", (lambda kt=kt, f=f, ubk=ubk: lambda e: e.matmul(psf[:, ubk, 0:128], lhsT=Wu[:, kt, f * 128:(f + 1) * 128], rhs=n2T[:, kt, :], start=(kt == 0), stop=(kt == 7)))(),
                         ["Wu", "n2T"], [], ps=[ubk])
                P.op("act", (lambda f=f, gb=gb: lambda e: e.activation(out=sg[:, f % 2, :], in_=psf[:, gb, 0:128], func=AF.Silu))(), [], ["sg%d" % (f % 2)], ps=[gb])
                P.op("dve", (lambda f=f, ubk=ubk: lambda e: e.tensor_tensor(out=actT[:, f, :], in0=sg[:, f % 2, :], in1=psf[:, ubk, 0:128], op=ALU.mult))(), ["sg%d" % (f % 2)], ["actT"], ps=[ubk])
            for dh in range(2):
                bank = dh
                for f in range(22):
                    P.op("pe", (lambda f=f, dh=dh, bank=bank: lambda e: e.matmul(psf[:, bank, :], lhsT=actT[:, f, :], rhs=Wd[:, f, dh * 512:(dh + 1) * 512], start=(f == 0), stop=(f == 21)))(),
                         ["actT", "Wd"], [], ps=[bank])
                P.op("dve", (lambda dh=dh, bank=bank, hb=hb: lambda e: e.tensor_tensor(out=hb[:, dh * 512:(dh + 1) * 512], in0=psf[:, bank, :], in1=hb[:, dh * 512:(dh + 1) * 512], op=ALU.add))(),
                     [hk], [hk], ps=[bank])
            P.op("act", (lambda hb=hb: lambda e: e.activation(out=n2[:], in_=hb[:], func=AF.Square, accum_out=st2[:, 2:3]))(), [hk, "n2"], ["n2", "st2"])
            P.op("act", lambda e: e.activation(out=st2[:, 3:4], in_=st2[:, 2:3], func=AF.Ln, scale=1.0 / D, bias=EPS), ["st2"], ["st2"])
            P.op("act", lambda e: e.activation(out=st2[:, 3:4], in_=st2[:, 3:4], func=AF.Exp, scale=-0.5), ["st2"], ["st2"])
            P.op("dve", (lambda hb=hb: lambda e: e.scalar_tensor_tensor(out=hb[:], in0=hb[:], scalar=st2[:, 3:4], in1=finw_s[:], op0=ALU.mult, op1=ALU.mult))(), [hk, "st2", "finw"], [hk])
            P.dma("sp", out[T0:T0 + 128, :], hb[:], reads=[hk], writes=["out"], semkey="ost%d" % (tt % 2))

        P.emit()
    return nc


_NC = None


def kernel(x, mix_norm_w, w_in, conv_w, conv_b, dt_bias, a_log, d_skip, ssd_norm_w,
           lam_q1, lam_k1, lam_q2, lam_k2, subln_w, w_out, ffn_norm_w, w_gate, w_up, w_down,
           final_norm_w):
    global _NC
    f32 = np.float32
    x = np.asarray(x, f32)
    w_in0 = np.asarray(w_in, f32)[0]
    conv_w0 = np.asarray(conv_w, f32)[0]
    conv_b0 = np.asarray(conv_b, f32)[0]

    def pk(v):
        v = np.asarray(v, f32).reshape(-1, 128)
        return np.ascontiguousarray(v.T)

    def rep(v):
        v = np.asarray(v, f32).reshape(1, -1)
        return np.ascontiguousarray(np.broadcast_to(v, (128, v.shape[1])))

    ar = np.arange(128)
    cU = (ar[:, None] <= ar[None, :]).astype(f32)
    cG = (ar[:, None] > ar[None, :]).astype(f32)
    cI = np.eye(128, dtype=f32)
    shared = dict(
        mixw=pk(np.asarray(mix_norm_w, f32)[0]),
        lamv=np.ascontiguousarray(np.stack([rep(np.asarray(v, f32)[0]) for v in (lam_q1, lam_k1, lam_q2, lam_k2)], axis=1)),
        w_out=np.ascontiguousarray(np.asarray(w_out, f32)[0]),
        wosc=pk(np.concatenate([np.asarray(ssd_norm_w, f32)[0]] + [np.asarray(subln_w, f32)[0]] * 8)),
        ffnw=pk(np.asarray(ffn_norm_w, f32)[0]),
        w_gate=np.ascontiguousarray(np.asarray(w_gate, f32)[0]),
        w_up=np.ascontiguousarray(np.asarray(w_up, f32)[0]),
        w_down=np.ascontiguousarray(np.asarray(w_down, f32)[0]),
        finw=rep(np.asarray(final_norm_w, f32)),
        cU=cU, cG=cG, cI=cI,
    )
    percore_j = []
    for j in range(2):
        cols = np.concatenate([
            np.arange(512 * j, 512 * j + 512),
            np.arange(1024 + 512 * j, 1024 + 512 * j + 512),
            np.arange(2048 + 128 * j, 2048 + 128 * j + 128),
            np.arange(2304 + 128 * j, 2304 + 128 * j + 128),
            np.arange(2560 + 8 * j, 2560 + 8 * j + 8),
            np.arange(2576 + 512 * j, 2576 + 512 * j + 512),
            np.arange(3600 + 512 * j, 3600 + 512 * j + 512),
            np.arange(4624 + 512 * j, 4624 + 512 * j + 512),
        ])
        ch = np.concatenate([np.arange(512 * j, 512 * j + 512), np.arange(1024 + 128 * j, 1024 + 128 * j + 128),
                             np.arange(1280 + 128 * j, 1280 + 128 * j + 128)])
        cw = conv_w0[:, ch]
        convw = np.ascontiguousarray(cw.reshape(4, 6, 128).transpose(2, 1, 0))
        convb = np.ascontiguousarray(conv_b0[ch].reshape(6, 128).T)
        hsl = slice(8 * j, 8 * j + 8)
        ssdp = np.ascontiguousarray(np.stack([rep(np.asarray(dt_bias, f32)[0][hsl]), rep(np.asarray(a_log, f32)[0][hsl]),
                                              rep(np.asarray(d_skip, f32)[0][hsl])], axis=1))
        fl = np.zeros((128, 2), f32)
        fl[:, j] = 1.0
        percore_j.append(dict(w_in=np.ascontiguousarray(w_in0[:, cols]), convw=convw, convb=convb, ssdp=ssdp, flags=fl))

    in_maps = []
    for c in range(8):
        b, j = c // 2, c % 2
        m = dict(shared)
        m.update(percore_j[j])
        m["xT"] = np.ascontiguousarray(x[b].T)
        m["xtok"] = np.ascontiguousarray(x[b, j * 2048:(j + 1) * 2048, :])
        in_maps.append(m)

    if _NC is None:
        _NC = build_program()
    res = run_bass_kernel_spmd(_NC, in_maps, core_ids=list(range(8)))
    outp = np.empty((4, T, D), f32)
    for c in range(8):
        b, j = c // 2, c % 2
        outp[b, j * 2048:(j + 1) * 2048, :] = res.results[c]["out"]
    if DEBUG:
        kernel.dbg = [dict(da=np.asarray(r["dbg_da"]), ssd=np.asarray(r["dbg_ssd"])) for r in res.results]
    return outp
```

```python
import bisect
from contextlib import ExitStack

import numpy as np
import ml_dtypes

import concourse.bass as bass
import concourse.mybir as mybir
from concourse.bass_utils import run_bass_kernel_spmd

F32 = mybir.dt.float32
BF16 = mybir.dt.bfloat16
AF = mybir.ActivationFunctionType
ALU = mybir.AluOpType
AX = mybir.AxisListType

ENGS = ("pe", "act", "dve", "pool", "sp")
EPS = 1e-5
T = 4096
D = 1024
DFF = 2816
NCOL = 2824
LAMBDA_INIT = 0.8 - 0.6 * 1.0
DEBUG = False
RUN_A = True
RUN_P2 = True
NT_B = 8
NT_A = 8


class Prog:
    def __init__(self, nc):
        self.nc = nc
        self.ops = []

    def op(self, eng, fn, reads=(), writes=(), ps=(), dma=False, semkey=None, inc=1):
        self.ops.append(dict(eng=eng, fn=fn, reads=tuple(reads), writes=tuple(writes), ps=tuple(ps),
                             dma=dma, semkey=semkey, inc=inc, deps=set(), signal=False))
        return len(self.ops) - 1

    def dma(self, eng, out, in_, reads=(), writes=(), semkey=None):
        return self.op(eng, lambda e: e.dma_start(out=out, in_=in_), reads, writes,
                       dma=True, semkey=semkey, inc=16)

    def analyze(self):
        ops = self.ops
        last_w, readers = {}, {}
        last_ps = {}
        for i, o in enumerate(ops):
            for k in o["reads"]:
                if k in last_w:
                    o["deps"].add(last_w[k])
            for k in o["writes"]:
                if k in last_w:
                    o["deps"].add(last_w[k])
                for r in readers.get(k, ()):
                    o["deps"].add(r)
            for k in o["reads"]:
                readers.setdefault(k, []).append(i)
            for k in o["writes"]:
                last_w[k] = i
                readers[k] = []
            for b in o["ps"]:
                d = last_ps.setdefault(b, {})
                for e2, j in d.items():
                    if e2 != o["eng"]:
                        o["deps"].add(j)
                d[o["eng"]] = i
            o["deps"].discard(i)
        for i, o in enumerate(ops):
            if o["eng"] == "pe" and not o["dma"]:
                o["deps"] = {d for d in o["deps"] if not (ops[d]["eng"] == "pe" and ops[d]["semkey"] is None)}
            best = {}
            keep = set()
            for d in o["deps"]:
                if ops[d]["semkey"] is not None:
                    keep.add(d)
                else:
                    e2 = ops[d]["eng"]
                    if best.get(e2, -1) < d:
                        best[e2] = d
            keep.update(best.values())
            o["deps"] = keep
            for d in o["deps"]:
                ops[d]["signal"] = True
        cnt = {e: 0 for e in ENGS}
        dcnt = {}
        self.own_keys = []
        self._idx = {}
        for i, o in enumerate(ops):
            if o["semkey"] is not None:
                k = o["semkey"]
                if k not in dcnt:
                    dcnt[k] = 0
                    self.own_keys.append(k)
                dcnt[k] += o["inc"]
                o["sem"] = ("own", k)
                o["val"] = dcnt[k]
                self._idx.setdefault(k, []).append((i, dcnt[k]))
            elif o["signal"]:
                cnt[o["eng"]] += 1
                o["sem"] = ("eng", o["eng"])
                o["val"] = cnt[o["eng"]]
        self.final = dict(dcnt)

    def count_before(self, key, i):
        lst = self._idx[key]
        p = bisect.bisect_left(lst, (i, -1))
        return lst[p - 1][1] if p > 0 else 0

    def emit(self):
        nc = self.nc
        self.analyze()
        ops = self.ops
        with ExitStack() as es:
            sems = {}
            for e in ENGS:
                sems[("eng", e)] = es.enter_context(nc.semaphore("s_" + e))
            for n, k in enumerate(self.own_keys):
                sems[("own", k)] = es.enter_context(nc.semaphore("o%d" % n))
            block = es.enter_context(nc.Block())

            def run_engine(ename, eng):
                waited = {}
                for i, o in enumerate(ops):
                    if o["eng"] != ename:
                        continue
                    need = {}
                    for d in o["deps"]:
                        do = ops[d]
                        s, v = do["sem"], do["val"]
                        if do["semkey"] is not None:
                            v = max(v, self.count_before(do["semkey"], i))
                        if need.get(s, 0) < v:
                            need[s] = v
                    for s, v in need.items():
                        if waited.get(s, 0) >= v:
                            continue
                        eng.wait_ge(sems[s], v)
                        waited[s] = v
                    ins = o["fn"](eng)
                    if o["semkey"] is not None:
                        ins.then_inc(sems[o["sem"]], o["inc"])
                    elif o["signal"]:
                        ins.then_inc(sems[o["sem"]], 1)
                return waited

            @block.tensor
            def _(e):
                run_engine("pe", e)

            @block.scalar
            def _(e):
                run_engine("act", e)

            @block.vector
            def _(e):
                run_engine("dve", e)

            @block.gpsimd
            def _(e):
                run_engine("pool", e)

            @block.sync
            def _(e):
                w = run_engine("sp", e)
                for k, v in self.final.items():
                    if w.get(("own", k), 0) < v:
                        e.wait_ge(sems[("own", k)], v)


def build_program():
    nc = bass.Bass("TRN2", target_bir_lowering=False)

    def din(name, shape, dt=F32):
        return nc.dram_tensor(name, list(shape), dt, kind="ExternalInput").ap()

    xT = din("xT", [D, T])
    xtok = din("xtok", [2048, D])
    w_in = din("w_in", [D, NCOL])
    mixw = din("mixw", [128, 8])
    convw = din("convw", [128, 6, 4])
    convb = din("convb", [128, 6])
    ssdp = din("ssdp", [128, 3, 8])
    lamv = din("lamv", [128, 4, 64])
    w_out = din("w_out", [2048, D])
    ynw = din("ynw", [128, 8])
    ffnwb = din("ffnwb", [128, D])
    w_gate = din("w_gate", [D, DFF])
    w_up = din("w_up", [D, DFF])
    w_down = din("w_down", [DFF, D])
    finw = din("finw", [128, D])
    cU = din("cU", [128, 128])
    cG = din("cG", [128, 128])
    cI = din("cI", [128, 128])
    flags = din("flags", [128, 2])
    out = nc.dram_tensor("out", [2048, D], F32, kind="ExternalOutput").ap()
    if DEBUG:
        dbg_da = nc.dram_tensor("dbg_da", [512, T], BF16, kind="ExternalOutput").ap()
        dbg_ssd = nc.dram_tensor("dbg_ssd", [512, T], BF16, kind="ExternalOutput").ap()

    yda_in = [nc.dram_tensor("yda_in%d" % i, [512, 512], BF16) for i in range(8)]
    yda_out = [nc.dram_tensor("yda_out%d" % i, [1024, 512], BF16) for i in range(8)]
    yssd_in = [nc.dram_tensor("yssd_in%d" % i, [512, 512], BF16) for i in range(8)]
    yssd_out = [nc.dram_tensor("yssd_out%d" % i, [1024, 512], BF16) for i in range(8)]

    P = Prog(nc)
    with ExitStack() as es:
        MEMW = 53000
        mem = es.enter_context(nc.sbuf_tensor("mem", [128, MEMW], F32))

        class Alloc:
            def __init__(self, start, end):
                self.off, self.end = start, end

            def take(self, shape, dt=F32):
                n = int(np.prod(shape[1:]))
                nbytes = n * (2 if dt == BF16 else 4)
                nbytes = (nbytes + 3) // 4 * 4
                assert self.off % 4 == 0 and self.off + nbytes <= self.end, (self.off, nbytes, self.end, shape)
                a = mem[:, self.off // 4:(self.off + nbytes) // 4]
                self.off += nbytes
                if dt == BF16:
                    a = a.bitcast(BF16)[:, 0:n]
                if len(shape) == 3:
                    a = a.rearrange("p (a b) -> p a b", a=shape[1])
                elif len(shape) == 4:
                    a = a.rearrange("p (a b c) -> p a b c", a=shape[1], b=shape[2])
                return a

        psf = es.enter_context(nc.psum_tensor("psf", [128, 7, 512], F32))
        psb = es.enter_context(nc.psum_tensor("psb", [128, 1024], BF16))

        pers = Alloc(0, 1024)
        cst = Alloc(1024, 8192)
        WO_OFF, WG_OFF, STG_OFF, TAIL_OFF, MEM_END = 111440, 144208, 189264, 197456, 212000

        def sb(name, shape, dt=F32):
            return pers.take(shape, dt)

        def sc(name, shape, dt=F32):
            return cst.take(shape, dt)

        Ib = sb("Ib", [128, 128], BF16)
        ynw_s = sb("ynw_s", [128, 8]); flags_s = sb("flags_s", [128, 2])
        wda = sb("wda", [128, 1]); st2 = sb("st2", [128, 4]); neglam = sb("neglam", [128, 1])
        mixw_s = sb("mixw_s", [128, 8])
        Uf = sc("Uf", [128, 128]); Gf = sc("Gf", [128, 128]); If = sc("If", [128, 128])
        Ub = sc("Ub", [128, 128], BF16)
        onesb = sc("onesb", [128, 128], BF16); onesf = sc("onesf", [128, 128])
        convw_s = sc("convw_s", [128, 6, 4]); convb_s = sc("convb_s", [128, 6])
        ssdp_s = sc("ssdp_s", [128, 3, 8]); lamv_s = sc("lamv_s", [128, 4, 64])
        Abc = sc("Abc", [128, 8]); Dsk = sc("Dsk", [128, 8, 64])
        lam2 = sc("lam2", [128, 2]); lamt = sc("lamt", [128, 2, 64])

        small_loads = [(Uf, cU), (Gf, cG), (If, cI), (mixw_s, mixw), (convw_s, convw), (convb_s, convb),
                       (ssdp_s, ssdp), (lamv_s, lamv), (ynw_s, ynw), (flags_s, flags)]
        for n, (dst, src) in enumerate(small_loads):
            P.dma("sp", dst[:], src, writes=["c%d" % n], semkey="c%d" % n)
        CK = ["c%d" % n for n in range(len(small_loads))]
        P.op("dve", lambda e: e.tensor_copy(out=Ub[:], in_=Uf[:]), CK, ["Ub"])
        P.op("dve", lambda e: e.tensor_copy(out=Ib[:], in_=If[:]), CK, ["Ib"])
        P.op("pool", lambda e: e.memset(onesb[:], 1.0), [], ["onesb"])
        P.op("pool", lambda e: e.memset(onesf[:], 1.0), [], ["onesf"])
        P.op("act", lambda e: e.activation(out=Abc[:], in_=ssdp_s[:, 1, :], func=AF.Exp), CK, ["Abc"])
        P.op("dve", lambda e: e.tensor_scalar(out=Abc[:], in0=Abc[:], scalar1=-1.0, scalar2=None, op0=ALU.mult), ["Abc"], ["Abc"])
        P.op("dve", lambda e: e.tensor_copy(out=Dsk[:], in_=ssdp_s[:, 2, :].unsqueeze(2).to_broadcast([128, 8, 64])), CK, ["Dsk"])
        P.op("dve", lambda e: e.tensor_tensor(out=lamt[:, 0, :], in0=lamv_s[:, 0, :], in1=lamv_s[:, 1, :], op=ALU.mult), CK, ["lamt"])
        P.op("dve", lambda e: e.tensor_tensor(out=lamt[:, 1, :], in0=lamv_s[:, 2, :], in1=lamv_s[:, 3, :], op=ALU.mult), ["lamt"], ["lamt"])
        P.op("dve", lambda e: e.tensor_reduce(out=lam2[:], in_=lamt[:], axis=AX.X, op=ALU.add), ["lamt"], ["lam2"])
        P.op("act", lambda e: e.activation(out=lam2[:], in_=lam2[:], func=AF.Exp), ["lam2"], ["lam2"])
        P.op("dve", lambda e: e.tensor_tensor(out=neglam[:], in0=lam2[:, 1:2], in1=lam2[:, 0:1], op=ALU.subtract), ["lam2"], ["neglam"])
        P.op("dve", lambda e: e.tensor_scalar(out=neglam[:], in0=neglam[:], scalar1=-LAMBDA_INIT, scalar2=None, op0=ALU.add), ["neglam"], ["neglam"])

        shared = Alloc(8192, 8192 + 53248)
        Win = shared.take([128, 8, 1536], BF16)
        xTt = shared.take([128, 8, 512])
        nT = shared.take([128, 8, 512], BF16)
        lnv = shared.take([128, 512]); rstd = shared.take([128, 512])
        ARENA0 = shared.off
        Wo = Alloc(WO_OFF, WG_OFF).take([128, 16, 1024], BF16)
        Wg = Alloc(WG_OFF, STG_OFF).take([128, 8, DFF], BF16)
        sreg = Alloc(STG_OFF, TAIL_OFF)
        ffnwb_s = sreg.take([128, D])
        c0b_1 = sreg.take([128, 16, 128], BF16)
        tail = Alloc(TAIL_OFF, MEM_END)
        finw_s = tail.take([128, D])
        actT = tail.take([128, 22, 128], BF16)
        sg = tail.take([128, 2, 128])
        n2T_1 = tail.take([128, 8, 128], BF16)

        xT_v = xT.rearrange("(kt p) t -> p kt t", p=128)
        w_in_v = w_in.rearrange("(kt p) c -> p kt c", p=128)

        def load_win(c0, ncols, tag):
            for kt in range(8):
                P.dma("pool", Win[:, kt, 0:ncols], w_in_v[:, kt, c0:c0 + ncols], writes=["Win"], semkey="Win")

        def norm_part1(i):
            t0 = i * 512
            P.dma("sp", xTt[:, 0:4, :], xT_v[:, 0:4, t0:t0 + 512], writes=["xTt0"], semkey="xTt0")
            P.dma("sp", xTt[:, 4:8, :], xT_v[:, 4:8, t0:t0 + 512], writes=["xTt1"], semkey="xTt1")
            P.op("act", lambda e: e.activation(out=nT[:], in_=xTt[:], func=AF.Square), ["xTt0", "xTt1"], ["nT"])

        def norm_part2(i):
            for kt in range(8):
                P.op("pe", (lambda kt=kt: lambda e: e.matmul(psf[:, 6, :], lhsT=onesb[:], rhs=nT[:, kt, :], start=(kt == 0), stop=(kt == 7)))(),
                     ["nT", "onesb"], [], ps=[6])
            P.op("act", lambda e: e.activation(out=lnv[:], in_=psf[:, 6, :], func=AF.Ln, scale=1.0 / D, bias=EPS), [], ["lnv"], ps=[6])
            P.op("act", lambda e: e.activation(out=rstd[:], in_=lnv[:], func=AF.Exp, scale=-0.5), ["lnv"], ["rstd"])
            for kt in range(8):
                P.op("dve", (lambda kt=kt: lambda e: e.scalar_tensor_tensor(out=nT[:, kt, :], in0=xTt[:, kt, :], scalar=mixw_s[:, kt:kt + 1], in1=rstd[:], op0=ALU.mult, op1=ALU.mult))(),
                     ["xTt0", "xTt1", "rstd", "nT"] + CK, ["nT"])

        def norm_tile(i):
            norm_part1(i)
            norm_part2(i)

        evac_rr = [0]

        def evac(out_ap, in_ap, reads, writes, ps, eng=None):
            if eng is None:
                eng = ("act", "dve")[evac_rr[0] % 2]
                evac_rr[0] += 1
            if eng == "act":
                P.op("act", lambda e: e.copy(out=out_ap, in_=in_ap), reads, writes, ps=ps)
            else:
                P.op(eng, lambda e: e.tensor_copy(out=out_ap, in_=in_ap), reads, writes, ps=ps)

        def proj_fm(bank, c0, reads_extra=()):
            for kt in range(8):
                P.op("pe", (lambda kt=kt: lambda e: e.matmul(psf[:, bank, :], lhsT=Win[:, kt, c0:c0 + 128], rhs=nT[:, kt, :], start=(kt == 0), stop=(kt == 7)))(),
                     ["Win", "nT"], [], ps=[bank])

        def proj_tm(bank, s, c0, n, col0=0):
            for kt in range(8):
                P.op("pe", (lambda kt=kt: lambda e: e.matmul(psf[:, bank, col0:col0 + n], lhsT=nT[:, kt, s * 128:(s + 1) * 128], rhs=Win[:, kt, c0:c0 + n], start=(kt == 0), stop=(kt == 7)))(),
                     ["Win", "nT"], [], ps=[bank])

        w_out_v = w_out.rearrange("(kt p) c -> p kt c", p=128)
        w_gate_v = w_gate.rearrange("(kt p) c -> p kt c", p=128)
        w_up_v = w_up.rearrange("(kt p) c -> p kt c", p=128)
        w_down_v = w_down.rearrange("(kt p) c -> p kt c", p=128)
        P.op("dve", lambda e: e.tensor_scalar(out=wda[:], in0=ynw_s[:, 4:5], scalar1=1.0 - LAMBDA_INIT, scalar2=None, op0=ALU.mult), CK, ["wda"])

        def wdma(dst, src, key, extra_reads=()):
            P.dma("pool", dst, src, reads=list(extra_reads), writes=[key], semkey="w_" + key)

        pre_jobs = []
        for kt in range(0, 16, 2):
            pre_jobs.append((lambda kt=kt: wdma(Wo[:, kt:kt + 2, :], w_out_v[:, kt:kt + 2, :], "Wo")))
        for kt in range(8):
            pre_jobs.append((lambda kt=kt: wdma(Wg[:, kt, :], w_gate_v[:, kt, :], "Wg")))

        def gather(kind, i):
            src = (yda_in if kind == "da" else yssd_in)[i]
            dst = (yda_out if kind == "da" else yssd_out)[i]
            P.op("pool", lambda e: e.collective_compute("AllGather", ALU.bypass, replica_groups=[[0, 1], [2, 3], [4, 5], [6, 7]],
                                                        ins=[src.ap().opt()], outs=[dst.ap().opt()]),
                 ["y%s_in%d" % (kind, i)], ["y%s_out%d" % (kind, i)], semkey="cc_%s%d" % (kind, i), inc=1)

        cv = Alloc(ARENA0, 150000)
        kT = cv.take([128, 4, T], BF16)
        Vaug = cv.take([128, 32, 4, 132], BF16)
        qT = cv.take([128, 4, 512], BF16)
        PT = [[cv.take([128, 512], BF16) for _ in range(2)] for _ in range(2)]
        O_sb = cv.take([128, 8, 130], F32)
        rcp = cv.take([128, 8], F32); rn = cv.take([128, 4], F32)
        o1 = cv.take([128, 4, 128], F32); o2 = cv.take([128, 4, 128], F32)
        ssq4 = cv.take([128, 4], F32); r4 = cv.take([128, 4], F32)
        on = cv.take([128, 4, 128], BF16)
        yTda = cv.take([128, 4, 512], BF16)

        load_win(1288, 1536, "B")
        P.op("pool", lambda e: e.memset(Vaug[:, :, :, 128:129], 1.0), [], ["Vaug"])

        pend = []

        def make_fin(i, h):
            def fin1():
                P.op("dve", lambda e: e.reciprocal(out=rcp[:], in_=O_sb[:, :, 128]), ["O_sb"], ["rcp"])
                P.op("dve", lambda e: e.tensor_scalar(out=rn[:], in0=rcp[:, 4:8], scalar1=neglam[:, 0:1], scalar2=None, op0=ALU.mult), ["rcp", "neglam"], ["rn"])
                P.op("dve", lambda e: e.tensor_tensor(out=o1[:], in0=O_sb[:, 0:4, 0:128], in1=rcp[:, 0:4].unsqueeze(2).to_broadcast([128, 4, 128]), op=ALU.mult), ["O_sb", "rcp"], ["o1"])
                P.op("pool", lambda e: e.tensor_tensor(out=o2[:], in0=O_sb[:, 4:8, 0:128], in1=rn[:].unsqueeze(2).to_broadcast([128, 4, 128]), op=ALU.mult), ["O_sb", "rn"], ["o2"])
                P.op("dve", lambda e: e.tensor_tensor(out=o1[:], in0=o1[:], in1=o2[:], op=ALU.add), ["o1", "o2"], ["o1"])
                P.op("pool", lambda e: e.tensor_tensor(out=o2[:], in0=o1[:], in1=o1[:], op=ALU.mult), ["o1", "o2"], ["o2"])
                P.op("dve", lambda e: e.tensor_reduce(out=ssq4[:], in_=o2[:], axis=AX.X, op=ALU.add), ["o2"], ["ssq4"])

            def fin2():
                P.op("act", lambda e: e.activation(out=r4[:], in_=ssq4[:], func=AF.Ln, scale=1.0 / 128, bias=EPS), ["ssq4"], ["r4"])
                P.op("act", lambda e: e.activation(out=r4[:], in_=r4[:], func=AF.Exp, scale=-0.5), ["r4"], ["r4"])
                P.op("dve", lambda e: e.tensor_tensor(out=on[:], in0=o1[:], in1=r4[:].unsqueeze(2).to_broadcast([128, 4, 128]), op=ALU.mult), ["o1", "r4"], ["on"])

            def fin3():
                for qs in range(4):
                    P.op("pe", (lambda qs=qs: lambda e: e.transpose(psb[:, qs * 128:(qs + 1) * 128], on[:, qs, :], Ib[:]))(), ["on", "Ib"], [], ps=[7])
                P.op("dve", lambda e: e.tensor_scalar(out=yTda[:, h, :], in0=psb[:, 0:512], scalar1=wda[:, 0:1], scalar2=None, op0=ALU.mult), ["wda"], ["yTda"], ps=[7])
                if h == 3:
                    P.dma("sp", yda_in[i].ap().rearrange("(h p) t -> p h t", p=128), yTda[:], reads=["yTda"], writes=["yda_in%d" % i], semkey="yda_st")
                    gather("da", i)
            return [fin1, fin2, fin3]

        def run_pending(stage):
            if pend and len(pend[0]) == 3 - stage:
                pend[0].pop(0)()
                if not pend[0]:
                    pend.pop(0)

        if NT_B:
            norm_tile(0)
        for i in range(NT_B):
            t0 = i * 512
            for h in range(4):
                bank = h % 4
                proj_fm(bank, h * 128)
                evac(qT[:, h, :], psf[:, bank, :], [], ["qT"], [bank], eng="dve")
            for h in range(4):
                bank = h % 4
                proj_fm(bank, 512 + h * 128)
                evac(kT[:, h, t0:t0 + 512], psf[:, bank, :], [], ["kT"], [bank], eng="dve")
            for s in range(4):
                bank = s % 4
                proj_tm(bank, s, 1024, 512)
                evac(Vaug[:, 4 * i + s, :, 0:128], psf[:, bank, :].rearrange("p (h v) -> p h v", h=4), [], ["Vaug"], [bank], eng="dve")
            nkb = 4 * i + 4
            for h in range(4):
                started = set()
                steps = list(range(nkb))

                def emit_S(st, kb, h=h, i=i):
                    qlo = max(0, kb - 4 * i) * 128
                    n = 512 - qlo
                    for m in range(2):
                        bank = 2 * (st % 2) + m
                        P.op("pe", (lambda m=m, bank=bank, kb=kb, qlo=qlo, n=n: lambda e: e.matmul(
                            psf[:, bank, 0:n], lhsT=kT[64 * m:64 * m + 64, h, kb * 128:(kb + 1) * 128],
                            rhs=qT[64 * m:64 * m + 64, h, qlo:512], start=True, stop=True))(),
                            ["kT", "qT"], [], ps=[bank])

                def emit_rest(st, kb, h=h, i=i, started=started):
                    qlo = max(0, kb - 4 * i) * 128
                    n = 512 - qlo
                    b = st % 2
                    for m in range(2):
                        bank = 2 * b + m
                        P.op("act", (lambda m=m, bank=bank, n=n, b=b: lambda e: e.activation(out=PT[m][b][:, 0:n], in_=psf[:, bank, 0:n], func=AF.Exp, scale=0.125))(),
                             [], ["PT%d%d" % (m, b)], ps=[bank])
                        if kb >= 4 * i:
                            P.op("pool", (lambda m=m, b=b: lambda e: e.tensor_tensor(out=PT[m][b][:, 0:128], in0=PT[m][b][:, 0:128], in1=Ub[:], op=ALU.mult))(),
                                 ["PT%d%d" % (m, b), "Ub"], ["PT%d%d" % (m, b)])
                    for m in range(2):
                        for qs in range(max(0, kb - 4 * i), 4):
                            a = m * 4 + qs
                            bank = 4 + a // 3
                            col = (a % 3) * 130
                            first = (kb == 0) and (bank not in started)
                            if kb == 0:
                                started.add(bank)
                            last = (kb == 4 * i + qs)
                            P.op("pe", (lambda m=m, b=b, qs=qs, qlo=qlo, bank=bank, col=col, first=first, last=last, kb=kb: lambda e: e.matmul(
                                psf[:, bank, col:col + 129], lhsT=PT[m][b][:, qs * 128 - qlo:qs * 128 - qlo + 128],
                                rhs=Vaug[:, kb, h, 0:129], start=first, stop=last, skip_group_check=True))(),
                                ["PT%d%d" % (m, b), "Vaug"], [], ps=[bank])

                emit_S(0, 0)
                for st, kb in enumerate(steps):
                    if st + 1 < nkb:
                        emit_S(st + 1, steps[st + 1])
                    emit_rest(st, kb)
                    if st == 0:
                        run_pending(0)
                    elif st == 1:
                        run_pending(1)
                    elif st == 3:
                        run_pending(2)
                    if h == 1 and st == 2 and i + 1 < NT_B:
                        norm_part1(i + 1)
                while pend:
                    run_pending(3 - len(pend[0]))
                for bank, na in ((4, 3), (5, 3), (6, 2)):
                    a0 = (bank - 4) * 3
                    P.op("act", (lambda bank=bank, na=na, a0=a0: lambda e: e.copy(
                        out=O_sb[:, a0:a0 + na, :], in_=psf[:, bank, 0:na * 130].rearrange("p (a c) -> p a c", a=na)))(),
                        [], ["O_sb"], ps=[bank])
                pend.append(make_fin(i, h))
                if h == 2 and i + 1 < NT_B:
                    norm_part2(i + 1)
        while pend:
            run_pending(3 - len(pend[0]))

        cv = Alloc(ARENA0, WO_OFF)
        ta = Alloc(TAIL_OFF, MEM_END)
        ub = [cv.take([128, 516], F32) for _ in range(2)]
        cacc = [cv.take([128, 512], F32) for _ in range(2)]
        carry = cv.take([128, 6, 3], F32)
        xcs = [cv.take([128, 6, 512], BF16), ta.take([128, 6, 512], BF16)]
        zs = cv.take([128, 4, 512], F32)

        def dtset(al):
            d = {}
            for nm in ("dtb", "dtv", "adt", "acs", "eacs", "dd", "decst", "cdec", "dtdec", "adthf"):
                d[nm] = al.take([128, 32], F32)
            d["adth"] = al.take([128, 32], BF16)
            d["adtl"] = al.take([128, 32], BF16)
            return d
        dts = [dtset(cv), dtset(ta)]
        xtm = ta.take([128, 640], BF16)
        xdt = ta.take([128, 8, 64], BF16)
        xdd = ta.take([128, 8, 64], BF16)
        rhs_hi = cv.take([128, 8, 128], BF16)
        rhs_lo = cv.take([128, 8, 128], BF16)
        expD = cv.take([128, 8, 128], F32)
        CBm = cv.take([128, 128], F32)
        MT = cv.take([128, 8, 128], BF16)
        t1s = [cv.take([128, 8, 64], F32) for _ in range(2)]
        t2 = cv.take([128, 8, 64], F32)
        Sst = cv.take([128, 8, 64], F32); Sbf = cv.take([128, 512], BF16)
        ssq1 = cv.take([128, 2], F32)
        ygn = cv.take([128, 512], BF16)
        yTs = cv.take([128, 4, 512], BF16)
        Gb = cv.take([128, 128], BF16)
        PB_KEYS = ["kT", "Vaug", "qT", "PT00", "PT01", "PT10", "PT11", "O_sb", "rcp", "rn", "o1", "o2", "ssq4", "r4", "on", "yTda"]
        P.op("pool", lambda e: e.memset(carry[:], 0.0), PB_KEYS, PB_KEYS + ["carry", "Wo", "Wg"])
        P.op("pool", lambda e: e.memset(Sst[:], 0.0), ["carry"], ["Sst"])
        P.op("pool", lambda e: e.memset(Sbf[:], 0.0), ["carry"], ["Sbf"])
        P.op("dve", lambda e: e.tensor_copy(out=Gb[:], in_=Gf[:]), ["carry"] + CK, ["Gb"])
        FENCE = ["carry"]

        P.op("pool", lambda e: e.memset(Win[:, :, 1280:1296], 0.0), ["Win"], ["Win"])
        load_win(0, 1288, "A")
        ZC, XC, DTC = 0, 512, 1280
        NTA = NT_A if RUN_A else 0

        def prologue_main(i):
            par = i % 2
            xc = xcs[par]
            d = dts[par]
            sx = "%d" % par
            for _ in range(2):
                if pre_jobs:
                    pre_jobs.pop(0)()
            for c in range(6):
                bank = c % 2
                u = ub[c % 2]
                acc = cacc[c % 2]
                ceng = "dve" if c % 2 == 0 else "pool"
                proj_fm(bank, XC + c * 128)
                P.op("pool", (lambda u=u, c=c: lambda e: e.tensor_copy(out=u[:, 0:3], in_=carry[:, c, :]))(), ["carry"] + FENCE, ["ub%d" % (c % 2)])
                P.op("act", (lambda u=u, bank=bank: lambda e: e.copy(out=u[:, 3:515], in_=psf[:, bank, :]))(), [], ["ub%d" % (c % 2)], ps=[bank])
                P.op("pool", (lambda u=u, c=c: lambda e: e.tensor_copy(out=carry[:, c, :], in_=u[:, 512:515]))(), ["ub%d" % (c % 2)], ["carry"])
                if ceng == "dve":
                    P.op(ceng, (lambda u=u, acc=acc, c=c: lambda e: e.tensor_scalar(out=acc[:], in0=u[:, 3:515], scalar1=convw_s[:, c, 3:4], scalar2=convb_s[:, c:c + 1], op0=ALU.mult, op1=ALU.add))(),
                         ["ub%d" % (c % 2)] + CK, ["cacc%d" % (c % 2)])
                else:
                    P.op(ceng, (lambda u=u, acc=acc, c=c: lambda e: e.tensor_tensor(out=acc[:], in0=u[:, 3:515], in1=convw_s[:, c, 3:4].to_broadcast([128, 512]), op=ALU.mult))(),
                         ["ub%d" % (c % 2)] + CK, ["cacc%d" % (c % 2)])
                    P.op(ceng, (lambda acc=acc, c=c: lambda e: e.tensor_tensor(out=acc[:], in0=acc[:], in1=convb_s[:, c:c + 1].to_broadcast([128, 512]), op=ALU.add))(),
                         ["cacc%d" % (c % 2)] + CK, ["cacc%d" % (c % 2)])
                for jt in (2, 1, 0):
                    if ceng == "dve":
                        P.op(ceng, (lambda u=u, acc=acc, c=c, jt=jt: lambda e: e.scalar_tensor_tensor(out=acc[:], in0=u[:, jt:jt + 512], scalar=convw_s[:, c, jt:jt + 1], in1=acc[:], op0=ALU.mult, op1=ALU.add))(),
                             ["ub%d" % (c % 2), "cacc%d" % (c % 2)], ["cacc%d" % (c % 2)])
                    else:
                        t2f = t2[:].rearrange("p h d -> p (h d)")
                        P.op(ceng, (lambda u=u, c=c, jt=jt, t2f=t2f: lambda e: e.tensor_tensor(out=t2f, in0=u[:, jt:jt + 512], in1=convw_s[:, c, jt:jt + 1].to_broadcast([128, 512]), op=ALU.mult))(),
                             ["ub%d" % (c % 2), "t2"], ["t2"])
                        P.op(ceng, (lambda acc=acc, t2f=t2f: lambda e: e.tensor_tensor(out=acc[:], in0=acc[:], in1=t2f, op=ALU.add))(),
                             ["t2", "cacc%d" % (c % 2)], ["cacc%d" % (c % 2)])
                P.op("act", (lambda acc=acc, c=c, xc=xc: lambda e: e.activation(out=xc[:, c, :], in_=acc[:], func=AF.Silu))(), ["cacc%d" % (c % 2)] + FENCE, ["xc" + sx])
            for s_ in range(4):
                proj_tm(6, s_, DTC, 8, col0=s_ * 8)
            v4 = lambda t_: t_[:].rearrange("p (c h) -> p c h", c=4)
            P.op("dve", lambda e: e.tensor_tensor(out=v4(d["dtb"]), in0=psf[:, 6, 0:32].rearrange("p (c h) -> p c h", c=4),
                                                  in1=ssdp_s[:, 0, :].unsqueeze(1).to_broadcast([128, 4, 8]), op=ALU.add), CK + FENCE, ["dtb" + sx], ps=[6])
            P.op("act", lambda e: e.activation(out=d["dtb"][:], in_=d["dtb"][:], func=AF.Exp), ["dtb" + sx], ["dtb" + sx])
            P.op("act", lambda e: e.activation(out=d["dtv"][:], in_=d["dtb"][:], func=AF.Ln, bias=1.0), ["dtb" + sx], ["dtv" + sx])
            P.op("dve", lambda e: e.tensor_tensor(out=v4(d["adt"]), in0=v4(d["dtv"]), in1=Abc[:].unsqueeze(1).to_broadcast([128, 4, 8]), op=ALU.mult), ["dtv" + sx, "Abc"], ["adt" + sx])
            P.op("pe", lambda e: e.matmul(psf[:, 6, 32:64], lhsT=Uf[:], rhs=d["adt"][:], start=True, stop=True), ["adt" + sx] + CK, [], ps=[6])
            P.op("pe", lambda e: e.matmul(psf[:, 6, 64:96], lhsT=onesf[:], rhs=d["adt"][:], start=False, stop=True, skip_group_check=True), ["adt" + sx, "onesf"], [], ps=[6])
            P.op("act", lambda e: e.copy(out=d["acs"][:], in_=psf[:, 6, 32:64]), [], ["acs" + sx], ps=[6])
            P.op("act", lambda e: e.activation(out=d["eacs"][:], in_=psf[:, 6, 32:64], func=AF.Exp), [], ["eacs" + sx], ps=[6])
            P.op("act", lambda e: e.activation(out=d["cdec"][:], in_=psf[:, 6, 64:96], func=AF.Exp), [], ["cdec" + sx], ps=[6])
            P.op("dve", lambda e: e.tensor_tensor(out=d["dd"][:], in0=psf[:, 6, 64:96], in1=d["acs"][:], op=ALU.subtract), ["acs" + sx], ["dd" + sx], ps=[6])
            P.op("act", lambda e: e.activation(out=d["decst"][:], in_=d["dd"][:], func=AF.Exp), ["dd" + sx], ["decst" + sx])
            P.op("dve", lambda e: e.tensor_tensor(out=d["dtdec"][:], in0=d["dtv"][:], in1=d["decst"][:], op=ALU.mult), ["dtv" + sx, "decst" + sx], ["dtdec" + sx])
            P.op("dve", lambda e: e.tensor_copy(out=d["adth"][:], in_=d["adt"][:]), ["adt" + sx], ["adth" + sx])
            P.op("dve", lambda e: e.tensor_copy(out=d["adthf"][:], in_=d["adth"][:]), ["adth" + sx], ["adthf" + sx])
            P.op("dve", lambda e: e.tensor_tensor(out=d["adtl"][:], in0=d["adt"][:], in1=d["adthf"][:], op=ALU.subtract), ["adt" + sx, "adthf" + sx], ["adtl" + sx])

        def zproj(i):
            for s_ in range(4):
                bank = s_ % 2
                proj_tm(bank, s_, ZC, 512)
                P.op("act", (lambda s_=s_, bank=bank: lambda e: e.activation(out=zs[:, s_, :], in_=psf[:, bank, :], func=AF.Silu))(), FENCE, ["zs%d" % s_], ps=[bank])

        def stageF(i, c):
            par = i % 2
            xc = xcs[par]
            d = dts[par]
            sx = "%d" % par
            t1 = t1s[c % 2]
            t1k = "t1%d" % (c % 2)
            cs = slice(c * 128, (c + 1) * 128)
            hs = slice(c * 8, (c + 1) * 8)
            for ct in range(5):
                P.op("pe", (lambda ct=ct: lambda e: e.transpose(psb[:, ct * 128:(ct + 1) * 128], xc[:, ct, cs], Ib[:]))(), ["xc" + sx, "Ib"], [], ps=[7])
            P.op("act", lambda e: e.copy(out=xtm[:], in_=psb[:, 0:640]), [], ["xtm"], ps=[7])
            P.op("dve", lambda e: e.tensor_tensor(out=rhs_hi[:], in0=Ub[:].unsqueeze(1).to_broadcast([128, 8, 128]),
                                                  in1=d["adth"][:, hs].unsqueeze(2).to_broadcast([128, 8, 128]), op=ALU.mult), ["adth" + sx, "Ub"], ["rhs_hi"])
            P.op("pool", lambda e: e.tensor_tensor(out=rhs_lo[:], in0=Ub[:].unsqueeze(1).to_broadcast([128, 8, 128]),
                                                   in1=d["adtl"][:, hs].unsqueeze(2).to_broadcast([128, 8, 128]), op=ALU.mult), ["adtl" + sx, "Ub"], ["rhs_lo"])
            for hh in range(2):
                P.op("pe", (lambda hh=hh: lambda e: e.matmul(psf[:, 2 + hh, :], lhsT=Gb[:], rhs=rhs_hi[:, 4 * hh:4 * hh + 4, :], start=True, stop=False))(), ["rhs_hi", "Gb"], [], ps=[2 + hh])
                P.op("pe", (lambda hh=hh: lambda e: e.matmul(psf[:, 2 + hh, :], lhsT=Gb[:], rhs=rhs_lo[:, 4 * hh:4 * hh + 4, :], start=False, stop=True))(), ["rhs_lo", "Gb"], [], ps=[2 + hh])
            P.op("pe", lambda e: e.matmul(psf[:, 4, 0:128], lhsT=xc[:, 4, cs], rhs=xc[:, 5, cs], start=True, stop=True), ["xc" + sx], [], ps=[4])
            P.op("dve", lambda e: e.tensor_tensor(out=CBm[:], in0=psf[:, 4, 0:128], in1=Uf[:], op=ALU.mult), CK, ["CBm"], ps=[4])
            for hh in range(2):
                P.op("act", (lambda hh=hh: lambda e: e.activation(out=expD[:, 4 * hh:4 * hh + 4, :], in_=psf[:, 2 + hh, :].rearrange("p (h l) -> p h l", h=4), func=AF.Exp))(),
                     [], ["expD"], ps=[2 + hh])
            P.op("dve", lambda e: e.tensor_tensor(out=MT[:], in0=expD[:], in1=CBm[:].unsqueeze(1).to_broadcast([128, 8, 128]), op=ALU.mult), ["expD", "CBm"], ["MT"])
            xv = xtm[:, 0:512].rearrange("p (h d) -> p h d", h=8)
            P.op("pool", lambda e: e.tensor_tensor(out=xdt[:], in0=xv, in1=d["dtv"][:, hs].unsqueeze(2).to_broadcast([128, 8, 64]), op=ALU.mult), ["xtm", "dtv" + sx], ["xdt"])
            P.op("pool", lambda e: e.tensor_tensor(out=xdd[:], in0=xv, in1=d["dtdec"][:, hs].unsqueeze(2).to_broadcast([128, 8, 64]), op=ALU.mult), ["xtm", "dtdec" + sx], ["xdd"])
            P.op("pool", lambda e: e.tensor_tensor(out=t2[:], in0=xv, in1=Dsk[:], op=ALU.mult), ["xtm", "Dsk"], ["t2"])
            for h in range(8):
                P.op("pe", (lambda h=h: lambda e: e.matmul(psf[:, 5, h * 64:(h + 1) * 64], lhsT=MT[:, h, :], rhs=xdt[:, h, :], start=(h == 0), stop=True, skip_group_check=True))(),
                     ["MT", "xdt"], [], ps=[5])
            P.op("pe", lambda e: e.matmul(psf[:, 0, :], lhsT=xc[:, 5, cs], rhs=Sbf[:], start=True, stop=True), ["xc" + sx, "Sbf"], [], ps=[0])
            P.op("pe", lambda e: e.matmul(psf[:, 1, :], lhsT=xtm[:, 512:640], rhs=xdd[:].rearrange("p h d -> p (h d)"), start=True, stop=True), ["xtm", "xdd"], [], ps=[1])
            P.op("dve", lambda e: e.tensor_tensor(out=Sst[:], in0=Sst[:], in1=d["cdec"][:, hs].unsqueeze(2).to_broadcast([128, 8, 64]), op=ALU.mult), ["Sst", "cdec" + sx], ["Sst"])
            P.op("dve", lambda e: e.tensor_tensor(out=Sst[:], in0=Sst[:], in1=psf[:, 1, :].rearrange("p (h d) -> p h d", h=8), op=ALU.add), ["Sst"], ["Sst"], ps=[1])
            P.op("act", lambda e: e.copy(out=Sbf[:], in_=Sst[:].rearrange("p h d -> p (h d)")), ["Sst"], ["Sbf"])
            P.op("dve", lambda e: e.tensor_tensor(out=t1[:], in0=psf[:, 0, :].rearrange("p (h d) -> p h d", h=8),
                                                  in1=d["eacs"][:, hs].unsqueeze(2).to_broadcast([128, 8, 64]), op=ALU.mult), ["eacs" + sx], [t1k], ps=[0])
            P.op("dve", lambda e: e.tensor_tensor(out=t1[:], in0=t1[:], in1=psf[:, 5, :].rearrange("p (h d) -> p h d", h=8), op=ALU.add), [t1k], [t1k], ps=[5])
            P.op("dve", lambda e: e.tensor_tensor(out=t1[:], in0=t1[:], in1=t2[:], op=ALU.add), [t1k, "t2"], [t1k])

        def stageK(i, c):
            t1 = t1s[c % 2]
            t1k = "t1%d" % (c % 2)
            cs = slice(c * 128, (c + 1) * 128)
            P.op("dve", lambda e: e.tensor_tensor(out=t1[:], in0=t1[:], in1=zs[:, c, :].rearrange("p (h d) -> p h d", h=8), op=ALU.mult), [t1k, "zs%d" % c], [t1k])
            P.op("act", lambda e: e.activation(out=ygn[:], in_=t1[:].rearrange("p h d -> p (h d)"), func=AF.Square, accum_out=ssq1[:, 0:1]), [t1k, "ygn"], ["ygn", "ssq1"])
            P.op("act", lambda e: e.activation(out=ssq1[:, 1:2], in_=ssq1[:, 0:1], func=AF.Ln, scale=1.0 / 512, bias=EPS), ["ssq1"], ["ssq1"])
            P.op("act", lambda e: e.activation(out=ssq1[:, 1:2], in_=ssq1[:, 1:2], func=AF.Exp, scale=-0.5), ["ssq1"], ["ssq1"])
            P.op("dve", lambda e: e.tensor_scalar(out=ygn[:], in0=t1[:].rearrange("p h d -> p (h d)"), scalar1=ssq1[:, 1:2], scalar2=None, op0=ALU.mult), [t1k, "ssq1", "ygn"], ["ygn"])
            for ft in range(4):
                P.op("pe", (lambda ft=ft: lambda e: e.transpose(psb[:, 512 + ft * 128:512 + (ft + 1) * 128], ygn[:, ft * 128:(ft + 1) * 128], Ib[:]))(), ["ygn", "Ib"], [], ps=[7])
            P.op("dve", lambda e: e.tensor_tensor(out=yTs[:, :, cs], in0=psb[:, 512:1024].rearrange("p (f t) -> p f t", f=4),
                                                  in1=ynw_s[:, 0:4].unsqueeze(2).to_broadcast([128, 4, 128]), op=ALU.mult), CK, ["yTs"], ps=[7])
            if c == 3:
                P.dma("sp", yssd_in[i].ap().rearrange("(f p) t -> p f t", p=128), yTs[:], reads=["yTs"], writes=["yssd_in%d" % i], semkey="yssd_st")
                gather("ssd", i)

        if NTA:
            norm_tile(0)
            prologue_main(0)
            zproj(0)
        for i in range(NTA):
            nxt = i + 1 < NTA
            stageF(i, 0)
            stageF(i, 1)
            stageK(i, 0)
            if nxt:
                norm_tile(i + 1)
            stageF(i, 2)
            stageK(i, 1)
            stageF(i, 3)
            stageK(i, 2)
            if nxt:
                prologue_main(i + 1)
            stageK(i, 3)
            if nxt:
                zproj(i + 1)

        while pre_jobs and RUN_P2:
            pre_jobs.pop(0)()

        if DEBUG:
            for i in range(NT_B):
                P.dma("sp", dbg_da[:, i * 512:(i + 1) * 512], yda_in[i].ap(), reads=["yda_in%d" % i], semkey="dbg1")
            for i in range(NT_A if RUN_A else 0):
                P.dma("sp", dbg_ssd[:, i * 512:(i + 1) * 512], yssd_in[i].ap(), reads=["yssd_in%d" % i], semkey="dbg2")

        PA_KEYS = (["ub0", "ub1", "cacc0", "cacc1", "carry", "xc0", "xc1", "zs0", "zs1", "zs2", "zs3", "xtm", "rhs_hi", "rhs_lo", "expD", "CBm", "MT",
                    "xdt", "xdd", "t10", "t11", "t2", "Sst", "Sbf", "ssq1", "ygn", "yTs", "Gb"]
                   + [nm + sx for nm in ("dtb", "dtv", "adt", "acs", "eacs", "dd", "decst", "cdec", "dtdec", "adthf", "adth", "adtl") for sx in ("0", "1")])
        cv = Alloc(1024, WO_OFF)
        Wu = cv.take([128, 8, DFF], BF16)
        Wd = cv.take([128, 22, 1024], BF16)
        c0bs = [cv.take([128, 16, 128], BF16), c0b_1]
        c1h = cv.take([128, 8, 128], BF16)
        h1 = [cv.take([128, 1024], F32) for _ in range(2)]
        n2 = cv.take([128, 1024], BF16)
        n2Ts = [cv.take([128, 8, 128], BF16), n2T_1]
        P.op("pool", lambda e: e.memset(st2[:], 0.0), PA_KEYS + CK + ["Win", "nT", "xTt0", "xTt1", "lnv", "rstd", "Ub", "onesb", "onesf", "Abc", "Dsk", "lamt", "lam2"],
             PA_KEYS + ["Wd", "Wu", "p2fence", "st2", "finw", "actT", "sg0", "sg1", "ffnwb", "c0b0", "c0b1", "c1h", "h10", "h11", "n2", "n2T0", "n2T1"])
        F2 = ["p2fence"]
        NT2 = 16 if RUN_P2 else 0
        if NT2:
            P.dma("sp", finw_s[:], finw, reads=F2, writes=["finw"], semkey="finw")
            P.dma("sp", ffnwb_s[:], ffnwb, reads=F2, writes=["ffnwb"], semkey="ffnwb")
            for kt in range(8):
                wdma(Wu[:, kt, :], w_up_v[:, kt, :], "Wu", F2)
            for kt in range(0, 22, 2):
                wdma(Wd[:, kt:kt + 2, :], w_down_v[:, kt:kt + 2, :], "Wd", F2)

        ys_v = [t_.ap().rearrange("(kt p) t -> p kt t", p=128) for t_ in yssd_out]
        yd_v = [t_.ap().rearrange("(kt p) t -> p kt t", p=128) for t_ in yda_out]

        def stA(tt):
            T0 = tt * 128
            hb, hk = h1[tt % 2], "h1%d" % (tt % 2)
            c0b, ck0 = c0bs[tt % 2], "c0b%d" % (tt % 2)
            n2T, nk = n2Ts[tt % 2], "n2T%d" % (tt % 2)
            P.dma("sp", hb[:], xtok[T0:T0 + 128, :], reads=F2, writes=[hk], semkey=hk)
            ti, tc = T0 // 512, T0 % 512
            for half, (src, srckey) in enumerate(((ys_v, "yssd_out"), (yd_v, "yda_out"))):
                P.dma("sp", c0b[:, 8 * half:8 * half + 8, :], src[ti][:, :, tc:tc + 128], reads=[srckey + str(ti)] + F2, writes=[ck0], semkey=ck0)
                P.dma("sp", c1h[:], src[4 + ti][:, :, tc:tc + 128], reads=[srckey + str(4 + ti)] + F2, writes=["c1h"], semkey="c1h")
                P.op("dve", lambda e: e.tensor_scalar(out=c1h[:], in0=c1h[:], scalar1=flags_s[:, 1:2], scalar2=None, op0=ALU.mult), ["c1h"], ["c1h"])
                P.op("dve", (lambda half=half: lambda e: e.scalar_tensor_tensor(out=c0b[:, 8 * half:8 * half + 8, :], in0=c0b[:, 8 * half:8 * half + 8, :], scalar=flags_s[:, 0:1], in1=c1h[:], op0=ALU.mult, op1=ALU.add))(),
                     [ck0, "c1h"], [ck0])
            for dh in range(2):
                bank = dh
                for kt in range(16):
                    P.op("pe", (lambda kt=kt, dh=dh, bank=bank: lambda e: e.matmul(psf[:, bank, :], lhsT=c0b[:, kt, :], rhs=Wo[:, kt, dh * 512:(dh + 1) * 512], start=(kt == 0), stop=(kt == 15)))(),
                         [ck0, "Wo"], [], ps=[bank])
                P.op("dve", (lambda dh=dh, bank=bank: lambda e: e.tensor_tensor(out=hb[:, dh * 512:(dh + 1) * 512], in0=psf[:, bank, :], in1=hb[:, dh * 512:(dh + 1) * 512], op=ALU.add))(),
                     [hk], [hk], ps=[bank])
            P.op("act", lambda e: e.activation(out=n2[:], in_=hb[:], func=AF.Square, accum_out=st2[:, 0:1]), [hk, "n2"], ["n2", "st2a"])
            P.op("act", lambda e: e.activation(out=st2[:, 1:2], in_=st2[:, 0:1], func=AF.Ln, scale=1.0 / D, bias=EPS), ["st2a"], ["st2a"])
            P.op("act", lambda e: e.activation(out=st2[:, 1:2], in_=st2[:, 1:2], func=AF.Exp, scale=-0.5), ["st2a"], ["st2a"])
            P.op("dve", lambda e: e.scalar_tensor_tensor(out=n2[:], in0=hb[:], scalar=st2[:, 1:2], in1=ffnwb_s[:], op0=ALU.mult, op1=ALU.mult), [hk, "st2a", "n2", "ffnwb"], ["n2"])
            for kt in range(8):
                P.op("pe", (lambda kt=kt: lambda e: e.transpose(psb[:, kt * 128:(kt + 1) * 128], n2[:, kt * 128:(kt + 1) * 128], Ib[:]))(), ["n2", "Ib"], [], ps=[7])
            P.op("act", lambda e: e.copy(out=n2T[:], in_=psb[:].rearrange("p (k t) -> p k t", k=8)), F2, [nk], ps=[7])

        def stB(tt):
            n2T, nk = n2Ts[tt % 2], "n2T%d" % (tt % 2)
            for f in range(22):
                gb = 2 + (f % 2)
                ubk = 4 + (f % 2)
                for kt in range(8):
                    P.op("pe", (lambda kt=kt, f=f, gb=gb: lambda e: e.matmul(psf[:, gb, 0:128], lhsT=Wg[:, kt, f * 128:(f + 1) * 128], rhs=n2T[:, kt, :], start=(kt == 0), stop=(kt == 7)))(),
                         ["Wg", nk], [], ps=[gb])
                for kt in range(8):
                    P.op("pe", (lambda kt=kt, f=f, ubk=ubk: lambda e: e.matmul(psf[:, ubk, 0:128], lhsT=Wu[:, kt, f * 128:(f + 1) * 128], rhs=n2T[:, kt, :], start=(kt == 0), stop=(kt == 7)))(),
                         ["Wu", nk], [], ps=[ubk])
                P.op("act", (lambda f=f, gb=gb: lambda e: e.activation(out=sg[:, f % 2, :], in_=psf[:, gb, 0:128], func=AF.Silu))(), F2, ["sg%d" % (f % 2)], ps=[gb])
                P.op("dve", (lambda f=f, ubk=ubk: lambda e: e.tensor_tensor(out=actT[:, f, :], in0=sg[:, f % 2, :], in1=psf[:, ubk, 0:128], op=ALU.mult))(), ["sg%d" % (f % 2)] + F2, ["actT"], ps=[ubk])

        def stC(tt):
            T0 = tt * 128
            hb, hk = h1[tt % 2], "h1%d" % (tt % 2)
            for dh in range(2):
                bank = dh
                for f in range(22):
                    P.op("pe", (lambda f=f, dh=dh, bank=bank: lambda e: e.matmul(psf[:, bank, :], lhsT=actT[:, f, :], rhs=Wd[:, f, dh * 512:(dh + 1) * 512], start=(f == 0), stop=(f == 21)))(),
                         ["actT", "Wd"], [], ps=[bank])
                P.op("dve", (lambda dh=dh, bank=bank: lambda e: e.tensor_tensor(out=hb[:, dh * 512:(dh + 1) * 512], in0=psf[:, bank, :], in1=hb[:, dh * 512:(dh + 1) * 512], op=ALU.add))(),
                     [hk], [hk], ps=[bank])
            P.op("act", lambda e: e.activation(out=actT[:, 0:8, :].rearrange("p a b -> p (a b)"), in_=hb[:], func=AF.Square, accum_out=st2[:, 2:3]), [hk, "actT"], ["actT", "st2b"])
            P.op("act", lambda e: e.activation(out=st2[:, 3:4], in_=st2[:, 2:3], func=AF.Ln, scale=1.0 / D, bias=EPS), ["st2b"], ["st2b"])
            P.op("act", lambda e: e.activation(out=st2[:, 3:4], in_=st2[:, 3:4], func=AF.Exp, scale=-0.5), ["st2b"], ["st2b"])
            P.op("dve", lambda e: e.scalar_tensor_tensor(out=hb[:], in0=hb[:], scalar=st2[:, 3:4], in1=finw_s[:], op0=ALU.mult, op1=ALU.mult), [hk, "st2b", "finw"], [hk])
            P.dma("sp", out[T0:T0 + 128, :], hb[:], reads=[hk], writes=["out"], semkey="ost%d" % (tt % 2))

        if NT2:
            stA(0)
        for tt in range(NT2):
            stB(tt)
            if tt + 1 < NT2:
                stA(tt + 1)
            stC(tt)

        P.emit()
    return nc


_NC = None


def kernel(x, mix_norm_w, w_in, conv_w, conv_b, dt_bias, a_log, d_skip, ssd_norm_w,
           lam_q1, lam_k1, lam_q2, lam_k2, subln_w, w_out, ffn_norm_w, w_gate, w_up, w_down,
           final_norm_w):
    global _NC
    f32 = np.float32
    x = np.asarray(x, f32)
    w_in0 = np.asarray(w_in, f32)[0]
    conv_w0 = np.asarray(conv_w, f32)[0]
    conv_b0 = np.asarray(conv_b, f32)[0]

    def pk(v):
        v = np.asarray(v, f32).reshape(-1, 128)
        return np.ascontiguousarray(v.T)

    def rep(v):
        v = np.asarray(v, f32).reshape(1, -1)
        return np.ascontiguousarray(np.broadcast_to(v, (128, v.shape[1])))

    ar = np.arange(128)
    cU = (ar[:, None] <= ar[None, :]).astype(f32)
    cG = (ar[:, None] > ar[None, :]).astype(f32)
    cI = np.eye(128, dtype=f32)
    shared = dict(
        mixw=pk(np.asarray(mix_norm_w, f32)[0]),
        lamv=np.ascontiguousarray(np.stack([rep(np.asarray(v, f32)[0]) for v in (lam_q1, lam_k1, lam_q2, lam_k2)], axis=1)),
        w_out=np.ascontiguousarray(np.asarray(w_out, f32)[0]),
        ffnwb=rep(np.asarray(ffn_norm_w, f32)[0]),
        w_gate=np.ascontiguousarray(np.asarray(w_gate, f32)[0]),
        w_up=np.ascontiguousarray(np.asarray(w_up, f32)[0]),
        w_down=np.ascontiguousarray(np.asarray(w_down, f32)[0]),
        finw=rep(np.asarray(final_norm_w, f32)),
        cU=cU, cG=cG, cI=cI,
    )
    percore_j = []
    for j in range(2):
        cols = np.concatenate([
            np.arange(512 * j, 512 * j + 512),
            np.arange(1024 + 512 * j, 1024 + 512 * j + 512),
            np.arange(2048 + 128 * j, 2048 + 128 * j + 128),
            np.arange(2304 + 128 * j, 2304 + 128 * j + 128),
            np.arange(2560 + 8 * j, 2560 + 8 * j + 8),
            np.arange(2576 + 512 * j, 2576 + 512 * j + 512),
            np.arange(3600 + 512 * j, 3600 + 512 * j + 512),
            np.arange(4624 + 512 * j, 4624 + 512 * j + 512),
        ])
        ch = np.concatenate([np.arange(512 * j, 512 * j + 512), np.arange(1024 + 128 * j, 1024 + 128 * j + 128),
                             np.arange(1280 + 128 * j, 1280 + 128 * j + 128)])
        cw = conv_w0[:, ch]
        convw = np.ascontiguousarray(cw.reshape(4, 6, 128).transpose(2, 1, 0))
        convb = np.ascontiguousarray(conv_b0[ch].reshape(6, 128).T)
        hsl = slice(8 * j, 8 * j + 8)
        ssdp = np.ascontiguousarray(np.stack([rep(np.asarray(dt_bias, f32)[0][hsl]), rep(np.asarray(a_log, f32)[0][hsl]),
                                              rep(np.asarray(d_skip, f32)[0][hsl])], axis=1))
        fl = np.zeros((128, 2), f32)
        fl[:, j] = 1.0
        ynw = np.ascontiguousarray(np.concatenate([pk(np.asarray(ssd_norm_w, f32)[0][512 * j:512 * j + 512]),
                                                   pk(np.tile(np.asarray(subln_w, f32)[0], 4))], axis=1))
        percore_j.append(dict(w_in=np.ascontiguousarray(w_in0[:, cols]), convw=convw, convb=convb, ssdp=ssdp, flags=fl, ynw=ynw))

    in_maps = []
    for c in range(8):
        b, j = c // 2, c % 2
        m = dict(shared)
        m.update(percore_j[j])
        m["xT"] = np.ascontiguousarray(x[b].T)
        m["xtok"] = np.ascontiguousarray(x[b, j * 2048:(j + 1) * 2048, :])
        in_maps.append(m)

    if _NC is None:
        _NC = build_program()
    res = run_bass_kernel_spmd(_NC, in_maps, core_ids=list(range(8)))
    outp = np.empty((4, T, D), f32)
    for c in range(8):
        b, j = c // 2, c % 2
        outp[b, j * 2048:(j + 1) * 2048, :] = res.results[c]["out"]
    if DEBUG:
        kernel.dbg = [dict(da=np.asarray(r["dbg_da"]), ssd=np.asarray(r["dbg_ssd"])) for r in res.results]
    return outp
```

```python
import bisect
from contextlib import ExitStack

import numpy as np
import ml_dtypes

import concourse.bass as bass
import concourse.mybir as mybir
from concourse.bass_utils import run_bass_kernel_spmd

F32 = mybir.dt.float32
BF16 = mybir.dt.bfloat16
AF = mybir.ActivationFunctionType
ALU = mybir.AluOpType
AX = mybir.AxisListType

ENGS = ("pe", "act", "dve", "pool", "sp")
EPS = 1e-5
T = 4096
D = 1024
DFF = 2816
NCOL = 2824
LAMBDA_INIT = 0.8 - 0.6 * 1.0
DEBUG = False
SCHED = True
SCHED_SEGS = (2,)
RUN_A = True
RUN_P2 = True
NT_B = 8
NT_A = 8


class Prog:
    def __init__(self, nc):
        self.nc = nc
        self.ops = []
        self.seg = 0

    DEF_COST = dict(pe=0.2, act=0.55, dve=0.7, pool=1.1, sp=0.15)

    def op(self, eng, fn, reads=(), writes=(), ps=(), dma=False, semkey=None, inc=1, c=None, lat=0.0):
        if c is None:
            c = self.DEF_COST[eng]
        self.ops.append(dict(eng=eng, fn=fn, reads=tuple(reads), writes=tuple(writes), ps=tuple(ps),
                             dma=dma, semkey=semkey, inc=inc, deps=set(), signal=False, c=c, lat=lat, seg=self.seg))
        return len(self.ops) - 1

    def dma(self, eng, out, in_, reads=(), writes=(), semkey=None, lat=2.5):
        return self.op(eng, lambda e: e.dma_start(out=out, in_=in_), reads, writes,
                       dma=True, semkey=semkey, inc=16, c=(0.15 if eng == "sp" else 1.0), lat=lat)

    def schedule(self, window=800, xlat=0.3):
        ops = self.ops
        n = len(ops)
        deps = [set() for _ in range(n)]
        oonly = [set() for _ in range(n)]
        last_w, readers, last_ps = {}, {}, {}
        for i, o in enumerate(ops):
            for k in o["reads"]:
                if k in last_w:
                    deps[i].add(last_w[k])
            for k in o["writes"]:
                if k in last_w:
                    deps[i].add(last_w[k])
                deps[i].update(readers.get(k, ()))
            for k in o["reads"]:
                readers.setdefault(k, []).append(i)
            for k in o["writes"]:
                last_w[k] = i
                readers[k] = []
            for b in o["ps"]:
                d = last_ps.setdefault(b, {})
                for e2, j in d.items():
                    if e2 != o["eng"]:
                        deps[i].add(j)
                    else:
                        oonly[i].add(j)
                d[o["eng"]] = i
            deps[i].discard(i)
            oonly[i].discard(i)
        alld = [deps[i] | oonly[i] for i in range(n)]
        succ = [[] for _ in range(n)]
        indeg = [0] * n
        for i in range(n):
            indeg[i] = len(alld[i])
            for d in alld[i]:
                succ[d].append(i)
        import heapq
        finish = [0.0] * n
        efree = {e: 0.0 for e in ENGS}
        ready = [i for i in range(n) if indeg[i] == 0]
        heapq.heapify(ready)
        done = [False] * n
        base = 0
        order = []
        while len(order) < n:
            while base < n and done[base]:
                base += 1
            lim = base + window
            best, bkey = None, None
            held = []
            while ready and ready[0] < lim:
                held.append(heapq.heappop(ready))
            cand = [i for i in held if (ops[i]["seg"] in SCHED_SEGS and ops[base]["seg"] in SCHED_SEGS and ops[i]["seg"] == ops[base]["seg"]) or i == base]
            for i in cand:
                o = ops[i]
                r = 0.0
                for d in alld[i]:
                    f = finish[d] + (xlat if ops[d]["eng"] != o["eng"] else 0.0)
                    if f > r:
                        r = f
                st = max(efree[o["eng"]], r)
                key = (st, i)
                if bkey is None or key < bkey:
                    best, bkey = i, key
            for i in held:
                if i != best:
                    heapq.heappush(ready, i)
            i = best
            o = ops[i]
            st = bkey[0]
            efree[o["eng"]] = st + o["c"]
            finish[i] = st + o["c"] + o["lat"]
            done[i] = True
            order.append(i)
            for s_ in succ[i]:
                indeg[s_] -= 1
                if indeg[s_] == 0:
                    heapq.heappush(ready, s_)
        pos = {old: new for new, old in enumerate(order)}
        self.ops = [ops[i] for i in order]
        for new, old in enumerate(order):
            self.ops[new]["deps"] = {pos[d] for d in deps[old]}
        self.model_time = max(finish)
        self.scheduled = True

    def analyze(self):
        ops = self.ops
        if not getattr(self, "scheduled", False):
            last_w, readers = {}, {}
            last_ps = {}
            for i, o in enumerate(ops):
                for k in o["reads"]:
                    if k in last_w:
                        o["deps"].add(last_w[k])
                for k in o["writes"]:
                    if k in last_w:
                        o["deps"].add(last_w[k])
                    for r in readers.get(k, ()):
                        o["deps"].add(r)
                for k in o["reads"]:
                    readers.setdefault(k, []).append(i)
                for k in o["writes"]:
                    last_w[k] = i
                    readers[k] = []
                for b in o["ps"]:
                    d = last_ps.setdefault(b, {})
                    for e2, j in d.items():
                        if e2 != o["eng"]:
                            o["deps"].add(j)
                    d[o["eng"]] = i
                o["deps"].discard(i)
        for i, o in enumerate(ops):
            if o["eng"] == "pe" and not o["dma"]:
                o["deps"] = {d for d in o["deps"] if not (ops[d]["eng"] == "pe" and ops[d]["semkey"] is None)}
            best = {}
            keep = set()
            for d in o["deps"]:
                if ops[d]["semkey"] is not None:
                    keep.add(d)
                else:
                    e2 = ops[d]["eng"]
                    if best.get(e2, -1) < d:
                        best[e2] = d
            keep.update(best.values())
            o["deps"] = keep
            for d in o["deps"]:
                ops[d]["signal"] = True
        cnt = {e: 0 for e in ENGS}
        dcnt = {}
        self.own_keys = []
        self._idx = {}
        for i, o in enumerate(ops):
            if o["semkey"] is not None:
                k = o["semkey"]
                if k not in dcnt:
                    dcnt[k] = 0
                    self.own_keys.append(k)
                dcnt[k] += o["inc"]
                o["sem"] = ("own", k)
                o["val"] = dcnt[k]
                self._idx.setdefault(k, []).append((i, dcnt[k]))
            elif o["signal"]:
                cnt[o["eng"]] += 1
                o["sem"] = ("eng", o["eng"])
                o["val"] = cnt[o["eng"]]
        self.final = dict(dcnt)

    def count_before(self, key, i):
        lst = self._idx[key]
        p = bisect.bisect_left(lst, (i, -1))
        return lst[p - 1][1] if p > 0 else 0

    def emit(self):
        nc = self.nc
        if SCHED:
            self.schedule()
        self.analyze()
        ops = self.ops
        with ExitStack() as es:
            sems = {}
            for e in ENGS:
                sems[("eng", e)] = es.enter_context(nc.semaphore("s_" + e))
            for n, k in enumerate(self.own_keys):
                sems[("own", k)] = es.enter_context(nc.semaphore("o%d" % n))
            block = es.enter_context(nc.Block())

            def run_engine(ename, eng):
                waited = {}
                for i, o in enumerate(ops):
                    if o["eng"] != ename:
                        continue
                    need = {}
                    for d in o["deps"]:
                        do = ops[d]
                        s, v = do["sem"], do["val"]
                        if do["semkey"] is not None:
                            v = max(v, self.count_before(do["semkey"], i))
                        if need.get(s, 0) < v:
                            need[s] = v
                    for s, v in need.items():
                        if waited.get(s, 0) >= v:
                            continue
                        eng.wait_ge(sems[s], v)
                        waited[s] = v
                    ins = o["fn"](eng)
                    if o["semkey"] is not None:
                        ins.then_inc(sems[o["sem"]], o["inc"])
                    elif o["signal"]:
                        ins.then_inc(sems[o["sem"]], 1)
                return waited

            @block.tensor
            def _(e):
                run_engine("pe", e)

            @block.scalar
            def _(e):
                run_engine("act", e)

            @block.vector
            def _(e):
                run_engine("dve", e)

            @block.gpsimd
            def _(e):
                run_engine("pool", e)

            @block.sync
            def _(e):
                w = run_engine("sp", e)
                for k, v in self.final.items():
                    if w.get(("own", k), 0) < v:
                        e.wait_ge(sems[("own", k)], v)


def build_program():
    nc = bass.Bass("TRN2", target_bir_lowering=False)

    def din(name, shape, dt=F32):
        return nc.dram_tensor(name, list(shape), dt, kind="ExternalInput").ap()

    xT = din("xT", [D, T])
    xtok = din("xtok", [2048, D])
    w_in = din("w_in", [D, NCOL])
    mixw = din("mixw", [128, 8])
    convw = din("convw", [128, 6, 4])
    convb = din("convb", [128, 6])
    ssdp = din("ssdp", [128, 3, 8])
    lamv = din("lamv", [128, 4, 64])
    w_out = din("w_out", [2048, D])
    ynw = din("ynw", [128, 8])
    ffnwb = din("ffnwb", [128, D])
    w_gate = din("w_gate", [D, DFF])
    w_up = din("w_up", [D, DFF])
    w_down = din("w_down", [DFF, D])
    finw = din("finw", [128, D])
    cU = din("cU", [128, 128])
    cG = din("cG", [128, 128])
    cI = din("cI", [128, 128])
    flags = din("flags", [128, 2])
    out = nc.dram_tensor("out", [2048, D], F32, kind="ExternalOutput").ap()
    if DEBUG:
        dbg_da = nc.dram_tensor("dbg_da", [512, T], BF16, kind="ExternalOutput").ap()
        dbg_ssd = nc.dram_tensor("dbg_ssd", [512, T], BF16, kind="ExternalOutput").ap()

    yda_in = [nc.dram_tensor("yda_in%d" % i, [512, 512], BF16) for i in range(8)]
    yda_out = [nc.dram_tensor("yda_out%d" % i, [1024, 512], BF16) for i in range(8)]
    yssd_in = [nc.dram_tensor("yssd_in%d" % i, [512, 512], BF16) for i in range(8)]
    yssd_out = [nc.dram_tensor("yssd_out%d" % i, [1024, 512], BF16) for i in range(8)]

    P = Prog(nc)
    with ExitStack() as es:
        MEMW = 53000
        mem = es.enter_context(nc.sbuf_tensor("mem", [128, MEMW], F32))

        class Alloc:
            def __init__(self, start, end):
                self.off, self.end = start, end

            def take(self, shape, dt=F32):
                n = int(np.prod(shape[1:]))
                nbytes = n * (2 if dt == BF16 else 4)
                nbytes = (nbytes + 3) // 4 * 4
                assert self.off % 4 == 0 and self.off + nbytes <= self.end, (self.off, nbytes, self.end, shape)
                a = mem[:, self.off // 4:(self.off + nbytes) // 4]
                self.off += nbytes
                if dt == BF16:
                    a = a.bitcast(BF16)[:, 0:n]
                if len(shape) == 3:
                    a = a.rearrange("p (a b) -> p a b", a=shape[1])
                elif len(shape) == 4:
                    a = a.rearrange("p (a b c) -> p a b c", a=shape[1], b=shape[2])
                return a

        psf = es.enter_context(nc.psum_tensor("psf", [128, 7, 512], F32))
        psb = es.enter_context(nc.psum_tensor("psb", [128, 1024], BF16))

        pers = Alloc(0, 1024)
        cst = Alloc(1024, 8192)
        WO_OFF, WG_OFF, STG_OFF, TAIL_OFF, MEM_END = 111440, 144208, 189264, 197456, 212000

        def sb(name, shape, dt=F32):
            return pers.take(shape, dt)

        def sc(name, shape, dt=F32):
            return cst.take(shape, dt)

        Ib = sb("Ib", [128, 128], BF16)
        ynw_s = sb("ynw_s", [128, 8]); flags_s = sb("flags_s", [128, 2])
        wda = sb("wda", [128, 1]); st2 = sb("st2", [128, 4]); neglam = sb("neglam", [128, 1])
        mixw_s = sb("mixw_s", [128, 8])
        Uf = sc("Uf", [128, 128]); Gf = sc("Gf", [128, 128]); If = sc("If", [128, 128])
        Ub = sc("Ub", [128, 128], BF16)
        onesb = sc("onesb", [128, 128], BF16); onesf = sc("onesf", [128, 128])
        convw_s = sc("convw_s", [128, 6, 4]); convb_s = sc("convb_s", [128, 6])
        ssdp_s = sc("ssdp_s", [128, 3, 8]); lamv_s = sc("lamv_s", [128, 4, 64])
        Abc = sc("Abc", [128, 8]); Dsk = sc("Dsk", [128, 8, 64])
        lam2 = sc("lam2", [128, 2]); lamt = sc("lamt", [128, 2, 64])

        small_loads = [(Uf, cU), (Gf, cG), (If, cI), (mixw_s, mixw), (convw_s, convw), (convb_s, convb),
                       (ssdp_s, ssdp), (lamv_s, lamv), (ynw_s, ynw), (flags_s, flags)]
        for n, (dst, src) in enumerate(small_loads):
            P.dma("sp", dst[:], src, writes=["c%d" % n], semkey="c%d" % n)
        CK = ["c%d" % n for n in range(len(small_loads))]
        P.op("dve", lambda e: e.tensor_copy(out=Ub[:], in_=Uf[:]), CK, ["Ub"])
        P.op("dve", lambda e: e.tensor_copy(out=Ib[:], in_=If[:]), CK, ["Ib"])
        P.op("pool", lambda e: e.memset(onesb[:], 1.0), [], ["onesb"])
        P.op("pool", lambda e: e.memset(onesf[:], 1.0), [], ["onesf"])
        P.op("act", lambda e: e.activation(out=Abc[:], in_=ssdp_s[:, 1, :], func=AF.Exp), CK, ["Abc"])
        P.op("dve", lambda e: e.tensor_scalar(out=Abc[:], in0=Abc[:], scalar1=-1.0, scalar2=None, op0=ALU.mult), ["Abc"], ["Abc"])
        P.op("dve", lambda e: e.tensor_copy(out=Dsk[:], in_=ssdp_s[:, 2, :].unsqueeze(2).to_broadcast([128, 8, 64])), CK, ["Dsk"])
        P.op("dve", lambda e: e.tensor_tensor(out=lamt[:, 0, :], in0=lamv_s[:, 0, :], in1=lamv_s[:, 1, :], op=ALU.mult), CK, ["lamt"])
        P.op("dve", lambda e: e.tensor_tensor(out=lamt[:, 1, :], in0=lamv_s[:, 2, :], in1=lamv_s[:, 3, :], op=ALU.mult), ["lamt"], ["lamt"])
        P.op("dve", lambda e: e.tensor_reduce(out=lam2[:], in_=lamt[:], axis=AX.X, op=ALU.add), ["lamt"], ["lam2"])
        P.op("act", lambda e: e.activation(out=lam2[:], in_=lam2[:], func=AF.Exp), ["lam2"], ["lam2"])
        P.op("dve", lambda e: e.tensor_tensor(out=neglam[:], in0=lam2[:, 1:2], in1=lam2[:, 0:1], op=ALU.subtract), ["lam2"], ["neglam"])
        P.op("dve", lambda e: e.tensor_scalar(out=neglam[:], in0=neglam[:], scalar1=-LAMBDA_INIT, scalar2=None, op0=ALU.add), ["neglam"], ["neglam"])

        shared = Alloc(8192, 8192 + 53248)
        Win = shared.take([128, 8, 1536], BF16)
        xTt = shared.take([128, 8, 512])
        nT = shared.take([128, 8, 512], BF16)
        lnv = shared.take([128, 512]); rstd = shared.take([128, 512])
        ARENA0 = shared.off
        Wo = Alloc(WO_OFF, WG_OFF).take([128, 16, 1024], BF16)
        Wg = Alloc(WG_OFF, STG_OFF).take([128, 8, DFF], BF16)
        sreg = Alloc(STG_OFF, TAIL_OFF)
        ffnwb_s = sreg.take([128, D])
        c0b_1 = sreg.take([128, 16, 128], BF16)
        tail = Alloc(TAIL_OFF, MEM_END)
        finw_s = tail.take([128, D])
        actT = tail.take([128, 22, 128], BF16)
        sg = tail.take([128, 2, 128])
        n2T_1 = tail.take([128, 8, 128], BF16)

        xT_v = xT.rearrange("(kt p) t -> p kt t", p=128)
        w_in_v = w_in.rearrange("(kt p) c -> p kt c", p=128)

        def load_win(c0, ncols, tag):
            for kt in range(8):
                P.dma("pool", Win[:, kt, 0:ncols], w_in_v[:, kt, c0:c0 + ncols], writes=["Win"], semkey="Win")

        def norm_part1(i):
            t0 = i * 512
            P.dma("sp", xTt[:, 0:4, :], xT_v[:, 0:4, t0:t0 + 512], writes=["xTt0"], semkey="xTt0")
            P.dma("sp", xTt[:, 4:8, :], xT_v[:, 4:8, t0:t0 + 512], writes=["xTt1"], semkey="xTt1")
            P.op("act", lambda e: e.activation(out=nT[:], in_=xTt[:], func=AF.Square), ["xTt0", "xTt1"], ["nT"], c=3.6)

        def norm_part2(i):
            for kt in range(8):
                P.op("pe", (lambda kt=kt: lambda e: e.matmul(psf[:, 6, :], lhsT=onesb[:], rhs=nT[:, kt, :], start=(kt == 0), stop=(kt == 7)))(),
                     ["nT", "onesb"], [], ps=[6])
            P.op("act", lambda e: e.activation(out=lnv[:], in_=psf[:, 6, :], func=AF.Ln, scale=1.0 / D, bias=EPS), [], ["lnv"], ps=[6])
            P.op("act", lambda e: e.activation(out=rstd[:], in_=lnv[:], func=AF.Exp, scale=-0.5), ["lnv"], ["rstd"])
            for kt in range(8):
                P.op("dve", (lambda kt=kt: lambda e: e.scalar_tensor_tensor(out=nT[:, kt, :], in0=xTt[:, kt, :], scalar=mixw_s[:, kt:kt + 1], in1=rstd[:], op0=ALU.mult, op1=ALU.mult))(),
                     ["xTt0", "xTt1", "rstd", "nT"] + CK, ["nT"])

        def norm_tile(i):
            norm_part1(i)
            norm_part2(i)

        evac_rr = [0]

        def evac(out_ap, in_ap, reads, writes, ps, eng=None):
            if eng is None:
                eng = ("act", "dve")[evac_rr[0] % 2]
                evac_rr[0] += 1
            if eng == "act":
                P.op("act", lambda e: e.copy(out=out_ap, in_=in_ap), reads, writes, ps=ps)
            else:
                P.op(eng, lambda e: e.tensor_copy(out=out_ap, in_=in_ap), reads, writes, ps=ps)

        def proj_fm(bank, c0, reads_extra=()):
            for kt in range(8):
                P.op("pe", (lambda kt=kt: lambda e: e.matmul(psf[:, bank, :], lhsT=Win[:, kt, c0:c0 + 128], rhs=nT[:, kt, :], start=(kt == 0), stop=(kt == 7)))(),
                     ["Win", "nT"], [], ps=[bank])

        def proj_tm(bank, s, c0, n, col0=0):
            for kt in range(8):
                P.op("pe", (lambda kt=kt: lambda e: e.matmul(psf[:, bank, col0:col0 + n], lhsT=nT[:, kt, s * 128:(s + 1) * 128], rhs=Win[:, kt, c0:c0 + n], start=(kt == 0), stop=(kt == 7)))(),
                     ["Win", "nT"], [], ps=[bank])

        w_out_v = w_out.rearrange("(kt p) c -> p kt c", p=128)
        w_gate_v = w_gate.rearrange("(kt p) c -> p kt c", p=128)
        w_up_v = w_up.rearrange("(kt p) c -> p kt c", p=128)
        w_down_v = w_down.rearrange("(kt p) c -> p kt c", p=128)
        P.op("dve", lambda e: e.tensor_scalar(out=wda[:], in0=ynw_s[:, 4:5], scalar1=1.0 - LAMBDA_INIT, scalar2=None, op0=ALU.mult), CK, ["wda"])

        def wdma(dst, src, key, extra_reads=()):
            P.dma("pool", dst, src, reads=list(extra_reads), writes=[key], semkey="w_" + key, lat=9.0)

        pre_jobs = []
        for kt in range(0, 16, 2):
            pre_jobs.append((lambda kt=kt: wdma(Wo[:, kt:kt + 2, :], w_out_v[:, kt:kt + 2, :], "Wo")))
        for kt in range(8):
            pre_jobs.append((lambda kt=kt: wdma(Wg[:, kt, :], w_gate_v[:, kt, :], "Wg")))

        def gather(kind, i):
            src = (yda_in if kind == "da" else yssd_in)[i]
            dst = (yda_out if kind == "da" else yssd_out)[i]
            P.op("pool", lambda e: e.collective_compute("AllGather", ALU.bypass, replica_groups=[[0, 1], [2, 3], [4, 5], [6, 7]],
                                                        ins=[src.ap().opt()], outs=[dst.ap().opt()]),
                 ["y%s_in%d" % (kind, i)], ["y%s_out%d" % (kind, i)], semkey="cc_%s%d" % (kind, i), inc=1, c=1.0, lat=40.0)

        P.seg = 1
        cv = Alloc(ARENA0, 150000)
        kT = cv.take([128, 4, T], BF16)
        Vaug = cv.take([128, 32, 4, 132], BF16)
        qT = cv.take([128, 4, 512], BF16)
        PT = [[cv.take([128, 512], BF16) for _ in range(2)] for _ in range(2)]
        O_sb = cv.take([128, 8, 130], F32)
        rcp = cv.take([128, 8], F32); rn = cv.take([128, 4], F32)
        o1 = cv.take([128, 4, 128], F32); o2 = cv.take([128, 4, 128], F32)
        ssq4 = cv.take([128, 4], F32); r4 = cv.take([128, 4], F32)
        on = cv.take([128, 4, 128], BF16)
        yTda = cv.take([128, 4, 512], BF16)

        load_win(1288, 1536, "B")
        P.op("pool", lambda e: e.memset(Vaug[:, :, :, 128:129], 1.0), [], ["Vaug"])

        pend = []

        def make_fin(i, h):
            def fin1():
                P.op("dve", lambda e: e.reciprocal(out=rcp[:], in_=O_sb[:, :, 128]), ["O_sb"], ["rcp"])
                P.op("dve", lambda e: e.tensor_scalar(out=rn[:], in0=rcp[:, 4:8], scalar1=neglam[:, 0:1], scalar2=None, op0=ALU.mult), ["rcp", "neglam"], ["rn"])
                P.op("dve", lambda e: e.tensor_tensor(out=o1[:], in0=O_sb[:, 0:4, 0:128], in1=rcp[:, 0:4].unsqueeze(2).to_broadcast([128, 4, 128]), op=ALU.mult), ["O_sb", "rcp"], ["o1"])
                P.op("pool", lambda e: e.tensor_tensor(out=o2[:], in0=O_sb[:, 4:8, 0:128], in1=rn[:].unsqueeze(2).to_broadcast([128, 4, 128]), op=ALU.mult), ["O_sb", "rn"], ["o2"])
                P.op("dve", lambda e: e.tensor_tensor(out=o1[:], in0=o1[:], in1=o2[:], op=ALU.add), ["o1", "o2"], ["o1"])
                P.op("pool", lambda e: e.tensor_tensor(out=o2[:], in0=o1[:], in1=o1[:], op=ALU.mult), ["o1", "o2"], ["o2"])
                P.op("dve", lambda e: e.tensor_reduce(out=ssq4[:], in_=o2[:], axis=AX.X, op=ALU.add), ["o2"], ["ssq4"])

            def fin2():
                P.op("act", lambda e: e.activation(out=r4[:], in_=ssq4[:], func=AF.Ln, scale=1.0 / 128, bias=EPS), ["ssq4"], ["r4"])
                P.op("act", lambda e: e.activation(out=r4[:], in_=r4[:], func=AF.Exp, scale=-0.5), ["r4"], ["r4"])
                P.op("dve", lambda e: e.tensor_tensor(out=on[:], in0=o1[:], in1=r4[:].unsqueeze(2).to_broadcast([128, 4, 128]), op=ALU.mult), ["o1", "r4"], ["on"])

            def fin3():
                for qs in range(4):
                    P.op("pe", (lambda qs=qs: lambda e: e.transpose(psb[:, qs * 128:(qs + 1) * 128], on[:, qs, :], Ib[:]))(), ["on", "Ib"], [], ps=[7], c=0.11)
                P.op("dve", lambda e: e.tensor_scalar(out=yTda[:, h, :], in0=psb[:, 0:512], scalar1=wda[:, 0:1], scalar2=None, op0=ALU.mult), ["wda"], ["yTda"], ps=[7])
                if h == 3:
                    P.dma("sp", yda_in[i].ap().rearrange("(h p) t -> p h t", p=128), yTda[:], reads=["yTda"], writes=["yda_in%d" % i], semkey="yda_st")
                    gather("da", i)
            return [fin1, fin2, fin3]

        def run_pending(stage):
            if pend and len(pend[0]) == 3 - stage:
                pend[0].pop(0)()
                if not pend[0]:
                    pend.pop(0)

        if NT_B:
            norm_tile(0)
        for i in range(NT_B):
            t0 = i * 512
            for h in range(4):
                bank = h % 4
                proj_fm(bank, h * 128)
                evac(qT[:, h, :], psf[:, bank, :], [], ["qT"], [bank], eng="dve")
            for h in range(4):
                bank = h % 4
                proj_fm(bank, 512 + h * 128)
                evac(kT[:, h, t0:t0 + 512], psf[:, bank, :], [], ["kT"], [bank], eng="dve")
            for s in range(4):
                bank = s % 4
                proj_tm(bank, s, 1024, 512)
                evac(Vaug[:, 4 * i + s, :, 0:128], psf[:, bank, :].rearrange("p (h v) -> p h v", h=4), [], ["Vaug"], [bank], eng="dve")
            nkb = 4 * i + 4
            for h in range(4):
                started = set()
                steps = list(range(nkb))

                def emit_S(st, kb, h=h, i=i):
                    qlo = max(0, kb - 4 * i) * 128
                    n = 512 - qlo
                    for m in range(2):
                        bank = 2 * (st % 2) + m
                        P.op("pe", (lambda m=m, bank=bank, kb=kb, qlo=qlo, n=n: lambda e: e.matmul(
                            psf[:, bank, 0:n], lhsT=kT[64 * m:64 * m + 64, h, kb * 128:(kb + 1) * 128],
                            rhs=qT[64 * m:64 * m + 64, h, qlo:512], start=True, stop=True))(),
                            ["kT", "qT"], [], ps=[bank], c=0.16)

                def emit_rest(st, kb, h=h, i=i, started=started):
                    qlo = max(0, kb - 4 * i) * 128
                    n = 512 - qlo
                    b = st % 2
                    for m in range(2):
                        bank = 2 * b + m
                        P.op("act", (lambda m=m, bank=bank, n=n, b=b: lambda e: e.activation(out=PT[m][b][:, 0:n], in_=psf[:, bank, 0:n], func=AF.Exp, scale=0.125))(),
                             [], ["PT%d%d" % (m, b)], ps=[bank], c=0.5)
                        if kb >= 4 * i:
                            P.op("pool", (lambda m=m, b=b: lambda e: e.tensor_tensor(out=PT[m][b][:, 0:128], in0=PT[m][b][:, 0:128], in1=Ub[:], op=ALU.mult))(),
                                 ["PT%d%d" % (m, b), "Ub"], ["PT%d%d" % (m, b)], c=0.45)
                    for m in range(2):
                        for qs in range(max(0, kb - 4 * i), 4):
                            a = m * 4 + qs
                            bank = 4 + a // 3
                            col = (a % 3) * 130
                            first = (kb == 0) and (bank not in started)
                            if kb == 0:
                                started.add(bank)
                            last = (kb == 4 * i + qs)
                            P.op("pe", (lambda m=m, b=b, qs=qs, qlo=qlo, bank=bank, col=col, first=first, last=last, kb=kb: lambda e: e.matmul(
                                psf[:, bank, col:col + 129], lhsT=PT[m][b][:, qs * 128 - qlo:qs * 128 - qlo + 128],
                                rhs=Vaug[:, kb, h, 0:129], start=first, stop=last, skip_group_check=True))(),
                                ["PT%d%d" % (m, b), "Vaug"], [], ps=[bank], c=0.06)

                emit_S(0, 0)
                for st, kb in enumerate(steps):
                    if st + 1 < nkb:
                        emit_S(st + 1, steps[st + 1])
                    emit_rest(st, kb)
                    if st == 0:
                        run_pending(0)
                    elif st == 1:
                        run_pending(1)
                    elif st == 3:
                        run_pending(2)
                    if h == 1 and st == 2 and i + 1 < NT_B:
                        norm_part1(i + 1)
                while pend:
                    run_pending(3 - len(pend[0]))
                for bank, na in ((4, 3), (5, 3), (6, 2)):
                    a0 = (bank - 4) * 3
                    P.op("act", (lambda bank=bank, na=na, a0=a0: lambda e: e.copy(
                        out=O_sb[:, a0:a0 + na, :], in_=psf[:, bank, 0:na * 130].rearrange("p (a c) -> p a c", a=na)))(),
                        [], ["O_sb"], ps=[bank])
                pend.append(make_fin(i, h))
                if h == 2 and i + 1 < NT_B:
                    norm_part2(i + 1)
        while pend:
            run_pending(3 - len(pend[0]))

        P.seg = 2
        cv = Alloc(ARENA0, WO_OFF)
        ta = Alloc(TAIL_OFF, MEM_END)
        ub = [cv.take([128, 516], F32) for _ in range(2)]
        cacc = [cv.take([128, 512], F32) for _ in range(2)]
        carry = cv.take([128, 6, 3], F32)
        xcs = [cv.take([128, 6, 512], BF16), ta.take([128, 6, 512], BF16)]
        zs = cv.take([128, 4, 512], F32)

        def dtset(al):
            d = {}
            for nm in ("dtb", "dtv", "adt", "acs", "eacs", "dd", "decst", "cdec", "dtdec", "adthf"):
                d[nm] = al.take([128, 32], F32)
            d["adth"] = al.take([128, 32], BF16)
            d["adtl"] = al.take([128, 32], BF16)
            return d
        dts = [dtset(cv), dtset(ta)]
        xtm = ta.take([128, 640], BF16)
        xdt = ta.take([128, 8, 64], BF16)
        xdd = ta.take([128, 8, 64], BF16)
        rhs_hi = cv.take([128, 8, 128], BF16)
        rhs_lo = cv.take([128, 8, 128], BF16)
        expD = cv.take([128, 8, 128], F32)
        CBm = cv.take([128, 128], F32)
        MT = cv.take([128, 8, 128], BF16)
        t1s = [cv.take([128, 8, 64], F32) for _ in range(2)]
        t2 = cv.take([128, 8, 64], F32)
        Sst = cv.take([128, 8, 64], F32); Sbf = cv.take([128, 512], BF16)
        ssq1 = cv.take([128, 2], F32)
        ygn = cv.take([128, 512], BF16)
        yTs = cv.take([128, 4, 512], BF16)
        Gb = cv.take([128, 128], BF16)
        PB_KEYS = ["kT", "Vaug", "qT", "PT00", "PT01", "PT10", "PT11", "O_sb", "rcp", "rn", "o1", "o2", "ssq4", "r4", "on", "yTda"]
        P.op("pool", lambda e: e.memset(carry[:], 0.0), PB_KEYS, PB_KEYS + ["carry", "Wo", "Wg"])
        P.op("pool", lambda e: e.memset(Sst[:], 0.0), ["carry"], ["Sst"])
        P.op("pool", lambda e: e.memset(Sbf[:], 0.0), ["carry"], ["Sbf"])
        P.op("dve", lambda e: e.tensor_copy(out=Gb[:], in_=Gf[:]), ["carry"] + CK, ["Gb"])
        FENCE = ["carry"]

        P.op("pool", lambda e: e.memset(Win[:, :, 1280:1296], 0.0), ["Win"], ["Win"])
        load_win(0, 1288, "A")
        ZC, XC, DTC = 0, 512, 1280
        NTA = NT_A if RUN_A else 0

        def prologue_main(i):
            par = i % 2
            xc = xcs[par]
            d = dts[par]
            sx = "%d" % par
            for _ in range(2):
                if pre_jobs:
                    pre_jobs.pop(0)()
            for c in range(6):
                bank = c % 2
                u = ub[c % 2]
                acc = cacc[c % 2]
                ceng = "dve" if c % 2 == 0 else "pool"
                proj_fm(bank, XC + c * 128)
                P.op("pool", (lambda u=u, c=c: lambda e: e.tensor_copy(out=u[:, 0:3], in_=carry[:, c, :]))(), ["carry"] + FENCE, ["ub%d" % (c % 2)])
                P.op("act", (lambda u=u, bank=bank: lambda e: e.copy(out=u[:, 3:515], in_=psf[:, bank, :]))(), [], ["ub%d" % (c % 2)], ps=[bank])
                P.op("pool", (lambda u=u, c=c: lambda e: e.tensor_copy(out=carry[:, c, :], in_=u[:, 512:515]))(), ["ub%d" % (c % 2)], ["carry"])
                if ceng == "dve":
                    P.op(ceng, (lambda u=u, acc=acc, c=c: lambda e: e.tensor_scalar(out=acc[:], in0=u[:, 3:515], scalar1=convw_s[:, c, 3:4], scalar2=convb_s[:, c:c + 1], op0=ALU.mult, op1=ALU.add))(),
                         ["ub%d" % (c % 2)] + CK, ["cacc%d" % (c % 2)])
                else:
                    P.op(ceng, (lambda u=u, acc=acc, c=c: lambda e: e.tensor_tensor(out=acc[:], in0=u[:, 3:515], in1=convw_s[:, c, 3:4].to_broadcast([128, 512]), op=ALU.mult))(),
                         ["ub%d" % (c % 2)] + CK, ["cacc%d" % (c % 2)])
                    P.op(ceng, (lambda acc=acc, c=c: lambda e: e.tensor_tensor(out=acc[:], in0=acc[:], in1=convb_s[:, c:c + 1].to_broadcast([128, 512]), op=ALU.add))(),
                         ["cacc%d" % (c % 2)] + CK, ["cacc%d" % (c % 2)])
                for jt in (2, 1, 0):
                    if ceng == "dve":
                        P.op(ceng, (lambda u=u, acc=acc, c=c, jt=jt: lambda e: e.scalar_tensor_tensor(out=acc[:], in0=u[:, jt:jt + 512], scalar=convw_s[:, c, jt:jt + 1], in1=acc[:], op0=ALU.mult, op1=ALU.add))(),
                             ["ub%d" % (c % 2), "cacc%d" % (c % 2)], ["cacc%d" % (c % 2)])
                    else:
                        t2f = t2[:].rearrange("p h d -> p (h d)")
                        P.op(ceng, (lambda u=u, c=c, jt=jt, t2f=t2f: lambda e: e.tensor_tensor(out=t2f, in0=u[:, jt:jt + 512], in1=convw_s[:, c, jt:jt + 1].to_broadcast([128, 512]), op=ALU.mult))(),
                             ["ub%d" % (c % 2), "t2"], ["t2"])
                        P.op(ceng, (lambda acc=acc, t2f=t2f: lambda e: e.tensor_tensor(out=acc[:], in0=acc[:], in1=t2f, op=ALU.add))(),
                             ["t2", "cacc%d" % (c % 2)], ["cacc%d" % (c % 2)])
                P.op("act", (lambda acc=acc, c=c, xc=xc: lambda e: e.activation(out=xc[:, c, :], in_=acc[:], func=AF.Silu))(), ["cacc%d" % (c % 2)] + FENCE, ["xc" + sx])
            for s_ in range(4):
                proj_tm(6, s_, DTC, 8, col0=s_ * 8)
            v4 = lambda t_: t_[:].rearrange("p (c h) -> p c h", c=4)
            P.op("dve", lambda e: e.tensor_tensor(out=v4(d["dtb"]), in0=psf[:, 6, 0:32].rearrange("p (c h) -> p c h", c=4),
                                                  in1=ssdp_s[:, 0, :].unsqueeze(1).to_broadcast([128, 4, 8]), op=ALU.add), CK + FENCE, ["dtb" + sx], ps=[6])
            P.op("act", lambda e: e.activation(out=d["dtb"][:], in_=d["dtb"][:], func=AF.Exp), ["dtb" + sx], ["dtb" + sx])
            P.op("act", lambda e: e.activation(out=d["dtv"][:], in_=d["dtb"][:], func=AF.Ln, bias=1.0), ["dtb" + sx], ["dtv" + sx])
            P.op("dve", lambda e: e.tensor_tensor(out=v4(d["adt"]), in0=v4(d["dtv"]), in1=Abc[:].unsqueeze(1).to_broadcast([128, 4, 8]), op=ALU.mult), ["dtv" + sx, "Abc"], ["adt" + sx])
            P.op("pe", lambda e: e.matmul(psf[:, 6, 32:64], lhsT=Uf[:], rhs=d["adt"][:], start=True, stop=True), ["adt" + sx] + CK, [], ps=[6])
            P.op("pe", lambda e: e.matmul(psf[:, 6, 64:96], lhsT=onesf[:], rhs=d["adt"][:], start=False, stop=True, skip_group_check=True), ["adt" + sx, "onesf"], [], ps=[6])
            P.op("act", lambda e: e.copy(out=d["acs"][:], in_=psf[:, 6, 32:64]), [], ["acs" + sx], ps=[6])
            P.op("act", lambda e: e.activation(out=d["eacs"][:], in_=psf[:, 6, 32:64], func=AF.Exp), [], ["eacs" + sx], ps=[6])
            P.op("act", lambda e: e.activation(out=d["cdec"][:], in_=psf[:, 6, 64:96], func=AF.Exp), [], ["cdec" + sx], ps=[6])
            P.op("dve", lambda e: e.tensor_tensor(out=d["dd"][:], in0=psf[:, 6, 64:96], in1=d["acs"][:], op=ALU.subtract), ["acs" + sx], ["dd" + sx], ps=[6])
            P.op("act", lambda e: e.activation(out=d["decst"][:], in_=d["dd"][:], func=AF.Exp), ["dd" + sx], ["decst" + sx])
            P.op("dve", lambda e: e.tensor_tensor(out=d["dtdec"][:], in0=d["dtv"][:], in1=d["decst"][:], op=ALU.mult), ["dtv" + sx, "decst" + sx], ["dtdec" + sx])
            P.op("dve", lambda e: e.tensor_copy(out=d["adth"][:], in_=d["adt"][:]), ["adt" + sx], ["adth" + sx])
            P.op("dve", lambda e: e.tensor_copy(out=d["adthf"][:], in_=d["adth"][:]), ["adth" + sx], ["adthf" + sx])
            P.op("dve", lambda e: e.tensor_tensor(out=d["adtl"][:], in0=d["adt"][:], in1=d["adthf"][:], op=ALU.subtract), ["adt" + sx, "adthf" + sx], ["adtl" + sx])

        def zproj(i):
            for s_ in range(4):
                bank = s_ % 2
                proj_tm(bank, s_, ZC, 512)
                P.op("act", (lambda s_=s_, bank=bank: lambda e: e.activation(out=zs[:, s_, :], in_=psf[:, bank, :], func=AF.Silu))(), FENCE, ["zs%d" % s_], ps=[bank])

        def stageF(i, c):
            par = i % 2
            xc = xcs[par]
            d = dts[par]
            sx = "%d" % par
            t1 = t1s[c % 2]
            t1k = "t1%d" % (c % 2)
            cs = slice(c * 128, (c + 1) * 128)
            hs = slice(c * 8, (c + 1) * 8)
            for ct in range(5):
                P.op("pe", (lambda ct=ct: lambda e: e.transpose(psb[:, ct * 128:(ct + 1) * 128], xc[:, ct, cs], Ib[:]))(), ["xc" + sx, "Ib"], [], ps=[7], c=0.11)
            P.op("act", lambda e: e.copy(out=xtm[:], in_=psb[:, 0:640]), [], ["xtm"], ps=[7])
            P.op("dve", lambda e: e.tensor_tensor(out=rhs_hi[:], in0=Ub[:].unsqueeze(1).to_broadcast([128, 8, 128]),
                                                  in1=d["adth"][:, hs].unsqueeze(2).to_broadcast([128, 8, 128]), op=ALU.mult), ["adth" + sx, "Ub"], ["rhs_hi"])
            P.op("pool", lambda e: e.tensor_tensor(out=rhs_lo[:], in0=Ub[:].unsqueeze(1).to_broadcast([128, 8, 128]),
                                                   in1=d["adtl"][:, hs].unsqueeze(2).to_broadcast([128, 8, 128]), op=ALU.mult), ["adtl" + sx, "Ub"], ["rhs_lo"])
            for hh in range(2):
                P.op("pe", (lambda hh=hh: lambda e: e.matmul(psf[:, 2 + hh, :], lhsT=Gb[:], rhs=rhs_hi[:, 4 * hh:4 * hh + 4, :], start=True, stop=False))(), ["rhs_hi", "Gb"], [], ps=[2 + hh])
                P.op("pe", (lambda hh=hh: lambda e: e.matmul(psf[:, 2 + hh, :], lhsT=Gb[:], rhs=rhs_lo[:, 4 * hh:4 * hh + 4, :], start=False, stop=True))(), ["rhs_lo", "Gb"], [], ps=[2 + hh])
            P.op("pe", lambda e: e.matmul(psf[:, 4, 0:128], lhsT=xc[:, 4, cs], rhs=xc[:, 5, cs], start=True, stop=True), ["xc" + sx], [], ps=[4])
            P.op("dve", lambda e: e.tensor_tensor(out=CBm[:], in0=psf[:, 4, 0:128], in1=Uf[:], op=ALU.mult), CK, ["CBm"], ps=[4])
            for hh in range(2):
                P.op("act", (lambda hh=hh: lambda e: e.activation(out=expD[:, 4 * hh:4 * hh + 4, :], in_=psf[:, 2 + hh, :].rearrange("p (h l) -> p h l", h=4), func=AF.Exp))(),
                     [], ["expD"], ps=[2 + hh])
            P.op("dve", lambda e: e.tensor_tensor(out=MT[:], in0=expD[:], in1=CBm[:].unsqueeze(1).to_broadcast([128, 8, 128]), op=ALU.mult), ["expD", "CBm"], ["MT"])
            xv = xtm[:, 0:512].rearrange("p (h d) -> p h d", h=8)
            P.op("pool", lambda e: e.tensor_tensor(out=xdt[:], in0=xv, in1=d["dtv"][:, hs].unsqueeze(2).to_broadcast([128, 8, 64]), op=ALU.mult), ["xtm", "dtv" + sx], ["xdt"])
            P.op("pool", lambda e: e.tensor_tensor(out=xdd[:], in0=xv, in1=d["dtdec"][:, hs].unsqueeze(2).to_broadcast([128, 8, 64]), op=ALU.mult), ["xtm", "dtdec" + sx], ["xdd"])
            P.op("pool", lambda e: e.tensor_tensor(out=t2[:], in0=xv, in1=Dsk[:], op=ALU.mult), ["xtm", "Dsk"], ["t2"])
            for h in range(8):
                P.op("pe", (lambda h=h: lambda e: e.matmul(psf[:, 5, h * 64:(h + 1) * 64], lhsT=MT[:, h, :], rhs=xdt[:, h, :], start=(h == 0), stop=True, skip_group_check=True))(),
                     ["MT", "xdt"], [], ps=[5], c=0.06)
            P.op("pe", lambda e: e.matmul(psf[:, 0, :], lhsT=xc[:, 5, cs], rhs=Sbf[:], start=True, stop=True), ["xc" + sx, "Sbf"], [], ps=[0])
            P.op("pe", lambda e: e.matmul(psf[:, 1, :], lhsT=xtm[:, 512:640], rhs=xdd[:].rearrange("p h d -> p (h d)"), start=True, stop=True), ["xtm", "xdd"], [], ps=[1])
            P.op("dve", lambda e: e.tensor_tensor(out=Sst[:], in0=Sst[:], in1=d["cdec"][:, hs].unsqueeze(2).to_broadcast([128, 8, 64]), op=ALU.mult), ["Sst", "cdec" + sx], ["Sst"])
            P.op("dve", lambda e: e.tensor_tensor(out=Sst[:], in0=Sst[:], in1=psf[:, 1, :].rearrange("p (h d) -> p h d", h=8), op=ALU.add), ["Sst"], ["Sst"], ps=[1])
            P.op("act", lambda e: e.copy(out=Sbf[:], in_=Sst[:].rearrange("p h d -> p (h d)")), ["Sst"], ["Sbf"])
            P.op("dve", lambda e: e.tensor_tensor(out=t1[:], in0=psf[:, 0, :].rearrange("p (h d) -> p h d", h=8),
                                                  in1=d["eacs"][:, hs].unsqueeze(2).to_broadcast([128, 8, 64]), op=ALU.mult), ["eacs" + sx], [t1k], ps=[0])
            P.op("dve", lambda e: e.tensor_tensor(out=t1[:], in0=t1[:], in1=psf[:, 5, :].rearrange("p (h d) -> p h d", h=8), op=ALU.add), [t1k], [t1k], ps=[5])
            P.op("dve", lambda e: e.tensor_tensor(out=t1[:], in0=t1[:], in1=t2[:], op=ALU.add), [t1k, "t2"], [t1k])

        def stageK(i, c):
            t1 = t1s[c % 2]
            t1k = "t1%d" % (c % 2)
            cs = slice(c * 128, (c + 1) * 128)
            P.op("dve", lambda e: e.tensor_tensor(out=t1[:], in0=t1[:], in1=zs[:, c, :].rearrange("p (h d) -> p h d", h=8), op=ALU.mult), [t1k, "zs%d" % c], [t1k])
            P.op("act", lambda e: e.activation(out=ygn[:], in_=t1[:].rearrange("p h d -> p (h d)"), func=AF.Square, accum_out=ssq1[:, 0:1]), [t1k, "ygn"], ["ygn", "ssq1"])
            P.op("act", lambda e: e.activation(out=ssq1[:, 1:2], in_=ssq1[:, 0:1], func=AF.Ln, scale=1.0 / 512, bias=EPS), ["ssq1"], ["ssq1"])
            P.op("act", lambda e: e.activation(out=ssq1[:, 1:2], in_=ssq1[:, 1:2], func=AF.Exp, scale=-0.5), ["ssq1"], ["ssq1"])
            P.op("dve", lambda e: e.tensor_scalar(out=ygn[:], in0=t1[:].rearrange("p h d -> p (h d)"), scalar1=ssq1[:, 1:2], scalar2=None, op0=ALU.mult), [t1k, "ssq1", "ygn"], ["ygn"])
            for ft in range(4):
                P.op("pe", (lambda ft=ft: lambda e: e.transpose(psb[:, 512 + ft * 128:512 + (ft + 1) * 128], ygn[:, ft * 128:(ft + 1) * 128], Ib[:]))(), ["ygn", "Ib"], [], ps=[7], c=0.11)
            P.op("dve", lambda e: e.tensor_tensor(out=yTs[:, :, cs], in0=psb[:, 512:1024].rearrange("p (f t) -> p f t", f=4),
                                                  in1=ynw_s[:, 0:4].unsqueeze(2).to_broadcast([128, 4, 128]), op=ALU.mult), CK, ["yTs"], ps=[7])
            if c == 3:
                P.dma("sp", yssd_in[i].ap().rearrange("(f p) t -> p f t", p=128), yTs[:], reads=["yTs"], writes=["yssd_in%d" % i], semkey="yssd_st")
                gather("ssd", i)

        if NTA:
            norm_tile(0)
            prologue_main(0)
            zproj(0)
        for i in range(NTA):
            nxt = i + 1 < NTA
            stageF(i, 0)
            stageF(i, 1)
            stageK(i, 0)
            if nxt:
                norm_tile(i + 1)
            stageF(i, 2)
            stageK(i, 1)
            stageF(i, 3)
            stageK(i, 2)
            if nxt:
                prologue_main(i + 1)
            stageK(i, 3)
            if nxt:
                zproj(i + 1)

        while pre_jobs and RUN_P2:
            pre_jobs.pop(0)()

        if DEBUG:
            for i in range(NT_B):
                P.dma("sp", dbg_da[:, i * 512:(i + 1) * 512], yda_in[i].ap(), reads=["yda_in%d" % i], semkey="dbg1")
            for i in range(NT_A if RUN_A else 0):
                P.dma("sp", dbg_ssd[:, i * 512:(i + 1) * 512], yssd_in[i].ap(), reads=["yssd_in%d" % i], semkey="dbg2")

        PA_KEYS = (["ub0", "ub1", "cacc0", "cacc1", "carry", "xc0", "xc1", "zs0", "zs1", "zs2", "zs3", "xtm", "rhs_hi", "rhs_lo", "expD", "CBm", "MT",
                    "xdt", "xdd", "t10", "t11", "t2", "Sst", "Sbf", "ssq1", "ygn", "yTs", "Gb"]
                   + [nm + sx for nm in ("dtb", "dtv", "adt", "acs", "eacs", "dd", "decst", "cdec", "dtdec", "adthf", "adth", "adtl") for sx in ("0", "1")])
        P.seg = 3
        cv = Alloc(1024, WO_OFF)
        Wu = cv.take([128, 8, DFF], BF16)
        Wd = cv.take([128, 22, 1024], BF16)
        c0bs = [cv.take([128, 16, 128], BF16), c0b_1]
        c1h = cv.take([128, 8, 128], BF16)
        h1 = [cv.take([128, 1024], F32) for _ in range(2)]
        n2 = cv.take([128, 1024], BF16)
        n2Ts = [cv.take([128, 8, 128], BF16), n2T_1]
        P.op("pool", lambda e: e.memset(st2[:], 0.0), PA_KEYS + CK + ["Win", "nT", "xTt0", "xTt1", "lnv", "rstd", "Ub", "onesb", "onesf", "Abc", "Dsk", "lamt", "lam2"],
             PA_KEYS + ["Wd", "Wu", "p2fence", "st2", "finw", "actT", "sg0", "sg1", "ffnwb", "c0b0", "c0b1", "c1h", "h10", "h11", "n2", "n2T0", "n2T1"])
        F2 = ["p2fence"]
        NT2 = 16 if RUN_P2 else 0
        if NT2:
            P.dma("sp", finw_s[:], finw, reads=F2, writes=["finw"], semkey="finw")
            P.dma("sp", ffnwb_s[:], ffnwb, reads=F2, writes=["ffnwb"], semkey="ffnwb")
            for kt in range(8):
                wdma(Wu[:, kt, :], w_up_v[:, kt, :], "Wu", F2)
            for kt in range(0, 22, 2):
                wdma(Wd[:, kt:kt + 2, :], w_down_v[:, kt:kt + 2, :], "Wd", F2)

        ys_v = [t_.ap().rearrange("(kt p) t -> p kt t", p=128) for t_ in yssd_out]
        yd_v = [t_.ap().rearrange("(kt p) t -> p kt t", p=128) for t_ in yda_out]

        def stA(tt):
            T0 = tt * 128
            hb, hk = h1[tt % 2], "h1%d" % (tt % 2)
            c0b, ck0 = c0bs[tt % 2], "c0b%d" % (tt % 2)
            n2T, nk = n2Ts[tt % 2], "n2T%d" % (tt % 2)
            P.dma("sp", hb[:], xtok[T0:T0 + 128, :], reads=F2, writes=[hk], semkey=hk)
            ti, tc = T0 // 512, T0 % 512
            for half, (src, srckey) in enumerate(((ys_v, "yssd_out"), (yd_v, "yda_out"))):
                P.dma("sp", c0b[:, 8 * half:8 * half + 8, :], src[ti][:, :, tc:tc + 128], reads=[srckey + str(ti)] + F2, writes=[ck0], semkey=ck0)
                P.dma("sp", c1h[:], src[4 + ti][:, :, tc:tc + 128], reads=[srckey + str(4 + ti)] + F2, writes=["c1h"], semkey="c1h")
                P.op("dve", lambda e: e.tensor_scalar(out=c1h[:], in0=c1h[:], scalar1=flags_s[:, 1:2], scalar2=None, op0=ALU.mult), ["c1h"], ["c1h"])
                P.op("dve", (lambda half=half: lambda e: e.scalar_tensor_tensor(out=c0b[:, 8 * half:8 * half + 8, :], in0=c0b[:, 8 * half:8 * half + 8, :], scalar=flags_s[:, 0:1], in1=c1h[:], op0=ALU.mult, op1=ALU.add))(),
                     [ck0, "c1h"], [ck0])
            for dh in range(2):
                bank = dh
                for kt in range(16):
                    P.op("pe", (lambda kt=kt, dh=dh, bank=bank: lambda e: e.matmul(psf[:, bank, :], lhsT=c0b[:, kt, :], rhs=Wo[:, kt, dh * 512:(dh + 1) * 512], start=(kt == 0), stop=(kt == 15)))(),
                         [ck0, "Wo"], [], ps=[bank])
                P.op("dve", (lambda dh=dh, bank=bank: lambda e: e.tensor_tensor(out=hb[:, dh * 512:(dh + 1) * 512], in0=psf[:, bank, :], in1=hb[:, dh * 512:(dh + 1) * 512], op=ALU.add))(),
                     [hk], [hk], ps=[bank])
            P.op("act", lambda e: e.activation(out=n2[:], in_=hb[:], func=AF.Square, accum_out=st2[:, 0:1]), [hk, "n2"], ["n2", "st2a"])
            P.op("act", lambda e: e.activation(out=st2[:, 1:2], in_=st2[:, 0:1], func=AF.Ln, scale=1.0 / D, bias=EPS), ["st2a"], ["st2a"])
            P.op("act", lambda e: e.activation(out=st2[:, 1:2], in_=st2[:, 1:2], func=AF.Exp, scale=-0.5), ["st2a"], ["st2a"])
            P.op("dve", lambda e: e.scalar_tensor_tensor(out=n2[:], in0=hb[:], scalar=st2[:, 1:2], in1=ffnwb_s[:], op0=ALU.mult, op1=ALU.mult), [hk, "st2a", "n2", "ffnwb"], ["n2"])
            for kt in range(8):
                P.op("pe", (lambda kt=kt: lambda e: e.transpose(psb[:, kt * 128:(kt + 1) * 128], n2[:, kt * 128:(kt + 1) * 128], Ib[:]))(), ["n2", "Ib"], [], ps=[7], c=0.11)
            P.op("act", lambda e: e.copy(out=n2T[:], in_=psb[:].rearrange("p (k t) -> p k t", k=8)), F2, [nk], ps=[7])

        def stB(tt):
            n2T, nk = n2Ts[tt % 2], "n2T%d" % (tt % 2)
            for f in range(22):
                gb = 2 + (f % 2)
                ubk = 4 + (f % 2)
                for kt in range(8):
                    P.op("pe", (lambda kt=kt, f=f, gb=gb: lambda e: e.matmul(psf[:, gb, 0:128], lhsT=Wg[:, kt, f * 128:(f + 1) * 128], rhs=n2T[:, kt, :], start=(kt == 0), stop=(kt == 7)))(),
                         ["Wg", nk], [], ps=[gb], c=0.06)
                for kt in range(8):
                    P.op("pe", (lambda kt=kt, f=f, ubk=ubk: lambda e: e.matmul(psf[:, ubk, 0:128], lhsT=Wu[:, kt, f * 128:(f + 1) * 128], rhs=n2T[:, kt, :], start=(kt == 0), stop=(kt == 7)))(),
                         ["Wu", nk], [], ps=[ubk], c=0.06)
                P.op("act", (lambda f=f, gb=gb: lambda e: e.activation(out=sg[:, f % 2, :], in_=psf[:, gb, 0:128], func=AF.Silu))(), F2, ["sg%d" % (f % 2)], ps=[gb], c=0.28)
                P.op("dve", (lambda f=f, ubk=ubk: lambda e: e.tensor_tensor(out=actT[:, f, :], in0=sg[:, f % 2, :], in1=psf[:, ubk, 0:128], op=ALU.mult))(), ["sg%d" % (f % 2)] + F2, ["actT"], ps=[ubk], c=0.29)

        def stC(tt):
            T0 = tt * 128
            hb, hk = h1[tt % 2], "h1%d" % (tt % 2)
            for dh in range(2):
                bank = dh
                for f in range(22):
                    P.op("pe", (lambda f=f, dh=dh, bank=bank: lambda e: e.matmul(psf[:, bank, :], lhsT=actT[:, f, :], rhs=Wd[:, f, dh * 512:(dh + 1) * 512], start=(f == 0), stop=(f == 21)))(),
                         ["actT", "Wd"], [], ps=[bank])
                P.op("dve", (lambda dh=dh, bank=bank: lambda e: e.tensor_tensor(out=hb[:, dh * 512:(dh + 1) * 512], in0=psf[:, bank, :], in1=hb[:, dh * 512:(dh + 1) * 512], op=ALU.add))(),
                     [hk], [hk], ps=[bank])
            P.op("act", lambda e: e.activation(out=actT[:, 0:8, :].rearrange("p a b -> p (a b)"), in_=hb[:], func=AF.Square, accum_out=st2[:, 2:3]), [hk, "actT"], ["actT", "st2b"])
            P.op("act", lambda e: e.activation(out=st2[:, 3:4], in_=st2[:, 2:3], func=AF.Ln, scale=1.0 / D, bias=EPS), ["st2b"], ["st2b"])
            P.op("act", lambda e: e.activation(out=st2[:, 3:4], in_=st2[:, 3:4], func=AF.Exp, scale=-0.5), ["st2b"], ["st2b"])
            P.op("dve", lambda e: e.scalar_tensor_tensor(out=hb[:], in0=hb[:], scalar=st2[:, 3:4], in1=finw_s[:], op0=ALU.mult, op1=ALU.mult), [hk, "st2b", "finw"], [hk])
            P.dma("sp", out[T0:T0 + 128, :], hb[:], reads=[hk], writes=["out"], semkey="ost%d" % (tt % 2))

        if NT2:
            stA(0)
        for tt in range(NT2):
            stB(tt)
            if tt + 1 < NT2:
                stA(tt + 1)
            stC(tt)

        P.emit()
    return nc


_NC = None


def kernel(x, mix_norm_w, w_in, conv_w, conv_b, dt_bias, a_log, d_skip, ssd_norm_w,
           lam_q1, lam_k1, lam_q2, lam_k2, subln_w, w_out, ffn_norm_w, w_gate, w_up, w_down,
           final_norm_w):
    global _NC
    f32 = np.float32
    x = np.asarray(x, f32)
    w_in0 = np.asarray(w_in, f32)[0]
    conv_w0 = np.asarray(conv_w, f32)[0]
    conv_b0 = np.asarray(conv_b, f32)[0]

    def pk(v):
        v = np.asarray(v, f32).reshape(-1, 128)
        return np.ascontiguousarray(v.T)

    def rep(v):
        v = np.asarray(v, f32).reshape(1, -1)
        return np.ascontiguousarray(np.broadcast_to(v, (128, v.shape[1])))

    ar = np.arange(128)
    cU = (ar[:, None] <= ar[None, :]).astype(f32)
    cG = (ar[:, None] > ar[None, :]).astype(f32)
    cI = np.eye(128, dtype=f32)
    shared = dict(
        mixw=pk(np.asarray(mix_norm_w, f32)[0]),
        lamv=np.ascontiguousarray(np.stack([rep(np.asarray(v, f32)[0]) for v in (lam_q1, lam_k1, lam_q2, lam_k2)], axis=1)),
        w_out=np.ascontiguousarray(np.asarray(w_out, f32)[0]),
        ffnwb=rep(np.asarray(ffn_norm_w, f32)[0]),
        w_gate=np.ascontiguousarray(np.asarray(w_gate, f32)[0]),
        w_up=np.ascontiguousarray(np.asarray(w_up, f32)[0]),
        w_down=np.ascontiguousarray(np.asarray(w_down, f32)[0]),
        finw=rep(np.asarray(final_norm_w, f32)),
        cU=cU, cG=cG, cI=cI,
    )
    percore_j = []
    for j in range(2):
        cols = np.concatenate([
            np.arange(512 * j, 512 * j + 512),
            np.arange(1024 + 512 * j, 1024 + 512 * j + 512),
            np.arange(2048 + 128 * j, 2048 + 128 * j + 128),
            np.arange(2304 + 128 * j, 2304 + 128 * j + 128),
            np.arange(2560 + 8 * j, 2560 + 8 * j + 8),
            np.arange(2576 + 512 * j, 2576 + 512 * j + 512),
            np.arange(3600 + 512 * j, 3600 + 512 * j + 512),
            np.arange(4624 + 512 * j, 4624 + 512 * j + 512),
        ])
        ch = np.concatenate([np.arange(512 * j, 512 * j + 512), np.arange(1024 + 128 * j, 1024 + 128 * j + 128),
                             np.arange(1280 + 128 * j, 1280 + 128 * j + 128)])
        cw = conv_w0[:, ch]
        convw = np.ascontiguousarray(cw.reshape(4, 6, 128).transpose(2, 1, 0))
        convb = np.ascontiguousarray(conv_b0[ch].reshape(6, 128).T)
        hsl = slice(8 * j, 8 * j + 8)
        ssdp = np.ascontiguousarray(np.stack([rep(np.asarray(dt_bias, f32)[0][hsl]), rep(np.asarray(a_log, f32)[0][hsl]),
                                              rep(np.asarray(d_skip, f32)[0][hsl])], axis=1))
        fl = np.zeros((128, 2), f32)
        fl[:, j] = 1.0
        ynw = np.ascontiguousarray(np.concatenate([pk(np.asarray(ssd_norm_w, f32)[0][512 * j:512 * j + 512]),
                                                   pk(np.tile(np.asarray(subln_w, f32)[0], 4))], axis=1))
        percore_j.append(dict(w_in=np.ascontiguousarray(w_in0[:, cols]), convw=convw, convb=convb, ssdp=ssdp, flags=fl, ynw=ynw))

    in_maps = []
    for c in range(8):
        b, j = c // 2, c % 2
        m = dict(shared)
        m.update(percore_j[j])
        m["xT"] = np.ascontiguousarray(x[b].T)
        m["xtok"] = np.ascontiguousarray(x[b, j * 2048:(j + 1) * 2048, :])
        in_maps.append(m)

    if _NC is None:
        _NC = build_program()
    res = run_bass_kernel_spmd(_NC, in_maps, core_ids=list(range(8)))
    outp = np.empty((4, T, D), f32)
    for c in range(8):
        b, j = c // 2, c % 2
        outp[b, j * 2048:(j + 1) * 2048, :] = res.results[c]["out"]
    if DEBUG:
        kernel.dbg = [dict(da=np.asarray(r["dbg_da"]), ssd=np.asarray(r["dbg_ssd"])) for r in res.results]
    return outp
```

```python
import bisect
from contextlib import ExitStack

import numpy as np
import ml_dtypes

import concourse.bass as bass
import concourse.mybir as mybir
from concourse.bass_utils import run_bass_kernel_spmd

F32 = mybir.dt.float32
BF16 = mybir.dt.bfloat16
AF = mybir.ActivationFunctionType
ALU = mybir.AluOpType
AX = mybir.AxisListType

ENGS = ("pe", "act", "dve", "pool", "sp")
EPS = 1e-5
T = 4096
D = 1024
DFF = 2816
NCOL = 2824
LAMBDA_INIT = 0.8 - 0.6 * 1.0
DEBUG = False
SCHED = True
SCHED_SEGS = (2, 3)
RUN_A = True
RUN_P2 = True
NT_B = 8
NT_A = 8


class Prog:
    def __init__(self, nc):
        self.nc = nc
        self.ops = []
        self.seg = 0

    DEF_COST = dict(pe=0.2, act=0.55, dve=0.7, pool=1.1, sp=0.15)

    def op(self, eng, fn, reads=(), writes=(), ps=(), dma=False, semkey=None, inc=1, c=None, lat=0.0):
        if c is None:
            c = self.DEF_COST[eng]
        self.ops.append(dict(eng=eng, fn=fn, reads=tuple(reads), writes=tuple(writes), ps=tuple(ps),
                             dma=dma, semkey=semkey, inc=inc, deps=set(), signal=False, c=c, lat=lat, seg=self.seg))
        return len(self.ops) - 1

    def dma(self, eng, out, in_, reads=(), writes=(), semkey=None, lat=2.5):
        return self.op(eng, lambda e: e.dma_start(out=out, in_=in_), reads, writes,
                       dma=True, semkey=semkey, inc=16, c=(0.15 if eng == "sp" else 1.0), lat=lat)

    def schedule(self, window=800, xlat=0.3):
        ops = self.ops
        n = len(ops)
        deps = [set() for _ in range(n)]
        oonly = [set() for _ in range(n)]
        last_w, readers, last_ps = {}, {}, {}
        for i, o in enumerate(ops):
            for k in o["reads"]:
                if k in last_w:
                    deps[i].add(last_w[k])
            for k in o["writes"]:
                if k in last_w:
                    deps[i].add(last_w[k])
                deps[i].update(readers.get(k, ()))
            for k in o["reads"]:
                readers.setdefault(k, []).append(i)
            for k in o["writes"]:
                last_w[k] = i
                readers[k] = []
            for b in o["ps"]:
                d = last_ps.setdefault(b, {})
                for e2, j in d.items():
                    if e2 != o["eng"]:
                        deps[i].add(j)
                    else:
                        oonly[i].add(j)
                d[o["eng"]] = i
            deps[i].discard(i)
            oonly[i].discard(i)
        alld = [deps[i] | oonly[i] for i in range(n)]
        succ = [[] for _ in range(n)]
        indeg = [0] * n
        for i in range(n):
            indeg[i] = len(alld[i])
            for d in alld[i]:
                succ[d].append(i)
        import heapq
        finish = [0.0] * n
        efree = {e: 0.0 for e in ENGS}
        ready = [i for i in range(n) if indeg[i] == 0]
        heapq.heapify(ready)
        done = [False] * n
        base = 0
        order = []
        while len(order) < n:
            while base < n and done[base]:
                base += 1
            lim = base + window
            best, bkey = None, None
            held = []
            while ready and ready[0] < lim:
                held.append(heapq.heappop(ready))
            cand = [i for i in held if (ops[i]["seg"] in SCHED_SEGS and ops[base]["seg"] in SCHED_SEGS and ops[i]["seg"] == ops[base]["seg"]) or i == base]
            for i in cand:
                o = ops[i]
                r = 0.0
                for d in alld[i]:
                    f = finish[d] + (xlat if ops[d]["eng"] != o["eng"] else 0.0)
                    if f > r:
                        r = f
                st = max(efree[o["eng"]], r)
                key = (st, i)
                if bkey is None or key < bkey:
                    best, bkey = i, key
            for i in held:
                if i != best:
                    heapq.heappush(ready, i)
            i = best
            o = ops[i]
            st = bkey[0]
            efree[o["eng"]] = st + o["c"]
            finish[i] = st + o["c"] + o["lat"]
            done[i] = True
            order.append(i)
            for s_ in succ[i]:
                indeg[s_] -= 1
                if indeg[s_] == 0:
                    heapq.heappush(ready, s_)
        pos = {old: new for new, old in enumerate(order)}
        self.ops = [ops[i] for i in order]
        for new, old in enumerate(order):
            self.ops[new]["deps"] = {pos[d] for d in deps[old]}
        self.model_time = max(finish)
        self.scheduled = True

    def analyze(self):
        ops = self.ops
        if not getattr(self, "scheduled", False):
            last_w, readers = {}, {}
            last_ps = {}
            for i, o in enumerate(ops):
                for k in o["reads"]:
                    if k in last_w:
                        o["deps"].add(last_w[k])
                for k in o["writes"]:
                    if k in last_w:
                        o["deps"].add(last_w[k])
                    for r in readers.get(k, ()):
                        o["deps"].add(r)
                for k in o["reads"]:
                    readers.setdefault(k, []).append(i)
                for k in o["writes"]:
                    last_w[k] = i
                    readers[k] = []
                for b in o["ps"]:
                    d = last_ps.setdefault(b, {})
                    for e2, j in d.items():
                        if e2 != o["eng"]:
                            o["deps"].add(j)
                    d[o["eng"]] = i
                o["deps"].discard(i)
        for i, o in enumerate(ops):
            if o["eng"] == "pe" and not o["dma"]:
                o["deps"] = {d for d in o["deps"] if not (ops[d]["eng"] == "pe" and ops[d]["semkey"] is None)}
            best = {}
            keep = set()
            for d in o["deps"]:
                if ops[d]["semkey"] is not None:
                    keep.add(d)
                else:
                    e2 = ops[d]["eng"]
                    if best.get(e2, -1) < d:
                        best[e2] = d
            keep.update(best.values())
            o["deps"] = keep
            for d in o["deps"]:
                ops[d]["signal"] = True
        cnt = {e: 0 for e in ENGS}
        dcnt = {}
        self.own_keys = []
        self._idx = {}
        for i, o in enumerate(ops):
            if o["semkey"] is not None:
                k = o["semkey"]
                if k not in dcnt:
                    dcnt[k] = 0
                    self.own_keys.append(k)
                dcnt[k] += o["inc"]
                o["sem"] = ("own", k)
                o["val"] = dcnt[k]
                self._idx.setdefault(k, []).append((i, dcnt[k]))
            elif o["signal"]:
                cnt[o["eng"]] += 1
                o["sem"] = ("eng", o["eng"])
                o["val"] = cnt[o["eng"]]
        self.final = dict(dcnt)

    def count_before(self, key, i):
        lst = self._idx[key]
        p = bisect.bisect_left(lst, (i, -1))
        return lst[p - 1][1] if p > 0 else 0

    def emit(self):
        nc = self.nc
        if SCHED:
            self.schedule()
        self.analyze()
        ops = self.ops
        with ExitStack() as es:
            sems = {}
            for e in ENGS:
                sems[("eng", e)] = es.enter_context(nc.semaphore("s_" + e))
            for n, k in enumerate(self.own_keys):
                sems[("own", k)] = es.enter_context(nc.semaphore("o%d" % n))
            block = es.enter_context(nc.Block())

            def run_engine(ename, eng):
                waited = {}
                for i, o in enumerate(ops):
                    if o["eng"] != ename:
                        continue
                    need = {}
                    for d in o["deps"]:
                        do = ops[d]
                        s, v = do["sem"], do["val"]
                        if do["semkey"] is not None:
                            v = max(v, self.count_before(do["semkey"], i))
                        if need.get(s, 0) < v:
                            need[s] = v
                    for s, v in need.items():
                        if waited.get(s, 0) >= v:
                            continue
                        eng.wait_ge(sems[s], v)
                        waited[s] = v
                    ins = o["fn"](eng)
                    if o["semkey"] is not None:
                        ins.then_inc(sems[o["sem"]], o["inc"])
                    elif o["signal"]:
                        ins.then_inc(sems[o["sem"]], 1)
                return waited

            @block.tensor
            def _(e):
                run_engine("pe", e)

            @block.scalar
            def _(e):
                run_engine("act", e)

            @block.vector
            def _(e):
                run_engine("dve", e)

            @block.gpsimd
            def _(e):
                run_engine("pool", e)

            @block.sync
            def _(e):
                w = run_engine("sp", e)
                for k, v in self.final.items():
                    if w.get(("own", k), 0) < v:
                        e.wait_ge(sems[("own", k)], v)


def build_program():
    nc = bass.Bass("TRN2", target_bir_lowering=False)

    def din(name, shape, dt=F32):
        return nc.dram_tensor(name, list(shape), dt, kind="ExternalInput").ap()

    xT = din("xT", [D, T])
    xtok = din("xtok", [2048, D])
    w_in = din("w_in", [D, NCOL])
    mixw = din("mixw", [128, 8])
    convw = din("convw", [128, 6, 4])
    convb = din("convb", [128, 6])
    ssdp = din("ssdp", [128, 3, 8])
    lamv = din("lamv", [128, 4, 64])
    w_out = din("w_out", [2048, D])
    ynw = din("ynw", [128, 8])
    ffnwb = din("ffnwb", [128, D])
    w_gate = din("w_gate", [D, DFF])
    w_up = din("w_up", [D, DFF])
    w_down = din("w_down", [DFF, D])
    finw = din("finw", [128, D])
    cU = din("cU", [128, 128])
    cG = din("cG", [128, 128])
    cI = din("cI", [128, 128])
    flags = din("flags", [128, 2])
    out = nc.dram_tensor("out", [2048, D], F32, kind="ExternalOutput").ap()
    if DEBUG:
        dbg_da = nc.dram_tensor("dbg_da", [512, T], BF16, kind="ExternalOutput").ap()
        dbg_ssd = nc.dram_tensor("dbg_ssd", [512, T], BF16, kind="ExternalOutput").ap()

    yda_in = [nc.dram_tensor("yda_in%d" % i, [512, 512], BF16) for i in range(8)]
    yda_out = [nc.dram_tensor("yda_out%d" % i, [1024, 512], BF16) for i in range(8)]
    yssd_in = [nc.dram_tensor("yssd_in%d" % i, [512, 512], BF16) for i in range(8)]
    yssd_out = [nc.dram_tensor("yssd_out%d" % i, [1024, 512], BF16) for i in range(8)]

    P = Prog(nc)
    with ExitStack() as es:
        MEMW = 53000
        mem = es.enter_context(nc.sbuf_tensor("mem", [128, MEMW], F32))

        class Alloc:
            def __init__(self, start, end):
                self.off, self.end = start, end

            def take(self, shape, dt=F32):
                n = int(np.prod(shape[1:]))
                nbytes = n * (2 if dt == BF16 else 4)
                nbytes = (nbytes + 3) // 4 * 4
                assert self.off % 4 == 0 and self.off + nbytes <= self.end, (self.off, nbytes, self.end, shape)
                a = mem[:, self.off // 4:(self.off + nbytes) // 4]
                self.off += nbytes
                if dt == BF16:
                    a = a.bitcast(BF16)[:, 0:n]
                if len(shape) == 3:
                    a = a.rearrange("p (a b) -> p a b", a=shape[1])
                elif len(shape) == 4:
                    a = a.rearrange("p (a b c) -> p a b c", a=shape[1], b=shape[2])
                return a

        psf = es.enter_context(nc.psum_tensor("psf", [128, 7, 512], F32))
        psb = es.enter_context(nc.psum_tensor("psb", [128, 1024], BF16))

        pers = Alloc(0, 1024)
        cst = Alloc(1024, 8192)
        WO_OFF, WG_OFF, STG_OFF, TAIL_OFF, MEM_END = 111440, 144208, 189264, 197456, 212000

        def sb(name, shape, dt=F32):
            return pers.take(shape, dt)

        def sc(name, shape, dt=F32):
            return cst.take(shape, dt)

        Ib = sb("Ib", [128, 128], BF16)
        ynw_s = sb("ynw_s", [128, 8]); flags_s = sb("flags_s", [128, 2])
        wda = sb("wda", [128, 1]); st2 = sb("st2", [128, 4]); neglam = sb("neglam", [128, 1])
        mixw_s = sb("mixw_s", [128, 8])
        Uf = sc("Uf", [128, 128]); Gf = sc("Gf", [128, 128]); If = sc("If", [128, 128])
        Ub = sc("Ub", [128, 128], BF16)
        onesb = sc("onesb", [128, 128], BF16); onesf = sc("onesf", [128, 128])
        convw_s = sc("convw_s", [128, 6, 4]); convb_s = sc("convb_s", [128, 6])
        ssdp_s = sc("ssdp_s", [128, 3, 8]); lamv_s = sc("lamv_s", [128, 4, 64])
        Abc = sc("Abc", [128, 8]); Dsk = sc("Dsk", [128, 8, 64])
        lam2 = sc("lam2", [128, 2]); lamt = sc("lamt", [128, 2, 64])

        small_loads = [(Uf, cU), (Gf, cG), (If, cI), (mixw_s, mixw), (convw_s, convw), (convb_s, convb),
                       (ssdp_s, ssdp), (lamv_s, lamv), (ynw_s, ynw), (flags_s, flags)]
        for n, (dst, src) in enumerate(small_loads):
            P.dma("sp", dst[:], src, writes=["c%d" % n], semkey="c%d" % n)
        CK = ["c%d" % n for n in range(len(small_loads))]
        P.op("dve", lambda e: e.tensor_copy(out=Ub[:], in_=Uf[:]), CK, ["Ub"])
        P.op("dve", lambda e: e.tensor_copy(out=Ib[:], in_=If[:]), CK, ["Ib"])
        P.op("pool", lambda e: e.memset(onesb[:], 1.0), [], ["onesb"])
        P.op("pool", lambda e: e.memset(onesf[:], 1.0), [], ["onesf"])
        P.op("act", lambda e: e.activation(out=Abc[:], in_=ssdp_s[:, 1, :], func=AF.Exp), CK, ["Abc"])
        P.op("dve", lambda e: e.tensor_scalar(out=Abc[:], in0=Abc[:], scalar1=-1.0, scalar2=None, op0=ALU.mult), ["Abc"], ["Abc"])
        P.op("dve", lambda e: e.tensor_copy(out=Dsk[:], in_=ssdp_s[:, 2, :].unsqueeze(2).to_broadcast([128, 8, 64])), CK, ["Dsk"])
        P.op("dve", lambda e: e.tensor_tensor(out=lamt[:, 0, :], in0=lamv_s[:, 0, :], in1=lamv_s[:, 1, :], op=ALU.mult), CK, ["lamt"])
        P.op("dve", lambda e: e.tensor_tensor(out=lamt[:, 1, :], in0=lamv_s[:, 2, :], in1=lamv_s[:, 3, :], op=ALU.mult), ["lamt"], ["lamt"])
        P.op("dve", lambda e: e.tensor_reduce(out=lam2[:], in_=lamt[:], axis=AX.X, op=ALU.add), ["lamt"], ["lam2"])
        P.op("act", lambda e: e.activation(out=lam2[:], in_=lam2[:], func=AF.Exp), ["lam2"], ["lam2"])
        P.op("dve", lambda e: e.tensor_tensor(out=neglam[:], in0=lam2[:, 1:2], in1=lam2[:, 0:1], op=ALU.subtract), ["lam2"], ["neglam"])
        P.op("dve", lambda e: e.tensor_scalar(out=neglam[:], in0=neglam[:], scalar1=-LAMBDA_INIT, scalar2=None, op0=ALU.add), ["neglam"], ["neglam"])

        shared = Alloc(8192, 8192 + 53248)
        Win = shared.take([128, 8, 1536], BF16)
        xTt = shared.take([128, 8, 512])
        nT = shared.take([128, 8, 512], BF16)
        lnv = shared.take([128, 512]); rstd = shared.take([128, 512])
        ARENA0 = shared.off
        Wo = Alloc(WO_OFF, WG_OFF).take([128, 16, 1024], BF16)
        Wg = Alloc(WG_OFF, STG_OFF).take([128, 8, DFF], BF16)
        sreg = Alloc(STG_OFF, TAIL_OFF)
        ffnwb_s = sreg.take([128, D])
        c0b_1 = sreg.take([128, 16, 128], BF16)
        tail = Alloc(TAIL_OFF, MEM_END)
        finw_s = tail.take([128, D])
        actT = tail.take([128, 22, 128], BF16)
        sg = tail.take([128, 2, 128])
        n2T_1 = tail.take([128, 8, 128], BF16)

        xT_v = xT.rearrange("(kt p) t -> p kt t", p=128)
        w_in_v = w_in.rearrange("(kt p) c -> p kt c", p=128)

        def load_win(c0, ncols, tag):
            for kt in range(8):
                P.dma("pool", Win[:, kt, 0:ncols], w_in_v[:, kt, c0:c0 + ncols], writes=["Win"], semkey="Win")

        def norm_part1(i):
            t0 = i * 512
            P.dma("sp", xTt[:, 0:4, :], xT_v[:, 0:4, t0:t0 + 512], writes=["xTt0"], semkey="xTt0")
            P.dma("sp", xTt[:, 4:8, :], xT_v[:, 4:8, t0:t0 + 512], writes=["xTt1"], semkey="xTt1")
            P.op("act", lambda e: e.activation(out=nT[:], in_=xTt[:], func=AF.Square), ["xTt0", "xTt1"], ["nT"], c=3.6)

        def norm_part2(i):
            for kt in range(8):
                P.op("pe", (lambda kt=kt: lambda e: e.matmul(psf[:, 6, :], lhsT=onesb[:], rhs=nT[:, kt, :], start=(kt == 0), stop=(kt == 7)))(),
                     ["nT", "onesb"], [], ps=[6])
            P.op("act", lambda e: e.activation(out=lnv[:], in_=psf[:, 6, :], func=AF.Ln, scale=1.0 / D, bias=EPS), [], ["lnv"], ps=[6])
            P.op("act", lambda e: e.activation(out=rstd[:], in_=lnv[:], func=AF.Exp, scale=-0.5), ["lnv"], ["rstd"])
            for kt in range(8):
                P.op("dve", (lambda kt=kt: lambda e: e.scalar_tensor_tensor(out=nT[:, kt, :], in0=xTt[:, kt, :], scalar=mixw_s[:, kt:kt + 1], in1=rstd[:], op0=ALU.mult, op1=ALU.mult))(),
                     ["xTt0", "xTt1", "rstd", "nT"] + CK, ["nT"])

        def norm_tile(i):
            norm_part1(i)
            norm_part2(i)

        evac_rr = [0]

        def evac(out_ap, in_ap, reads, writes, ps, eng=None):
            if eng is None:
                eng = ("act", "dve")[evac_rr[0] % 2]
                evac_rr[0] += 1
            if eng == "act":
                P.op("act", lambda e: e.copy(out=out_ap, in_=in_ap), reads, writes, ps=ps)
            else:
                P.op(eng, lambda e: e.tensor_copy(out=out_ap, in_=in_ap), reads, writes, ps=ps)

        def proj_fm(bank, c0, reads_extra=()):
            for kt in range(8):
                P.op("pe", (lambda kt=kt: lambda e: e.matmul(psf[:, bank, :], lhsT=Win[:, kt, c0:c0 + 128], rhs=nT[:, kt, :], start=(kt == 0), stop=(kt == 7)))(),
                     ["Win", "nT"], [], ps=[bank])

        def proj_tm(bank, s, c0, n, col0=0):
            for kt in range(8):
                P.op("pe", (lambda kt=kt: lambda e: e.matmul(psf[:, bank, col0:col0 + n], lhsT=nT[:, kt, s * 128:(s + 1) * 128], rhs=Win[:, kt, c0:c0 + n], start=(kt == 0), stop=(kt == 7)))(),
                     ["Win", "nT"], [], ps=[bank])

        w_out_v = w_out.rearrange("(kt p) c -> p kt c", p=128)
        w_gate_v = w_gate.rearrange("(kt p) c -> p kt c", p=128)
        w_up_v = w_up.rearrange("(kt p) c -> p kt c", p=128)
        w_down_v = w_down.rearrange("(kt p) c -> p kt c", p=128)
        P.op("dve", lambda e: e.tensor_scalar(out=wda[:], in0=ynw_s[:, 4:5], scalar1=1.0 - LAMBDA_INIT, scalar2=None, op0=ALU.mult), CK, ["wda"])

        def wdma(dst, src, key, extra_reads=()):
            P.dma("pool", dst, src, reads=list(extra_reads), writes=[key], semkey="w_" + key, lat=9.0)

        pre_jobs = []
        for kt in range(0, 16, 2):
            pre_jobs.append((lambda kt=kt: wdma(Wo[:, kt:kt + 2, :], w_out_v[:, kt:kt + 2, :], "Wo")))
        for kt in range(8):
            pre_jobs.append((lambda kt=kt: wdma(Wg[:, kt, :], w_gate_v[:, kt, :], "Wg")))

        def gather(kind, i):
            src = (yda_in if kind == "da" else yssd_in)[i]
            dst = (yda_out if kind == "da" else yssd_out)[i]
            P.op("pool", lambda e: e.collective_compute("AllGather", ALU.bypass, replica_groups=[[0, 1], [2, 3], [4, 5], [6, 7]],
                                                        ins=[src.ap().opt()], outs=[dst.ap().opt()]),
                 ["y%s_in%d" % (kind, i)], ["y%s_out%d" % (kind, i)], semkey="cc_%s%d" % (kind, i), inc=1, c=1.0, lat=40.0)

        P.seg = 1
        cv = Alloc(ARENA0, 150000)
        kT = cv.take([128, 4, T], BF16)
        Vaug = cv.take([128, 32, 4, 132], BF16)
        qT = cv.take([128, 4, 512], BF16)
        PT = [[cv.take([128, 512], BF16) for _ in range(2)] for _ in range(2)]
        O_sb = cv.take([128, 8, 130], F32)
        rcp = cv.take([128, 8], F32); rn = cv.take([128, 4], F32)
        o1 = cv.take([128, 4, 128], F32); o2 = cv.take([128, 4, 128], F32)
        ssq4 = cv.take([128, 4], F32); r4 = cv.take([128, 4], F32)
        on = cv.take([128, 4, 128], BF16)
        yTda = cv.take([128, 4, 512], BF16)

        load_win(1288, 1536, "B")
        P.op("pool", lambda e: e.memset(Vaug[:, :, :, 128:129], 1.0), [], ["Vaug"])

        pend = []

        def make_fin(i, h):
            def fin1():
                P.op("dve", lambda e: e.reciprocal(out=rcp[:], in_=O_sb[:, :, 128]), ["O_sb"], ["rcp"])
                P.op("dve", lambda e: e.tensor_scalar(out=rn[:], in0=rcp[:, 4:8], scalar1=neglam[:, 0:1], scalar2=None, op0=ALU.mult), ["rcp", "neglam"], ["rn"])
                P.op("dve", lambda e: e.tensor_tensor(out=o1[:], in0=O_sb[:, 0:4, 0:128], in1=rcp[:, 0:4].unsqueeze(2).to_broadcast([128, 4, 128]), op=ALU.mult), ["O_sb", "rcp"], ["o1"])
                P.op("pool", lambda e: e.tensor_tensor(out=o2[:], in0=O_sb[:, 4:8, 0:128], in1=rn[:].unsqueeze(2).to_broadcast([128, 4, 128]), op=ALU.mult), ["O_sb", "rn"], ["o2"])
                P.op("dve", lambda e: e.tensor_tensor(out=o1[:], in0=o1[:], in1=o2[:], op=ALU.add), ["o1", "o2"], ["o1"])
                P.op("pool", lambda e: e.tensor_tensor(out=o2[:], in0=o1[:], in1=o1[:], op=ALU.mult), ["o1", "o2"], ["o2"])
                P.op("dve", lambda e: e.tensor_reduce(out=ssq4[:], in_=o2[:], axis=AX.X, op=ALU.add), ["o2"], ["ssq4"])

            def fin2():
                P.op("act", lambda e: e.activation(out=r4[:], in_=ssq4[:], func=AF.Ln, scale=1.0 / 128, bias=EPS), ["ssq4"], ["r4"])
                P.op("act", lambda e: e.activation(out=r4[:], in_=r4[:], func=AF.Exp, scale=-0.5), ["r4"], ["r4"])
                P.op("dve", lambda e: e.tensor_tensor(out=on[:], in0=o1[:], in1=r4[:].unsqueeze(2).to_broadcast([128, 4, 128]), op=ALU.mult), ["o1", "r4"], ["on"])

            def fin3():
                for qs in range(4):
                    P.op("pe", (lambda qs=qs: lambda e: e.transpose(psb[:, qs * 128:(qs + 1) * 128], on[:, qs, :], Ib[:]))(), ["on", "Ib"], [], ps=[7], c=0.11)
                P.op("dve", lambda e: e.tensor_scalar(out=yTda[:, h, :], in0=psb[:, 0:512], scalar1=wda[:, 0:1], scalar2=None, op0=ALU.mult), ["wda"], ["yTda"], ps=[7])
                if h == 3:
                    P.dma("sp", yda_in[i].ap().rearrange("(h p) t -> p h t", p=128), yTda[:], reads=["yTda"], writes=["yda_in%d" % i], semkey="yda_st")
                    gather("da", i)
            return [fin1, fin2, fin3]

        def run_pending(stage):
            if pend and len(pend[0]) == 3 - stage:
                pend[0].pop(0)()
                if not pend[0]:
                    pend.pop(0)

        if NT_B:
            norm_tile(0)
        for i in range(NT_B):
            t0 = i * 512
            for h in range(4):
                bank = h % 4
                proj_fm(bank, h * 128)
                evac(qT[:, h, :], psf[:, bank, :], [], ["qT"], [bank], eng="dve")
            for h in range(4):
                bank = h % 4
                proj_fm(bank, 512 + h * 128)
                evac(kT[:, h, t0:t0 + 512], psf[:, bank, :], [], ["kT"], [bank], eng="dve")
            for s in range(4):
                bank = s % 4
                proj_tm(bank, s, 1024, 512)
                evac(Vaug[:, 4 * i + s, :, 0:128], psf[:, bank, :].rearrange("p (h v) -> p h v", h=4), [], ["Vaug"], [bank], eng="dve")
            nkb = 4 * i + 4
            for h in range(4):
                started = set()
                steps = list(range(nkb))

                def emit_S(st, kb, h=h, i=i):
                    qlo = max(0, kb - 4 * i) * 128
                    n = 512 - qlo
                    for m in range(2):
                        bank = 2 * (st % 2) + m
                        P.op("pe", (lambda m=m, bank=bank, kb=kb, qlo=qlo, n=n: lambda e: e.matmul(
                            psf[:, bank, 0:n], lhsT=kT[64 * m:64 * m + 64, h, kb * 128:(kb + 1) * 128],
                            rhs=qT[64 * m:64 * m + 64, h, qlo:512], start=True, stop=True))(),
                            ["kT", "qT"], [], ps=[bank], c=0.16)

                def emit_rest(st, kb, h=h, i=i, started=started):
                    qlo = max(0, kb - 4 * i) * 128
                    n = 512 - qlo
                    b = st % 2
                    for m in range(2):
                        bank = 2 * b + m
                        P.op("act", (lambda m=m, bank=bank, n=n, b=b: lambda e: e.activation(out=PT[m][b][:, 0:n], in_=psf[:, bank, 0:n], func=AF.Exp, scale=0.125))(),
                             [], ["PT%d%d" % (m, b)], ps=[bank], c=0.5)
                        if kb >= 4 * i:
                            P.op("pool", (lambda m=m, b=b: lambda e: e.tensor_tensor(out=PT[m][b][:, 0:128], in0=PT[m][b][:, 0:128], in1=Ub[:], op=ALU.mult))(),
                                 ["PT%d%d" % (m, b), "Ub"], ["PT%d%d" % (m, b)], c=0.45)
                    for m in range(2):
                        for qs in range(max(0, kb - 4 * i), 4):
                            a = m * 4 + qs
                            bank = 4 + a // 3
                            col = (a % 3) * 130
                            first = (kb == 0) and (bank not in started)
                            if kb == 0:
                                started.add(bank)
                            last = (kb == 4 * i + qs)
                            P.op("pe", (lambda m=m, b=b, qs=qs, qlo=qlo, bank=bank, col=col, first=first, last=last, kb=kb: lambda e: e.matmul(
                                psf[:, bank, col:col + 129], lhsT=PT[m][b][:, qs * 128 - qlo:qs * 128 - qlo + 128],
                                rhs=Vaug[:, kb, h, 0:129], start=first, stop=last, skip_group_check=True))(),
                                ["PT%d%d" % (m, b), "Vaug"], [], ps=[bank], c=0.06)

                emit_S(0, 0)
                for st, kb in enumerate(steps):
                    if st + 1 < nkb:
                        emit_S(st + 1, steps[st + 1])
                    emit_rest(st, kb)
                    if st == 0:
                        run_pending(0)
                    elif st == 1:
                        run_pending(1)
                    elif st == 3:
                        run_pending(2)
                    if h == 1 and st == 2 and i + 1 < NT_B:
                        norm_part1(i + 1)
                while pend:
                    run_pending(3 - len(pend[0]))
                for bank, na in ((4, 3), (5, 3), (6, 2)):
                    a0 = (bank - 4) * 3
                    P.op("act", (lambda bank=bank, na=na, a0=a0: lambda e: e.copy(
                        out=O_sb[:, a0:a0 + na, :], in_=psf[:, bank, 0:na * 130].rearrange("p (a c) -> p a c", a=na)))(),
                        [], ["O_sb"], ps=[bank])
                pend.append(make_fin(i, h))
                if h == 2 and i + 1 < NT_B:
                    norm_part2(i + 1)
        while pend:
            run_pending(3 - len(pend[0]))

        P.seg = 2
        cv = Alloc(ARENA0, WO_OFF)
        ta = Alloc(TAIL_OFF, MEM_END)
        ub = [cv.take([128, 516], F32) for _ in range(2)]
        cacc = [cv.take([128, 512], F32) for _ in range(2)]
        carry = cv.take([128, 6, 3], F32)
        xcs = [cv.take([128, 6, 512], BF16), ta.take([128, 6, 512], BF16)]
        zs = cv.take([128, 4, 512], F32)

        def dtset(al):
            d = {}
            for nm in ("dtb", "dtv", "adt", "acs", "eacs", "dd", "decst", "cdec", "dtdec", "adthf"):
                d[nm] = al.take([128, 32], F32)
            d["adth"] = al.take([128, 32], BF16)
            d["adtl"] = al.take([128, 32], BF16)
            return d
        dts = [dtset(cv), dtset(ta)]
        xtm = ta.take([128, 640], BF16)
        xdt = ta.take([128, 8, 64], BF16)
        xdd = ta.take([128, 8, 64], BF16)
        rhs_hi = cv.take([128, 8, 128], BF16)
        rhs_lo = cv.take([128, 8, 128], BF16)
        expD = cv.take([128, 8, 128], F32)
        CBm = cv.take([128, 128], F32)
        MT = cv.take([128, 8, 128], BF16)
        t1s = [cv.take([128, 8, 64], F32) for _ in range(2)]
        t2 = cv.take([128, 8, 64], F32)
        Sst = cv.take([128, 8, 64], F32); Sbf = cv.take([128, 512], BF16)
        ssq1 = cv.take([128, 2], F32)
        ygn = cv.take([128, 512], BF16)
        yTs = cv.take([128, 4, 512], BF16)
        Gb = cv.take([128, 128], BF16)
        PB_KEYS = ["kT", "Vaug", "qT", "PT00", "PT01", "PT10", "PT11", "O_sb", "rcp", "rn", "o1", "o2", "ssq4", "r4", "on", "yTda"]
        P.op("pool", lambda e: e.memset(carry[:], 0.0), PB_KEYS, PB_KEYS + ["carry", "Wo", "Wg"])
        P.op("pool", lambda e: e.memset(Sst[:], 0.0), ["carry"], ["Sst"])
        P.op("pool", lambda e: e.memset(Sbf[:], 0.0), ["carry"], ["Sbf"])
        P.op("dve", lambda e: e.tensor_copy(out=Gb[:], in_=Gf[:]), ["carry"] + CK, ["Gb"])
        FENCE = ["carry"]

        P.op("pool", lambda e: e.memset(Win[:, :, 1280:1296], 0.0), ["Win"], ["Win"])
        load_win(0, 1288, "A")
        ZC, XC, DTC = 0, 512, 1280
        NTA = NT_A if RUN_A else 0

        def prologue_main(i):
            par = i % 2
            xc = xcs[par]
            d = dts[par]
            sx = "%d" % par
            for _ in range(2):
                if pre_jobs:
                    pre_jobs.pop(0)()
            for c in range(6):
                bank = c % 2
                u = ub[c % 2]
                acc = cacc[c % 2]
                ceng = "dve" if c % 2 == 0 else "pool"
                proj_fm(bank, XC + c * 128)
                P.op("pool", (lambda u=u, c=c: lambda e: e.tensor_copy(out=u[:, 0:3], in_=carry[:, c, :]))(), ["carry"] + FENCE, ["ub%d" % (c % 2)])
                P.op("act", (lambda u=u, bank=bank: lambda e: e.copy(out=u[:, 3:515], in_=psf[:, bank, :]))(), [], ["ub%d" % (c % 2)], ps=[bank])
                P.op("pool", (lambda u=u, c=c: lambda e: e.tensor_copy(out=carry[:, c, :], in_=u[:, 512:515]))(), ["ub%d" % (c % 2)], ["carry"])
                if ceng == "dve":
                    P.op(ceng, (lambda u=u, acc=acc, c=c: lambda e: e.tensor_scalar(out=acc[:], in0=u[:, 3:515], scalar1=convw_s[:, c, 3:4], scalar2=convb_s[:, c:c + 1], op0=ALU.mult, op1=ALU.add))(),
                         ["ub%d" % (c % 2)] + CK, ["cacc%d" % (c % 2)])
                else:
                    P.op(ceng, (lambda u=u, acc=acc, c=c: lambda e: e.tensor_tensor(out=acc[:], in0=u[:, 3:515], in1=convw_s[:, c, 3:4].to_broadcast([128, 512]), op=ALU.mult))(),
                         ["ub%d" % (c % 2)] + CK, ["cacc%d" % (c % 2)])
                    P.op(ceng, (lambda acc=acc, c=c: lambda e: e.tensor_tensor(out=acc[:], in0=acc[:], in1=convb_s[:, c:c + 1].to_broadcast([128, 512]), op=ALU.add))(),
                         ["cacc%d" % (c % 2)] + CK, ["cacc%d" % (c % 2)])
                for jt in (2, 1, 0):
                    if ceng == "dve":
                        P.op(ceng, (lambda u=u, acc=acc, c=c, jt=jt: lambda e: e.scalar_tensor_tensor(out=acc[:], in0=u[:, jt:jt + 512], scalar=convw_s[:, c, jt:jt + 1], in1=acc[:], op0=ALU.mult, op1=ALU.add))(),
                             ["ub%d" % (c % 2), "cacc%d" % (c % 2)], ["cacc%d" % (c % 2)])
                    else:
                        t2f = t2[:].rearrange("p h d -> p (h d)")
                        P.op(ceng, (lambda u=u, c=c, jt=jt, t2f=t2f: lambda e: e.tensor_tensor(out=t2f, in0=u[:, jt:jt + 512], in1=convw_s[:, c, jt:jt + 1].to_broadcast([128, 512]), op=ALU.mult))(),
                             ["ub%d" % (c % 2), "t2"], ["t2"])
                        P.op(ceng, (lambda acc=acc, t2f=t2f: lambda e: e.tensor_tensor(out=acc[:], in0=acc[:], in1=t2f, op=ALU.add))(),
                             ["t2", "cacc%d" % (c % 2)], ["cacc%d" % (c % 2)])
                P.op("act", (lambda acc=acc, c=c, xc=xc: lambda e: e.activation(out=xc[:, c, :], in_=acc[:], func=AF.Silu))(), ["cacc%d" % (c % 2)] + FENCE, ["xc" + sx])
            for s_ in range(4):
                proj_tm(6, s_, DTC, 8, col0=s_ * 8)
            v4 = lambda t_: t_[:].rearrange("p (c h) -> p c h", c=4)
            P.op("dve", lambda e: e.tensor_tensor(out=v4(d["dtb"]), in0=psf[:, 6, 0:32].rearrange("p (c h) -> p c h", c=4),
                                                  in1=ssdp_s[:, 0, :].unsqueeze(1).to_broadcast([128, 4, 8]), op=ALU.add), CK + FENCE, ["dtb" + sx], ps=[6])
            P.op("act", lambda e: e.activation(out=d["dtb"][:], in_=d["dtb"][:], func=AF.Exp), ["dtb" + sx], ["dtb" + sx])
            P.op("act", lambda e: e.activation(out=d["dtv"][:], in_=d["dtb"][:], func=AF.Ln, bias=1.0), ["dtb" + sx], ["dtv" + sx])
            P.op("dve", lambda e: e.tensor_tensor(out=v4(d["adt"]), in0=v4(d["dtv"]), in1=Abc[:].unsqueeze(1).to_broadcast([128, 4, 8]), op=ALU.mult), ["dtv" + sx, "Abc"], ["adt" + sx])
            P.op("pe", lambda e: e.matmul(psf[:, 6, 32:64], lhsT=Uf[:], rhs=d["adt"][:], start=True, stop=True), ["adt" + sx] + CK, [], ps=[6])
            P.op("pe", lambda e: e.matmul(psf[:, 6, 64:96], lhsT=onesf[:], rhs=d["adt"][:], start=False, stop=True, skip_group_check=True), ["adt" + sx, "onesf"], [], ps=[6])
            P.op("act", lambda e: e.copy(out=d["acs"][:], in_=psf[:, 6, 32:64]), [], ["acs" + sx], ps=[6])
            P.op("act", lambda e: e.activation(out=d["eacs"][:], in_=psf[:, 6, 32:64], func=AF.Exp), [], ["eacs" + sx], ps=[6])
            P.op("act", lambda e: e.activation(out=d["cdec"][:], in_=psf[:, 6, 64:96], func=AF.Exp), [], ["cdec" + sx], ps=[6])
            P.op("dve", lambda e: e.tensor_tensor(out=d["dd"][:], in0=psf[:, 6, 64:96], in1=d["acs"][:], op=ALU.subtract), ["acs" + sx], ["dd" + sx], ps=[6])
            P.op("act", lambda e: e.activation(out=d["decst"][:], in_=d["dd"][:], func=AF.Exp), ["dd" + sx], ["decst" + sx])
            P.op("dve", lambda e: e.tensor_tensor(out=d["dtdec"][:], in0=d["dtv"][:], in1=d["decst"][:], op=ALU.mult), ["dtv" + sx, "decst" + sx], ["dtdec" + sx])
            P.op("dve", lambda e: e.tensor_copy(out=d["adth"][:], in_=d["adt"][:]), ["adt" + sx], ["adth" + sx])
            P.op("dve", lambda e: e.tensor_copy(out=d["adthf"][:], in_=d["adth"][:]), ["adth" + sx], ["adthf" + sx])
            P.op("dve", lambda e: e.tensor_tensor(out=d["adtl"][:], in0=d["adt"][:], in1=d["adthf"][:], op=ALU.subtract), ["adt" + sx, "adthf" + sx], ["adtl" + sx])

        def zproj(i):
            for s_ in range(4):
                bank = s_ % 2
                proj_tm(bank, s_, ZC, 512)
                P.op("act", (lambda s_=s_, bank=bank: lambda e: e.activation(out=zs[:, s_, :], in_=psf[:, bank, :], func=AF.Silu))(), FENCE, ["zs%d" % s_], ps=[bank])

        def stageF(i, c):
            par = i % 2
            xc = xcs[par]
            d = dts[par]
            sx = "%d" % par
            t1 = t1s[c % 2]
            t1k = "t1%d" % (c % 2)
            cs = slice(c * 128, (c + 1) * 128)
            hs = slice(c * 8, (c + 1) * 8)
            for ct in range(5):
                P.op("pe", (lambda ct=ct: lambda e: e.transpose(psb[:, ct * 128:(ct + 1) * 128], xc[:, ct, cs], Ib[:]))(), ["xc" + sx, "Ib"], [], ps=[7], c=0.11)
            P.op("act", lambda e: e.copy(out=xtm[:], in_=psb[:, 0:640]), [], ["xtm"], ps=[7])
            P.op("dve", lambda e: e.tensor_tensor(out=rhs_hi[:], in0=Ub[:].unsqueeze(1).to_broadcast([128, 8, 128]),
                                                  in1=d["adth"][:, hs].unsqueeze(2).to_broadcast([128, 8, 128]), op=ALU.mult), ["adth" + sx, "Ub"], ["rhs_hi"])
            P.op("pool", lambda e: e.tensor_tensor(out=rhs_lo[:], in0=Ub[:].unsqueeze(1).to_broadcast([128, 8, 128]),
                                                   in1=d["adtl"][:, hs].unsqueeze(2).to_broadcast([128, 8, 128]), op=ALU.mult), ["adtl" + sx, "Ub"], ["rhs_lo"])
            for hh in range(2):
                P.op("pe", (lambda hh=hh: lambda e: e.matmul(psf[:, 2 + hh, :], lhsT=Gb[:], rhs=rhs_hi[:, 4 * hh:4 * hh + 4, :], start=True, stop=False))(), ["rhs_hi", "Gb"], [], ps=[2 + hh])
                P.op("pe", (lambda hh=hh: lambda e: e.matmul(psf[:, 2 + hh, :], lhsT=Gb[:], rhs=rhs_lo[:, 4 * hh:4 * hh + 4, :], start=False, stop=True))(), ["rhs_lo", "Gb"], [], ps=[2 + hh])
            P.op("pe", lambda e: e.matmul(psf[:, 4, 0:128], lhsT=xc[:, 4, cs], rhs=xc[:, 5, cs], start=True, stop=True), ["xc" + sx], [], ps=[4])
            P.op("dve", lambda e: e.tensor_tensor(out=CBm[:], in0=psf[:, 4, 0:128], in1=Uf[:], op=ALU.mult), CK, ["CBm"], ps=[4])
            for hh in range(2):
                P.op("act", (lambda hh=hh: lambda e: e.activation(out=expD[:, 4 * hh:4 * hh + 4, :], in_=psf[:, 2 + hh, :].rearrange("p (h l) -> p h l", h=4), func=AF.Exp))(),
                     [], ["expD"], ps=[2 + hh])
            P.op("dve", lambda e: e.tensor_tensor(out=MT[:], in0=expD[:], in1=CBm[:].unsqueeze(1).to_broadcast([128, 8, 128]), op=ALU.mult), ["expD", "CBm"], ["MT"])
            xv = xtm[:, 0:512].rearrange("p (h d) -> p h d", h=8)
            P.op("pool", lambda e: e.tensor_tensor(out=xdt[:], in0=xv, in1=d["dtv"][:, hs].unsqueeze(2).to_broadcast([128, 8, 64]), op=ALU.mult), ["xtm", "dtv" + sx], ["xdt"])
            P.op("pool", lambda e: e.tensor_tensor(out=xdd[:], in0=xv, in1=d["dtdec"][:, hs].unsqueeze(2).to_broadcast([128, 8, 64]), op=ALU.mult), ["xtm", "dtdec" + sx], ["xdd"])
            P.op("pool", lambda e: e.tensor_tensor(out=t2[:], in0=xv, in1=Dsk[:], op=ALU.mult), ["xtm", "Dsk"], ["t2"])
            for h in range(8):
                P.op("pe", (lambda h=h: lambda e: e.matmul(psf[:, 5, h * 64:(h + 1) * 64], lhsT=MT[:, h, :], rhs=xdt[:, h, :], start=(h == 0), stop=True, skip_group_check=True))(),
                     ["MT", "xdt"], [], ps=[5], c=0.06)
            P.op("pe", lambda e: e.matmul(psf[:, 0, :], lhsT=xc[:, 5, cs], rhs=Sbf[:], start=True, stop=True), ["xc" + sx, "Sbf"], [], ps=[0])
            P.op("pe", lambda e: e.matmul(psf[:, 1, :], lhsT=xtm[:, 512:640], rhs=xdd[:].rearrange("p h d -> p (h d)"), start=True, stop=True), ["xtm", "xdd"], [], ps=[1])
            P.op("dve", lambda e: e.tensor_tensor(out=Sst[:], in0=Sst[:], in1=d["cdec"][:, hs].unsqueeze(2).to_broadcast([128, 8, 64]), op=ALU.mult), ["Sst", "cdec" + sx], ["Sst"])
            P.op("dve", lambda e: e.tensor_tensor(out=Sst[:], in0=Sst[:], in1=psf[:, 1, :].rearrange("p (h d) -> p h d", h=8), op=ALU.add), ["Sst"], ["Sst"], ps=[1])
            P.op("act", lambda e: e.copy(out=Sbf[:], in_=Sst[:].rearrange("p h d -> p (h d)")), ["Sst"], ["Sbf"])
            P.op("dve", lambda e: e.tensor_tensor(out=t1[:], in0=psf[:, 0, :].rearrange("p (h d) -> p h d", h=8),
                                                  in1=d["eacs"][:, hs].unsqueeze(2).to_broadcast([128, 8, 64]), op=ALU.mult), ["eacs" + sx], [t1k], ps=[0])
            P.op("dve", lambda e: e.tensor_tensor(out=t1[:], in0=t1[:], in1=psf[:, 5, :].rearrange("p (h d) -> p h d", h=8), op=ALU.add), [t1k], [t1k], ps=[5])
            P.op("dve", lambda e: e.tensor_tensor(out=t1[:], in0=t1[:], in1=t2[:], op=ALU.add), [t1k, "t2"], [t1k])

        def stageK(i, c):
            t1 = t1s[c % 2]
            t1k = "t1%d" % (c % 2)
            cs = slice(c * 128, (c + 1) * 128)
            P.op("dve", lambda e: e.tensor_tensor(out=t1[:], in0=t1[:], in1=zs[:, c, :].rearrange("p (h d) -> p h d", h=8), op=ALU.mult), [t1k, "zs%d" % c], [t1k])
            P.op("act", lambda e: e.activation(out=ygn[:], in_=t1[:].rearrange("p h d -> p (h d)"), func=AF.Square, accum_out=ssq1[:, 0:1]), [t1k, "ygn"], ["ygn", "ssq1"])
            P.op("act", lambda e: e.activation(out=ssq1[:, 1:2], in_=ssq1[:, 0:1], func=AF.Ln, scale=1.0 / 512, bias=EPS), ["ssq1"], ["ssq1"])
            P.op("act", lambda e: e.activation(out=ssq1[:, 1:2], in_=ssq1[:, 1:2], func=AF.Exp, scale=-0.5), ["ssq1"], ["ssq1"])
            P.op("dve", lambda e: e.tensor_scalar(out=ygn[:], in0=t1[:].rearrange("p h d -> p (h d)"), scalar1=ssq1[:, 1:2], scalar2=None, op0=ALU.mult), [t1k, "ssq1", "ygn"], ["ygn"])
            for ft in range(4):
                P.op("pe", (lambda ft=ft: lambda e: e.transpose(psb[:, 512 + ft * 128:512 + (ft + 1) * 128], ygn[:, ft * 128:(ft + 1) * 128], Ib[:]))(), ["ygn", "Ib"], [], ps=[7], c=0.11)
            P.op("dve", lambda e: e.tensor_tensor(out=yTs[:, :, cs], in0=psb[:, 512:1024].rearrange("p (f t) -> p f t", f=4),
                                                  in1=ynw_s[:, 0:4].unsqueeze(2).to_broadcast([128, 4, 128]), op=ALU.mult), CK, ["yTs"], ps=[7])
            if c == 3:
                P.dma("sp", yssd_in[i].ap().rearrange("(f p) t -> p f t", p=128), yTs[:], reads=["yTs"], writes=["yssd_in%d" % i], semkey="yssd_st")
                gather("ssd", i)

        if NTA:
            norm_tile(0)
            prologue_main(0)
            zproj(0)
        for i in range(NTA):
            nxt = i + 1 < NTA
            stageF(i, 0)
            stageF(i, 1)
            stageK(i, 0)
            if nxt:
                norm_tile(i + 1)
            stageF(i, 2)
            stageK(i, 1)
            stageF(i, 3)
            stageK(i, 2)
            if nxt:
                prologue_main(i + 1)
            stageK(i, 3)
            if nxt:
                zproj(i + 1)

        while pre_jobs and RUN_P2:
            pre_jobs.pop(0)()

        if DEBUG:
            for i in range(NT_B):
                P.dma("sp", dbg_da[:, i * 512:(i + 1) * 512], yda_in[i].ap(), reads=["yda_in%d" % i], semkey="dbg1")
            for i in range(NT_A if RUN_A else 0):
                P.dma("sp", dbg_ssd[:, i * 512:(i + 1) * 512], yssd_in[i].ap(), reads=["yssd_in%d" % i], semkey="dbg2")

        PA_KEYS = (["ub0", "ub1", "cacc0", "cacc1", "carry", "xc0", "xc1", "zs0", "zs1", "zs2", "zs3", "xtm", "rhs_hi", "rhs_lo", "expD", "CBm", "MT",
                    "xdt", "xdd", "t10", "t11", "t2", "Sst", "Sbf", "ssq1", "ygn", "yTs", "Gb"]
                   + [nm + sx for nm in ("dtb", "dtv", "adt", "acs", "eacs", "dd", "decst", "cdec", "dtdec", "adthf", "adth", "adtl") for sx in ("0", "1")])
        P.seg = 3
        cv = Alloc(1024, WO_OFF)
        Wu = cv.take([128, 8, DFF], BF16)
        Wd = cv.take([128, 22, 1024], BF16)
        c0bs = [cv.take([128, 16, 128], BF16), c0b_1]
        c1h = cv.take([128, 8, 128], BF16)
        h1 = [cv.take([128, 1024], F32) for _ in range(2)]
        n2 = cv.take([128, 1024], BF16)
        n2Ts = [cv.take([128, 8, 128], BF16), n2T_1]
        P.op("pool", lambda e: e.memset(st2[:], 0.0), PA_KEYS + CK + ["Win", "nT", "xTt0", "xTt1", "lnv", "rstd", "Ub", "onesb", "onesf", "Abc", "Dsk", "lamt", "lam2"],
             PA_KEYS + ["Wd", "Wu", "p2fence", "st2", "finw", "actT", "sg0", "sg1", "ffnwb", "c0b0", "c0b1", "c1h", "h10", "h11", "n2", "n2T0", "n2T1"])
        F2 = ["p2fence"]
        NT2 = 16 if RUN_P2 else 0
        if NT2:
            P.dma("sp", finw_s[:], finw, reads=F2, writes=["finw"], semkey="finw")
            P.dma("sp", ffnwb_s[:], ffnwb, reads=F2, writes=["ffnwb"], semkey="ffnwb")
            for kt in range(8):
                wdma(Wu[:, kt, :], w_up_v[:, kt, :], "Wu", F2)
            for kt in range(0, 22, 2):
                wdma(Wd[:, kt:kt + 2, :], w_down_v[:, kt:kt + 2, :], "Wd", F2)

        ys_v = [t_.ap().rearrange("(kt p) t -> p kt t", p=128) for t_ in yssd_out]
        yd_v = [t_.ap().rearrange("(kt p) t -> p kt t", p=128) for t_ in yda_out]

        def stA(tt):
            T0 = tt * 128
            hb, hk = h1[tt % 2], "h1%d" % (tt % 2)
            c0b, ck0 = c0bs[tt % 2], "c0b%d" % (tt % 2)
            n2T, nk = n2Ts[tt % 2], "n2T%d" % (tt % 2)
            P.dma("sp", hb[:], xtok[T0:T0 + 128, :], reads=F2, writes=[hk], semkey=hk)
            ti, tc = T0 // 512, T0 % 512
            for half, (src, srckey) in enumerate(((ys_v, "yssd_out"), (yd_v, "yda_out"))):
                P.dma("sp", c0b[:, 8 * half:8 * half + 8, :], src[ti][:, :, tc:tc + 128], reads=[srckey + str(ti)] + F2, writes=[ck0], semkey=ck0)
                P.dma("sp", c1h[:], src[4 + ti][:, :, tc:tc + 128], reads=[srckey + str(4 + ti)] + F2, writes=["c1h"], semkey="c1h")
                P.op("dve", lambda e: e.tensor_scalar(out=c1h[:], in0=c1h[:], scalar1=flags_s[:, 1:2], scalar2=None, op0=ALU.mult), ["c1h"], ["c1h"])
                P.op("dve", (lambda half=half: lambda e: e.scalar_tensor_tensor(out=c0b[:, 8 * half:8 * half + 8, :], in0=c0b[:, 8 * half:8 * half + 8, :], scalar=flags_s[:, 0:1], in1=c1h[:], op0=ALU.mult, op1=ALU.add))(),
                     [ck0, "c1h"], [ck0])
            for dh in range(2):
                bank = dh
                for kt in range(16):
                    P.op("pe", (lambda kt=kt, dh=dh, bank=bank: lambda e: e.matmul(psf[:, bank, :], lhsT=c0b[:, kt, :], rhs=Wo[:, kt, dh * 512:(dh + 1) * 512], start=(kt == 0), stop=(kt == 15)))(),
                         [ck0, "Wo"], [], ps=[bank])
                P.op("dve", (lambda dh=dh, bank=bank: lambda e: e.tensor_tensor(out=hb[:, dh * 512:(dh + 1) * 512], in0=psf[:, bank, :], in1=hb[:, dh * 512:(dh + 1) * 512], op=ALU.add))(),
                     [hk], [hk], ps=[bank])
            P.op("act", lambda e: e.activation(out=n2[:], in_=hb[:], func=AF.Square, accum_out=st2[:, 0:1]), [hk, "n2"], ["n2", "st2a"])
            P.op("act", lambda e: e.activation(out=st2[:, 1:2], in_=st2[:, 0:1], func=AF.Ln, scale=1.0 / D, bias=EPS), ["st2a"], ["st2a"])
            P.op("act", lambda e: e.activation(out=st2[:, 1:2], in_=st2[:, 1:2], func=AF.Exp, scale=-0.5), ["st2a"], ["st2a"])
            P.op("dve", lambda e: e.scalar_tensor_tensor(out=n2[:], in0=hb[:], scalar=st2[:, 1:2], in1=ffnwb_s[:], op0=ALU.mult, op1=ALU.mult), [hk, "st2a", "n2", "ffnwb"], ["n2"])
            for kt in range(8):
                P.op("pe", (lambda kt=kt: lambda e: e.transpose(psb[:, kt * 128:(kt + 1) * 128], n2[:, kt * 128:(kt + 1) * 128], Ib[:]))(), ["n2", "Ib"], [], ps=[7], c=0.11)
            P.op("act", lambda e: e.copy(out=n2T[:], in_=psb[:].rearrange("p (k t) -> p k t", k=8)), F2, [nk], ps=[7])

        def stB(tt):
            n2T, nk = n2Ts[tt % 2], "n2T%d" % (tt % 2)
            for f in range(22):
                gb = 2 + (f % 2)
                ubk = 4 + (f % 2)
                for kt in range(8):
                    P.op("pe", (lambda kt=kt, f=f, gb=gb: lambda e: e.matmul(psf[:, gb, 0:128], lhsT=Wg[:, kt, f * 128:(f + 1) * 128], rhs=n2T[:, kt, :], start=(kt == 0), stop=(kt == 7)))(),
                         ["Wg", nk], [], ps=[gb], c=0.06)
                for kt in range(8):
                    P.op("pe", (lambda kt=kt, f=f, ubk=ubk: lambda e: e.matmul(psf[:, ubk, 0:128], lhsT=Wu[:, kt, f * 128:(f + 1) * 128], rhs=n2T[:, kt, :], start=(kt == 0), stop=(kt == 7)))(),
                         ["Wu", nk], [], ps=[ubk], c=0.06)
                P.op("act", (lambda f=f, gb=gb: lambda e: e.activation(out=sg[:, f % 2, :], in_=psf[:, gb, 0:128], func=AF.Silu))(), F2, ["sg%d" % (f % 2)], ps=[gb], c=0.28)
                P.op("dve", (lambda f=f, ubk=ubk: lambda e: e.tensor_tensor(out=actT[:, f, :], in0=sg[:, f % 2, :], in1=psf[:, ubk, 0:128], op=ALU.mult))(), ["sg%d" % (f % 2)] + F2, ["actT"], ps=[ubk], c=0.29)

        def stC(tt):
            T0 = tt * 128
            hb, hk = h1[tt % 2], "h1%d" % (tt % 2)
            for dh in range(2):
                bank = dh
                for f in range(22):
                    P.op("pe", (lambda f=f, dh=dh, bank=bank: lambda e: e.matmul(psf[:, bank, :], lhsT=actT[:, f, :], rhs=Wd[:, f, dh * 512:(dh + 1) * 512], start=(f == 0), stop=(f == 21)))(),
                         ["actT", "Wd"], [], ps=[bank])
                P.op("dve", (lambda dh=dh, bank=bank: lambda e: e.tensor_tensor(out=hb[:, dh * 512:(dh + 1) * 512], in0=psf[:, bank, :], in1=hb[:, dh * 512:(dh + 1) * 512], op=ALU.add))(),
                     [hk], [hk], ps=[bank])
            P.op("act", lambda e: e.activation(out=actT[:, 0:8, :].rearrange("p a b -> p (a b)"), in_=hb[:], func=AF.Square, accum_out=st2[:, 2:3]), [hk, "actT"], ["actT", "st2b"])
            P.op("act", lambda e: e.activation(out=st2[:, 3:4], in_=st2[:, 2:3], func=AF.Ln, scale=1.0 / D, bias=EPS), ["st2b"], ["st2b"])
            P.op("act", lambda e: e.activation(out=st2[:, 3:4], in_=st2[:, 3:4], func=AF.Exp, scale=-0.5), ["st2b"], ["st2b"])
            P.op("dve", lambda e: e.scalar_tensor_tensor(out=hb[:], in0=hb[:], scalar=st2[:, 3:4], in1=finw_s[:], op0=ALU.mult, op1=ALU.mult), [hk, "st2b", "finw"], [hk])
            P.dma("sp", out[T0:T0 + 128, :], hb[:], reads=[hk], writes=["out"], semkey="ost%d" % (tt % 2))

        if NT2:
            stA(0)
        for tt in range(NT2):
            stB(tt)
            if tt + 1 < NT2:
                stA(tt + 1)
            stC(tt)

        P.emit()
    return nc


_NC = None


def kernel(x, mix_norm_w, w_in, conv_w, conv_b, dt_bias, a_log, d_skip, ssd_norm_w,
           lam_q1, lam_k1, lam_q2, lam_k2, subln_w, w_out, ffn_norm_w, w_gate, w_up, w_down,
           final_norm_w):
    global _NC
    f32 = np.float32
    x = np.asarray(x, f32)
    w_in0 = np.asarray(w_in, f32)[0]
    conv_w0 = np.asarray(conv_w, f32)[0]
    conv_b0 = np.asarray(conv_b, f32)[0]

    def pk(v):
        v = np.asarray(v, f32).reshape(-1, 128)
        return np.ascontiguousarray(v.T)

    def rep(v):
        v = np.asarray(v, f32).reshape(1, -1)
        return np.ascontiguousarray(np.broadcast_to(v, (128, v.shape[1])))

    ar = np.arange(128)
    cU = (ar[:, None] <= ar[None, :]).astype(f32)
    cG = (ar[:, None] > ar[None, :]).astype(f32)
    cI = np.eye(128, dtype=f32)
    shared = dict(
        mixw=pk(np.asarray(mix_norm_w, f32)[0]),
        lamv=np.ascontiguousarray(np.stack([rep(np.asarray(v, f32)[0]) for v in (lam_q1, lam_k1, lam_q2, lam_k2)], axis=1)),
        w_out=np.ascontiguousarray(np.asarray(w_out, f32)[0]),
        ffnwb=rep(np.asarray(ffn_norm_w, f32)[0]),
        w_gate=np.ascontiguousarray(np.asarray(w_gate, f32)[0]),
        w_up=np.ascontiguousarray(np.asarray(w_up, f32)[0]),
        w_down=np.ascontiguousarray(np.asarray(w_down, f32)[0]),
        finw=rep(np.asarray(final_norm_w, f32)),
        cU=cU, cG=cG, cI=cI,
    )
    percore_j = []
    for j in range(2):
        cols = np.concatenate([
            np.arange(512 * j, 512 * j + 512),
            np.arange(1024 + 512 * j, 1024 + 512 * j + 512),
            np.arange(2048 + 128 * j, 2048 + 128 * j + 128),
            np.arange(2304 + 128 * j, 2304 + 128 * j + 128),
            np.arange(2560 + 8 * j, 2560 + 8 * j + 8),
            np.arange(2576 + 512 * j, 2576 + 512 * j + 512),
            np.arange(3600 + 512 * j, 3600 + 512 * j + 512),
            np.arange(4624 + 512 * j, 4624 + 512 * j + 512),
        ])
        ch = np.concatenate([np.arange(512 * j, 512 * j + 512), np.arange(1024 + 128 * j, 1024 + 128 * j + 128),
                             np.arange(1280 + 128 * j, 1280 + 128 * j + 128)])
        cw = conv_w0[:, ch]
        convw = np.ascontiguousarray(cw.reshape(4, 6, 128).transpose(2, 1, 0))
        convb = np.ascontiguousarray(conv_b0[ch].reshape(6, 128).T)
        hsl = slice(8 * j, 8 * j + 8)
        ssdp = np.ascontiguousarray(np.stack([rep(np.asarray(dt_bias, f32)[0][hsl]), rep(np.asarray(a_log, f32)[0][hsl]),
                                              rep(np.asarray(d_skip, f32)[0][hsl])], axis=1))
        fl = np.zeros((128, 2), f32)
        fl[:, j] = 1.0
        ynw = np.ascontiguousarray(np.concatenate([pk(np.asarray(ssd_norm_w, f32)[0][512 * j:512 * j + 512]),
                                                   pk(np.tile(np.asarray(subln_w, f32)[0], 4))], axis=1))
        percore_j.append(dict(w_in=np.ascontiguousarray(w_in0[:, cols]), convw=convw, convb=convb, ssdp=ssdp, flags=fl, ynw=ynw))

    in_maps = []
    for c in range(8):
        b, j = c // 2, c % 2
        m = dict(shared)
        m.update(percore_j[j])
        m["xT"] = np.ascontiguousarray(x[b].T)
        m["xtok"] = np.ascontiguousarray(x[b, j * 2048:(j + 1) * 2048, :])
        in_maps.append(m)

    if _NC is None:
        _NC = build_program()
    res = run_bass_kernel_spmd(_NC, in_maps, core_ids=list(range(8)))
    outp = np.empty((4, T, D), f32)
    for c in range(8):
        b, j = c // 2, c % 2
        outp[b, j * 2048:(j + 1) * 2048, :] = res.results[c]["out"]
    if DEBUG:
        kernel.dbg = [dict(da=np.asarray(r["dbg_da"]), ssd=np.asarray(r["dbg_ssd"])) for r in res.results]
    return outp
```
